# Optimizing a Trainium2 kernel written in Bass

```python
import jax
import jax.numpy as jnp
from jax import lax
import numpy as np

D_MODEL = 1024
BATCH = 4
SEQ = 4096
DEPTH = 4
DEC_BATCH = 8
DEC_SEQ = 32
PAST_LEN = 1024

CHUNK = 64
N_META = 16
WINDOW = 128
WINDOW_CHUNKS = WINDOW // CHUNK
A_HEADS = 8
A_KV_HEADS = 2
A_HEAD_DIM = 64
A_GROUP = A_HEADS // A_KV_HEADS
A_Q_W = A_HEADS * A_HEAD_DIM
A_KV_W = A_KV_HEADS * A_HEAD_DIM
B_HEADS = 4
B_KEY_DIM = 128
B_VAL_DIM = 128
B_QK_W = B_HEADS * B_KEY_DIM
B_V_W = B_HEADS * B_VAL_DIM
B_BLOCK = 16
CONV_WIDTH = 3
D_FF = 2816
FFN_RESIDUAL = 0.5
N_EVEN = (DEPTH + 1) // 2
N_ODD = DEPTH // 2
EVEN_IN_W = A_Q_W + 2 * A_KV_W + 2 * B_QK_W + 2 * B_V_W
EVEN_OUT_W = A_Q_W + B_V_W
EPS = 1e-6
MASK_VALUE = -1e30
LB_FLOOR = 1e-30

kernel_name = 'hybrid_streaming_swa_hgrn2_shortconv_step'


def rms_norm(x, gain):
    xf = x.astype(jnp.float32)
    y = xf * lax.rsqrt(jnp.mean(xf * xf, axis=-1, keepdims=True) + EPS)
    return (y * gain.astype(jnp.float32)).astype(x.dtype)


def swiglu(h, w_gate, w_up, w_down):
    return (jax.nn.silu(h @ w_gate) * (h @ w_up)) @ w_down


def hgrn_lower_bounds(logits):
    p = jax.nn.softmax(logits.astype(jnp.float32), axis=0)
    return jnp.maximum(jnp.cumsum(p, axis=0) - p[0:1], 0.0)


def sink_attention(q, k, v, sink, key_valid):
    scale = A_HEAD_DIM ** -0.5
    s = jnp.einsum('bcqkgd,bcskd->bckgqs', q.astype(jnp.float32), k.astype(jnp.float32)) * scale
    s = jnp.where(key_valid[None, :, None, None, None, :], s, MASK_VALUE)
    sk = sink.astype(jnp.float32).reshape(A_KV_HEADS, A_GROUP)[None, None, :, :, None, None]
    m = jnp.maximum(jnp.max(s, axis=-1, keepdims=True), sk)
    p = jnp.exp(s - m)
    w = p / (jnp.sum(p, axis=-1, keepdims=True) + jnp.exp(sk - m))
    o = jnp.einsum('bckgqs,bcskd->bcqkgd', w, v.astype(jnp.float32))
    return o.astype(q.dtype)


def swa_prompt(q, k, v, sink):
    bsz, length = q.shape[:2]
    pad = (-N_META) % CHUNK
    n_chunks = (length + pad) // CHUNK
    back = WINDOW_CHUNKS * CHUNK
    qb = jnp.pad(q, ((0, 0), (pad, 0), (0, 0), (0, 0))).reshape(
        bsz, n_chunks, CHUNK, A_KV_HEADS, A_GROUP, A_HEAD_DIM)

    def band(a):
        ap = jnp.pad(a, ((0, 0), (pad + back, 0), (0, 0), (0, 0))).reshape(
            bsz, n_chunks + WINDOW_CHUNKS, CHUNK, A_KV_HEADS, A_HEAD_DIM)
        return jnp.concatenate([ap[:, j:j + n_chunks] for j in range(WINDOW_CHUNKS + 1)], axis=2)

    valid = (jnp.arange((n_chunks + WINDOW_CHUNKS) * CHUNK) >= pad + back).reshape(
        n_chunks + WINDOW_CHUNKS, CHUNK)
    valid = jnp.concatenate([valid[j:j + n_chunks] for j in range(WINDOW_CHUNKS + 1)], axis=1)
    o = sink_attention(qb, band(k), band(v), sink, valid)
    return o.reshape(bsz, n_chunks * CHUNK, A_Q_W)[:, pad:]


def hgrn2_blocks(q, k, v, log_f, s0, block):
    bsz, length, heads, dk = q.shape
    dv = v.shape[-1]
    n = length // block
    causal = jnp.tril(jnp.ones((block, block), bool))[None, :, :, None, None]

    def to_blocks(a):
        return jnp.moveaxis(a.reshape(bsz, n, block, heads, a.shape[-1]), 1, 0)

    def step(state, inp):
        qc, kc, vc, gc = inp
        b = jnp.cumsum(gc, axis=1)
        o_inter = jnp.einsum('bthk,bhkv->bthv', qc * jnp.exp(b), state)
        diff = b[:, :, None] - b[:, None, :]
        decay = jnp.where(causal, jnp.exp(jnp.where(causal, diff, 0.0)), 0.0)
        scores = jnp.einsum('bthk,bshk,btshk->bhts', qc, kc, decay)
        o_intra = jnp.einsum('bhts,bshv->bthv', scores, vc)
        b_last = b[:, -1]
        state = jnp.exp(b_last)[..., None] * state + jnp.einsum(
            'bshk,bshv->bhkv', kc * jnp.exp(b_last[:, None] - b), vc)
        return state, o_inter + o_intra

    state, o = lax.scan(step, s0, (to_blocks(q), to_blocks(k), to_blocks(v), to_blocks(log_f)))
    return jnp.moveaxis(o, 0, 1).reshape(bsz, length, heads, dv), state


def even_mixer(h, w_in, w_out, sink, lb, out_gain, cache_k, cache_v, state):
    bsz, length, _ = h.shape
    widths = (A_Q_W, A_KV_W, A_KV_W, B_QK_W, B_QK_W, B_V_W, B_V_W)
    cuts = [int(c) for c in np.cumsum(widths)[:-1]]
    qa, ka, va, qb, fb, ib, gb = jnp.split(h @ w_in, cuts, axis=-1)
    qa = qa.reshape(bsz, length, A_HEADS, A_HEAD_DIM)
    ka = ka.reshape(bsz, length, A_KV_HEADS, A_HEAD_DIM)
    va = va.reshape(bsz, length, A_KV_HEADS, A_HEAD_DIM)
    if cache_k is None:
        oa = swa_prompt(qa, ka, va, sink)
        new_k, new_v = ka[:, -WINDOW:], va[:, -WINDOW:]
        s0 = jnp.zeros((bsz, B_HEADS, B_KEY_DIM, B_VAL_DIM), jnp.float32)
        block = B_BLOCK
    else:
        kk = jnp.concatenate([cache_k.astype(ka.dtype), ka], axis=1)
        vv = jnp.concatenate([cache_v.astype(va.dtype), va], axis=1)
        qq = qa.reshape(bsz, 1, length, A_KV_HEADS, A_GROUP, A_HEAD_DIM)
        oa = sink_attention(qq, kk[:, None], vv[:, None], sink,
                            jnp.ones((1, kk.shape[1]), bool)).reshape(bsz, length, A_Q_W)
        win = cache_k.shape[1]
        new_k, new_v = kk[:, -win:], vv[:, -win:]
        s0 = state.astype(jnp.float32)
        block = length
    fx = fb.astype(jnp.float32).reshape(bsz, length, B_HEADS, B_KEY_DIM)
    lbh = lb.reshape(B_HEADS, B_KEY_DIM)
    log_f = jnp.logaddexp(jax.nn.log_sigmoid(fx),
                          jnp.log(jnp.maximum(lbh, LB_FLOOR)) + jax.nn.log_sigmoid(-fx))
    k_in = (1.0 - lbh) * jax.nn.sigmoid(-fx)
    q_in = jax.nn.silu(qb.astype(jnp.float32)).reshape(bsz, length, B_HEADS, B_KEY_DIM)
    v_in = ib.astype(jnp.float32).reshape(bsz, length, B_HEADS, B_VAL_DIM)
    ob, s_new = hgrn2_blocks(q_in, k_in, v_in, log_f, s0, block)
    ob = rms_norm(ob.astype(h.dtype), out_gain) * jax.nn.silu(gb).reshape(bsz, length, B_HEADS, B_VAL_DIM)
    out = jnp.concatenate([oa, ob.reshape(bsz, length, B_V_W)], axis=-1) @ w_out
    return out, new_k, new_v, s_new.astype(h.dtype)


def odd_mixer(h, w_in, conv_w, w_out, cache):
    bsz, length, d = h.shape
    bg, cg, xv = jnp.split(h @ w_in, 3, axis=-1)
    u = cg * xv
    if cache is None:
        left = jnp.zeros((bsz, CONV_WIDTH - 1, d), u.dtype)
    else:
        left = cache.astype(u.dtype)
    up = jnp.concatenate([left, u], axis=1)
    y = up[:, 0:length] * conv_w[0]
    for j in range(1, CONV_WIDTH):
        y = y + up[:, j:j + length] * conv_w[j]
    return (bg * y) @ w_out, up[:, -(CONV_WIDTH - 1):]


def run_trunk(x, cache_k, cache_v, rec_state, conv_cache, norm_gains, w_ffn_gate, w_ffn_up, w_ffn_down,
              w_in_even, w_out_even, attn_sinks, lower_bounds, hgrn_norm_gain, w_in_odd, conv_w, w_out_odd):
    streaming = cache_k is not None
    new_k, new_v, new_rec, new_conv = [], [], [], []
    for layer in range(DEPTH):
        g = norm_gains[layer]
        x = x + FFN_RESIDUAL * rms_norm(
            swiglu(rms_norm(x, g[0]), w_ffn_gate[layer, 0], w_ffn_up[layer, 0], w_ffn_down[layer, 0]), g[1])
        h = rms_norm(x, g[2])
        j = layer // 2
        if layer % 2 == 0:
            m, k_rows, v_rows, s_new = even_mixer(
                h, w_in_even[j], w_out_even[j], attn_sinks[j], lower_bounds[j], hgrn_norm_gain[j],
                cache_k[j] if streaming else None, cache_v[j] if streaming else None,
                rec_state[j] if streaming else None)
            new_k.append(k_rows)
            new_v.append(v_rows)
            new_rec.append(s_new)
        else:
            m, c_rows = odd_mixer(h, w_in_odd[j], conv_w[j], w_out_odd[j],
                                  conv_cache[j] if streaming else None)
            new_conv.append(c_rows)
        x = x + rms_norm(m, g[3])
        x = x + FFN_RESIDUAL * rms_norm(
            swiglu(rms_norm(x, g[4]), w_ffn_gate[layer, 1], w_ffn_up[layer, 1], w_ffn_down[layer, 1]), g[5])
    return x, jnp.stack(new_k), jnp.stack(new_v), jnp.stack(new_rec), jnp.stack(new_conv)


def setup_inputs(seed: int = 0) -> dict:
    key = jax.random.key(seed)
    ks = jax.random.split(key, 19)
    nrm = lambda k, shape, s: jax.random.normal(k, shape, jnp.float32) * s
    win = min(WINDOW, PAST_LEN)
    return {
        'x_prompt': nrm(ks[0], (BATCH, SEQ, D_MODEL), 1.0),
        'x_sample': nrm(ks[1], (DEC_BATCH, DEC_SEQ, D_MODEL), 1.0),
        'cache_swa_k': nrm(ks[2], (N_EVEN, DEC_BATCH, win, A_KV_HEADS, A_HEAD_DIM), 1.0),
        'cache_swa_v': nrm(ks[3], (N_EVEN, DEC_BATCH, win, A_KV_HEADS, A_HEAD_DIM), 1.0),
        'state_hgrn': nrm(ks[4], (N_EVEN, DEC_BATCH, B_HEADS, B_KEY_DIM, B_VAL_DIM), 0.5),
        'cache_conv': nrm(ks[5], (N_ODD, DEC_BATCH, CONV_WIDTH - 1, D_MODEL), 1.0),
        'meta_tokens': nrm(ks[6], (N_META, D_MODEL), 1.0),
        'norm_gains': 1.0 + nrm(ks[7], (DEPTH, 6, D_MODEL), 0.05),
        'w_ffn_gate': nrm(ks[8], (DEPTH, 2, D_MODEL, D_FF), D_MODEL ** -0.5),
        'w_ffn_up': nrm(ks[9], (DEPTH, 2, D_MODEL, D_FF), D_MODEL ** -0.5),
        'w_ffn_down': nrm(ks[10], (DEPTH, 2, D_FF, D_MODEL), D_FF ** -0.5),
        'w_in_even': nrm(ks[11], (N_EVEN, D_MODEL, EVEN_IN_W), D_MODEL ** -0.5),
        'w_out_even': nrm(ks[12], (N_EVEN, EVEN_OUT_W, D_MODEL), EVEN_OUT_W ** -0.5),
        'attn_sinks': nrm(ks[13], (N_EVEN, A_HEADS), 0.5),
        'hgrn_lb_logits': 1.0 + nrm(ks[14], (N_EVEN, B_HEADS * B_KEY_DIM), 0.1),
        'hgrn_norm_gain': 1.0 + nrm(ks[15], (N_EVEN, B_HEADS, B_VAL_DIM), 0.05),
        'w_in_odd': nrm(ks[16], (N_ODD, D_MODEL, 3 * D_MODEL), D_MODEL ** -0.5),
        'conv_w': nrm(ks[17], (N_ODD, CONV_WIDTH, D_MODEL), CONV_WIDTH ** -0.5),
        'w_out_odd': nrm(ks[18], (N_ODD, D_MODEL, D_MODEL), D_MODEL ** -0.5),
    }


def reference(x_prompt, x_sample, cache_swa_k, cache_swa_v, state_hgrn, cache_conv, meta_tokens, norm_gains,
              w_ffn_gate, w_ffn_up, w_ffn_down, w_in_even, w_out_even, attn_sinks, hgrn_lb_logits,
              hgrn_norm_gain, w_in_odd, conv_w, w_out_odd):
    lower_bounds = hgrn_lower_bounds(hgrn_lb_logits)
    bsz = x_prompt.shape[0]
    meta = jnp.broadcast_to(meta_tokens[None].astype(x_prompt.dtype), (bsz, N_META, D_MODEL))
    xp = jnp.concatenate([meta, x_prompt], axis=1)
    yp, kp, vp, sp, cp = run_trunk(xp, None, None, None, None, norm_gains, w_ffn_gate, w_ffn_up, w_ffn_down,
                                   w_in_even, w_out_even, attn_sinks, lower_bounds, hgrn_norm_gain,
                                   w_in_odd, conv_w, w_out_odd)
    y_prompt = yp[:, N_META:]
    y_sample, ks_, vs_, ss_, cs_ = run_trunk(x_sample, cache_swa_k, cache_swa_v, state_hgrn, cache_conv,
                                             norm_gains, w_ffn_gate, w_ffn_up, w_ffn_down, w_in_even,
                                             w_out_even, attn_sinks, lower_bounds, hgrn_norm_gain,
                                             w_in_odd, conv_w, w_out_odd)
    return (y_prompt, y_sample, kp, vp, sp, cp, ks_, vs_, ss_, cs_)
```

```python
import numpy as np
from contextlib import ExitStack
import concourse.bass as bass
import concourse.mybir as mybir
from concourse.bass_utils import run_bass_kernel_spmd

F32 = mybir.dt.float32
BF16 = mybir.dt.bfloat16
AF = mybir.ActivationFunctionType
ALU = mybir.AluOpType
AX = mybir.AxisListType


class Reg:
    __slots__ = ("name", "last_w", "readers", "multi", "writers", "excl")

    def __init__(self, name, multi=False, excl=False):
        self.name = name
        self.last_w = None
        self.readers = []
        self.multi = multi
        self.writers = []
        self.excl = excl


class _Op:
    __slots__ = ("eng", "fn", "deps", "is_dma", "needs_inc", "sem", "val", "lane_prev", "idx")


EPOCH = 30000
NLANES = 32


class Sched:
    def __init__(self, nc):
        self.nc = nc
        self.stack = ExitStack()
        self.ops = []
        self.nsem = 0

    def sb(self, name, shape, dtype):
        return self.stack.enter_context(self.nc.sbuf_tensor(name, shape, dtype))

    def ps(self, name, shape, dtype):
        return self.stack.enter_context(self.nc.psum_tensor(name, shape, dtype))

    def _sem(self, name):
        self.nsem += 1
        return self.stack.enter_context(self.nc.semaphore(name))

    @staticmethod
    def _flat(regs):
        out = []
        for r in regs:
            if isinstance(r, (list, tuple)):
                out.extend(Sched._flat(r))
            else:
                out.append(r)
        return out

    def _add(self, eng, fn, reads, writes, is_dma):
        reads = self._flat(reads)
        writes = self._flat(writes)
        op = _Op()
        op.eng = eng
        op.fn = fn
        op.is_dma = is_dma
        op.needs_inc = False
        op.sem = None
        op.val = 0
        op.lane_prev = None
        op.idx = len(self.ops)
        deps = {}
        for r in reads:
            if r.multi:
                for w in r.writers:
                    deps[w.idx] = w
            else:
                if r.last_w is not None:
                    deps[r.last_w.idx] = r.last_w
                if r.excl:
                    for rd in r.readers:
                        if rd.eng != eng:
                            deps[rd.idx] = rd
        for w in writes:
            if w.multi:
                continue
            if w.last_w is not None:
                deps[w.last_w.idx] = w.last_w
            for rd in w.readers:
                deps[rd.idx] = rd
        for r in reads:
            if not r.multi:
                r.readers.append(op)
        for w in writes:
            if w.multi:
                w.writers.append(op)
            else:
                w.last_w = op
                w.readers = []
        final = []
        for i in sorted(deps):
            d = deps[i]
            if d is op:
                continue
            if (not is_dma) and (not d.is_dma) and d.eng == "pe" and eng == "pe":
                continue
            d.needs_inc = True
            final.append(d)
        op.deps = final
        self.ops.append(op)
        return op

    def op(self, eng, fn, reads=(), writes=()):
        return self._add(eng, fn, reads, writes, False)

    def dma(self, eng, fn, reads=(), writes=()):
        return self._add(eng, fn, reads, writes, True)

    def finish(self, outputs):
        self._add("sp", None, outputs, (), False)
        cnt = {}
        engsems = {}
        lanes = {}
        lane_cnt = {}
        lane_last = {}
        ndma_e = {}
        ndma = 0
        for op in self.ops:
            if op.is_dma:
                nd = ndma_e.get(op.eng, 0)
                ndma_e[op.eng] = nd + 1
                ndma += 1
                ln = (op.eng, nd % NLANES)
                n = lane_cnt.get(ln, 0)
                ep = n // (EPOCH // 16)
                lst = lanes.setdefault(ln, [])
                while len(lst) <= ep:
                    lst.append(self._sem("ln%s%d_%d" % (op.eng, ln[1], len(lst))))
                op.sem = lst[ep]
                op.val = (n - ep * (EPOCH // 16) + 1) * 16
                op.lane_prev = lane_last.get(ln)
                lane_last[ln] = op
                lane_cnt[ln] = n + 1
            elif op.needs_inc:
                n = cnt.get(op.eng, 0)
                ep = n // EPOCH
                lst = engsems.setdefault(op.eng, [])
                while len(lst) <= ep:
                    lst.append(self._sem("%s_%d" % (op.eng, len(lst))))
                op.sem = lst[ep]
                op.val = n - ep * EPOCH + 1
                cnt[op.eng] = n + 1
        by_eng = {}
        for op in self.ops:
            by_eng.setdefault(op.eng, []).append(op)
        self.stats = {k: len(v) for k, v in by_eng.items()}
        self.stats["ndma"] = ndma
        self.stats["nsem"] = self.nsem

        def emit(name, e):
            known = {}
            for op in by_eng.get(name, []):
                ws = list(op.deps)
                if op.lane_prev is not None:
                    ws.append(op.lane_prev)
                for d in ws:
                    k = id(d.sem)
                    if known.get(k, 0) < d.val:
                        e.wait_ge(d.sem, d.val)
                        known[k] = d.val
                if op.fn is not None:
                    ins = op.fn(e)
                    if op.is_dma:
                        ins.then_inc(op.sem, 16)
                    elif op.needs_inc:
                        ins.then_inc(op.sem, 1)

        with self.nc.Block() as block:
            @block.tensor
            def _(e):
                emit("pe", e)

            @block.scalar
            def _(e):
                emit("act", e)

            @block.vector
            def _(e):
                emit("dve", e)

            @block.gpsimd
            def _(e):
                emit("pool", e)

            @block.sync
            def _(e):
                emit("sp", e)
        self.stack.close()


D = 1024
KC = 8
DFF = 2816
SEQ = 4096
NMETA = 16
NSAMP = 32
SBF = 1024
NSB = SEQ // SBF
NCM = SBF + NMETA + NSAMP
CB = 1080
META0 = SBF
SAMP0 = SBF + NMETA
EPS = 1e-6
NEG = -30000.0
NCONST = 128 + 256 + 256 + 64


def host_consts():
    c = np.zeros((128, NCONST), np.float32)
    c[:, 0:128] = np.eye(128, dtype=np.float32)
    q = np.arange(128)[:, None]
    k = np.arange(256)[None, :]
    gen = np.where(q < 64, k < 192, k >= 64)
    c[:, 128:384] = np.where(gen, 0.0, NEG)
    first = np.where(k < 16, True, np.where(k < 128, False, np.where(q < 64, k < 192, True)))
    c[:, 384:640] = np.where(first, 0.0, NEG)
    s = (np.arange(128) % 64)[:, None]
    t = np.arange(64)[None, :]
    c[:, 640:704] = (s <= t).astype(np.float32)
    return c


class Ring:
    def __init__(self, items):
        self.items = items
        self.i = 0

    def next(self):
        it = self.items[self.i % len(self.items)]
        self.i += 1
        return it


LA_D, LA_C = 3, 3


def build_program(nc, plan=None, n_sb=NSB, n_layers=4, dbg_at=None):
    S = Sched(nc)
    T = {}

    def din(name, shape):
        T[name] = nc.dram_tensor(name, list(shape), F32, kind="ExternalInput").ap()
        return T[name]

    def dout(name, shape):
        T[name] = nc.dram_tensor(name, list(shape), F32, kind="ExternalOutput").ap()
        return T[name]

    xp = din("xp", [SEQ, D]); xs = din("xs", [NSAMP, D])
    ck = din("ck", [2, 128, 128]); cv = din("cv", [2, 128, 128])
    sh = din("sh", [2, 4, 128, 128]); ccv = din("ccv", [4, D])
    meta = din("meta", [NMETA, D]); gains = din("gains", [24, D])
    din("wg", [8, D, DFF]); din("wu", [8, D, DFF]); din("wd", [8, DFF, D])
    din("wie", [2, D, DFF]); din("woe", [2, D, D])
    sinks = din("sinks", [1, 16]); lbl = din("lbl", [2, 512]); hng = din("hng", [2, 512])
    din("wio", [2, D, 3 * D]); cw = din("cw", [6, D]); din("woo", [2, D, D])
    consts = din("consts", [128, NCONST])
    yp = dout("yp", [SEQ, D]); ys = dout("ys", [NSAMP, D])
    okp = dout("okp", [2, 128, 128]); ovp = dout("ovp", [2, 128, 128])
    ohp = dout("ohp", [2, 4, 128, 128]); ocp = dout("ocp", [4, D])
    oks = dout("oks", [2, 128, 128]); ovs = dout("ovs", [2, 128, 128])
    ohs = dout("ohs", [2, 4, 128, 128]); ocs = dout("ocs", [4, D])
    OUT = Reg("outputs", multi=True)
    dbgx = dout("dbgx", [128, KC * NCM]) if dbg_at is not None else None

    def sbt(name, shape, dt=F32):
        return S.sb(name, shape, dt), Reg(name)

    NBLK = 5
    xT = S.sb("xT", [128, KC, NCM], F32)
    hT = S.sb("hT", [128, KC, NCM], BF16)
    R_x = [Reg("x%d" % b) for b in range(NBLK)]
    R_h = [Reg("h%d" % b) for b in range(NBLK)]
    zT = S.sb("zT", [128, KC, NCM], BF16)
    R_z = [Reg("z%d" % c) for c in range(KC)]
    NCB = KC + 2
    yacc = S.sb("yacc", [128, KC, CB], F32)
    cbs = [yacc[:, i, :] for i in range(KC)] + [S.sb("cb%d" % i, [128, CB], F32)[:] for i in range(KC, NCB)]
    sqb, R_sqb = sbt("sqb", [128, KC, 256], BF16)
    R_cb = [[Reg("cb%d_%d" % (i, b)) for b in range(NBLK)] for i in range(NCB)]
    NSLOT, NSTG = 6, 1
    wsl = [S.sb("wsl%d" % i, [128, 4096], BF16) for i in range(NSLOT)]
    R_wsl = [Reg("wsl%d" % i) for i in range(NSLOT)]
    wst = [S.sb("wst%d" % i, [128, 2048], F32) for i in range(NSTG)]
    R_wst = [Reg("wst%d" % i) for i in range(NSTG)]
    cst, R_cst = sbt("cst", [128, NCONST])
    ident_f = cst[:, 0:128]
    mask_gen = cst[:, 128:384]
    mask_first = cst[:, 384:640]
    mask_hg = cst[:, 640:704]
    ident_b, R_idb = sbt("ident_b", [128, 128], BF16)
    ones_b, R_onb = sbt("ones_b", [128, 128], BF16)
    ones_f, R_onf = sbt("ones_f", [128, 1], F32)
    gainT, R_gain = sbt("gainT", [128, KC, 24])
    gpost, R_gpost = sbt("gpost", [128, KC, 24])
    cwT, R_cw = sbt("cwT", [128, KC, 6])
    sinkc, R_sink = sbt("sinkc", [128, 16])
    negsink = S.sb("negsink", [128, 16], F32)
    mskb, R_mskb = sbt("mskb", [128, 512], BF16)
    lbT, R_lbT = sbt("lbT", [128, 4, 2])
    lbv, R_lbv = sbt("lbv", [128, 4, 2])
    omlb, R_omlb = sbt("omlb", [128, 4, 2])
    hngT, R_hng = sbt("hngT", [128, 4, 2])
    kprev = [sbt("kprev%d" % j, [128, 128], BF16) for j in range(2)]
    vprev = [sbt("vprev%d" % j, [128, 128], BF16) for j in range(2)]
    vtm, R_vtm = sbt("vtm", [128, 10, 128], BF16)
    kvf, R_kvf = sbt("kvf", [128, 2, 128], F32)
    ckT, R_ckT = sbt("ckT", [128, 128], BF16)
    cvb, R_cvb = sbt("cvb", [128, 128], BF16)
    col_r = Ring([sbt("col%d" % i, [128, 20]) for i in range(2)])
    Sst = [[sbt("S%d_%d" % (j, h), [128, 128]) for h in range(4)] for j in range(2)]
    Ssm = [sbt("Ss%d" % h, [128, 128]) for h in range(4)]
    Sb_r = Ring([sbt("Sb%d" % i, [128, 128], BF16) for i in range(4)])
    tmp_r = Ring([sbt("tmpS%d" % i, [128, 128]) for i in range(3)])
    ktm_r = Ring([sbt("ktm%d" % i, [128, 128], BF16) for i in range(2)])
    pth_r = Ring([sbt("pth%d" % i, [128, 64], BF16) for i in range(4)])
    ctab, R_ctab = sbt("ctab", [128, 3, 20])
    uhalo = [sbt("uhalo%d" % j, [128, KC, 2]) for j in range(2)]
    uSh, R_uSh = sbt("uSh", [128, KC, 2])
    ccT, R_ccT = sbt("ccT", [128, KC, 4])
    uaux, R_uaux = sbt("uaux", [128, 64])
    sg_r = Ring([sbt("sg%d" % i, [128, 256]) for i in range(3)])
    a_r = Ring([sbt("a%d" % i, [128, 256], BF16) for i in range(8)])

    pY = S.ps("pY", [128, 2048], F32)
    pX = S.ps("pX", [128, 2048], F32)
    bank = [pY[:, i * 512:(i + 1) * 512] for i in range(4)] + [pX[:, i * 512:(i + 1) * 512] for i in range(4)]
    R_bank = [Reg("bank%d" % b, excl=True) for b in range(8)]
    gu_r = Ring([(bank[b], R_bank[b]) for b in range(4, 8)])
    misc_r = gu_r
    ya_r = Ring([(bank[b], R_bank[b]) for b in range(0, 2)])
    yb_r = Ring([(bank[b], R_bank[b]) for b in range(2, 4)])
    stg_r = Ring(list(zip(wst, R_wst)))
    slot_r = Ring(list(zip(wsl, R_wsl)))

    def pieces_of(spec):
        kind = spec[0]
        if kind == "km":
            _, tn, idx, cols = spec
            wm = T[tn][idx].rearrange("(k p) c -> p k c", p=128)
            out, off = [], 0
            for c0, w in cols:
                out.append((lambda st3, off=off, w=w: st3[:, :, off:off + w], wm[:, :, c0:c0 + w]))
                off += w
            return out, 8, off
        if kind == "rows":
            _, tn, idx, r0, ncc = spec
            src = T[tn][idx][r0:r0 + 128 * ncc, :].rearrange("(c p) m -> p c m", p=128)
            return [(lambda st3: st3[:, :, :], src)], ncc, 1024
        if kind == "woe":
            _, j, m0 = spec
            wm = T["woe"][j]
            out = []
            for q in range(4):
                out.append((lambda st3, q=q: st3[0:64, q, :], wm[64 * q:64 * q + 64, m0:m0 + 256]))
                out.append((lambda st3, q=q: st3[64:128, q, :], wm[256 + 64 * q:256 + 64 * q + 64, m0:m0 + 256]))
            out.append((lambda st3: st3[:, 4:8, :],
                        wm[512:1024, :].rearrange("(k p) c -> p k c", p=128)[:, :, m0:m0 + 256]))
            return out, 8, 256
        raise ValueError(kind)

    ws = {"rec": [], "dma": 0, "cast": 0, "h": [], "st": [], "pend": {}}

    def take_stage():
        idx = stg_r.i % NSTG
        stg, R_stg = stg_r.next()
        return idx, stg, R_stg

    ws["busy"] = [False] * NSLOT
    ws["nslot"] = 0

    def _w_dma(spec):
        si = ws["nslot"] % NSLOT
        if ws["busy"][si]:
            return False
        ws["nslot"] += 1
        ws["busy"][si] = True
        pieces, k, w = pieces_of(spec)
        slot, R_slot = wsl[si], R_wsl[si]
        sl3 = slot[:, 0:k * w].rearrange("p (k w) -> p k w", w=w)
        for dst_fn, src in pieces:
            S.dma("pool", lambda e, d=dst_fn(sl3), s=src: e.dma_start(out=d, in_=s), writes=[R_slot])
        ws["h"].append((sl3, R_slot))
        ws["dma"] += 1
        return True

    def relw(*regs):
        for r in regs:
            ws["busy"][R_wsl.index(r)] = False

    def getw(spec):
        i = len(ws["rec"])
        ws["rec"].append(spec)
        if plan is None:
            assert _w_dma(spec)
            return ws["h"][i]
        assert plan[i] == spec, (i, plan[i], spec)
        while ws["dma"] < min(len(plan), i + 1 + LA_D):
            if not _w_dma(plan[ws["dma"]]):
                break
        assert ws["dma"] > i, "weight slot ring exhausted (missing relw?)"
        return ws["h"][i]

    def rows_to_fm(src_rows, R, nch, dst_fn, wregs):
        _, stg, R_stg = take_stage()
        S.dma("pool", lambda e: e.dma_start(out=stg[0:R, 0:nch * 128], in_=src_rows), writes=[R_stg])
        g = max(1, min(256 // R, nch))
        for c0 in range(0, nch, g):
            n = min(g, nch - c0)
            pt, R_pt = misc_r.next()

            def f(e, c0=c0, n=n, pt=pt):
                ins = None
                for c in range(n):
                    ins = e.transpose(pt[:, c * R:(c + 1) * R],
                                      stg[0:R, (c0 + c) * 128:(c0 + c + 1) * 128], ident_f[0:R, 0:R])
                return ins
            S.op("pe", f, reads=[R_stg, R_cst], writes=[R_pt])
            S.op("dve", lambda e, c0=c0, n=n, pt=pt: e.tensor_copy(
                out=dst_fn(c0, n), in_=pt[:, 0:n * R].rearrange("p (c r) -> p c r", r=R)),
                reads=[R_pt], writes=wregs)

    def fm_to_rows(src_fn, R, nch, dst_rows, rregs):
        _, stg, R_stg = take_stage()
        for c0 in range(0, nch, 2):
            n = min(2, nch - c0)
            pt, R_pt = misc_r.next()

            def f(e, c0=c0, n=n, pt=pt):
                ins = None
                for c in range(n):
                    ins = e.transpose(pt[0:R, c * 128:(c + 1) * 128], src_fn(c0 + c), ident_f)
                return ins
            S.op("pe", f, reads=rregs + [R_cst], writes=[R_pt])
            S.op("dve", lambda e, c0=c0, n=n, pt=pt: e.tensor_copy(
                out=stg[0:R, c0 * 128:(c0 + n) * 128], in_=pt[0:R, 0:n * 128]),
                reads=[R_pt], writes=[R_stg])
        S.dma("pool", lambda e: e.dma_start(out=dst_rows, in_=stg[0:R, 0:nch * 128]), reads=[R_stg], writes=[OUT])

    def bfv(i):
        return cbs[i].bitcast(BF16)

    sm_r = Ring([(cbs[5][:, 0:1024], R_cb[5]), (cbs[6][:, 0:1024], R_cb[6])])
    pn_r = Ring([(bfv(7)[:, 0:1024], R_cb[7]), (bfv(8)[:, 0:1024], R_cb[8])])
    pT_r = Ring([(bfv(7)[:, 1024:2048].rearrange("p (g q) -> p g q", q=128), R_cb[7]),
                 (bfv(8)[:, 1024:2048].rearrange("p (g q) -> p g q", q=128), R_cb[8])])

    def blocks_of(ncols):
        bl = [(c, 256) for c in range(0, SBF, 256)]
        if ncols > SBF:
            bl.append((SBF, ncols - SBF))
        return bl

    def mm_group(out, lhs_fn, rhs_fn, nk, reads, writes):
        ops = [(lhs_fn(k), rhs_fn(k)) for k in range(nk)]

        def f(e):
            ins = None
            for k, (l, r) in enumerate(ops):
                ins = e.matmul(out, lhsT=l, rhs=r, start=(k == 0), stop=(k == nk - 1))
            return ins
        S.op("pe", f, reads=reads, writes=writes)

    def setup():
        S.dma("sp", lambda e: e.dma_start(out=cst[:], in_=consts[:, :]), writes=[R_cst])
        S.op("dve", lambda e: e.tensor_copy(out=ident_b[:], in_=ident_f), reads=[R_cst], writes=[R_idb])
        S.op("dve", lambda e: e.memset(ones_b[:], 1.0), writes=[R_onb])
        S.op("dve", lambda e: e.memset(ones_f[:], 1.0), writes=[R_onf])
        rows_to_fm(gains[:, :], 24, KC, lambda c0, n: gainT[:, c0:c0 + n, :], [R_gain])
        S.op("dve", lambda e: e.tensor_scalar(out=gpost[:], in0=gainT[:], scalar1=0.5, scalar2=None, op0=ALU.mult),
             reads=[R_gain], writes=[R_gpost])
        g4 = gainT[:].rearrange("p c (l i) -> p c l i", i=6)
        gp4 = gpost[:].rearrange("p c (l i) -> p c l i", i=6)
        S.op("dve", lambda e: e.tensor_copy(out=gp4[:, :, :, 3], in_=g4[:, :, :, 3]), reads=[R_gain], writes=[R_gpost])
        rows_to_fm(cw[:, :], 6, KC, lambda c0, n: cwT[:, c0:c0 + n, :], [R_cw])
        rows_to_fm(ccv[:, :], 4, KC, lambda c0, n: ccT[:, c0:c0 + n, :], [R_ccT])
        rows_to_fm(lbl[:, :], 2, 4, lambda c0, n: lbT[:, c0:c0 + n, :], [R_lbT])
        rows_to_fm(hng[:, :], 2, 4, lambda c0, n: hngT[:, c0:c0 + n, :], [R_hng])
        S.dma("pool", lambda e: e.dma_start(out=sinkc[:], in_=sinks[0, :].partition_broadcast(128)), writes=[R_sink])
        S.op("dve", lambda e: e.tensor_scalar(out=negsink[:], in0=sinkc[:], scalar1=-1.0, scalar2=None, op0=ALU.mult),
             reads=[R_sink], writes=[R_sink])
        S.op("dve", lambda e: e.tensor_scalar(out=mskb[:], in0=cst[:, 128:640], scalar1=1.0 / (64 ** -0.5), scalar2=None, op0=ALU.mult),
             reads=[R_cst], writes=[R_mskb])
        S.op("dve", lambda e: e.memset(lbv[:], 0.0), writes=[R_lbv])
        S.op("dve", lambda e: e.tensor_tensor(out=lbv[:, :, 1], in0=lbT[:, :, 1], in1=lbT[:, :, 0], op=ALU.subtract),
             reads=[R_lbT], writes=[R_lbv])
        S.op("act", lambda e: e.activation(out=lbv[:, :, 1], in_=lbv[:, :, 1], func=AF.Sigmoid), reads=[R_lbv], writes=[R_lbv])
        S.op("dve", lambda e: e.tensor_scalar(out=omlb[:], in0=lbv[:], scalar1=-1.0, scalar2=1.0, op0=ALU.mult, op1=ALU.add),
             reads=[R_lbv], writes=[R_omlb])
        for j in range(2):
            S.op("dve", lambda e, j=j: e.memset(kprev[j][0][:], 0.0), writes=[kprev[j][1]])
            S.op("dve", lambda e, j=j: e.memset(vprev[j][0][:], 0.0), writes=[vprev[j][1]])
            for h in range(4):
                S.op("dve", lambda e, j=j, h=h: e.memset(Sst[j][h][0][:], 0.0), writes=[Sst[j][h][1]])

    def load_x(sb):
        for t in range(SBF // 128):
            r0 = sb * SBF + t * 128
            rows_to_fm(xp[r0:r0 + 128, :], 128, KC,
                       lambda c0, n, t=t: xT[:, c0:c0 + n, t * 128:(t + 1) * 128], [R_x])
        if sb == 0:
            rows_to_fm(meta[:, :], NMETA, KC, lambda c0, n: xT[:, c0:c0 + n, META0:META0 + NMETA], [R_x])
            rows_to_fm(xs[:, :], NSAMP, KC, lambda c0, n: xT[:, c0:c0 + n, SAMP0:SAMP0 + NSAMP], [R_x])

    def store_y(sb):
        for t in range(SBF // 128):
            r0 = sb * SBF + t * 128
            fm_to_rows(lambda c, t=t: xT[:, c, t * 128:(t + 1) * 128], 128, KC, yp[r0:r0 + 128, :], [R_x])
        if sb == 0:
            fm_to_rows(lambda c: xT[:, c, SAMP0:SAMP0 + NSAMP], NSAMP, KC, ys[:, :], [R_x])

    def rstd_pieces(src3, src_regs, nchunks, c0, n, denom, out_cb):
        bi = c0 // 256
        rs = cbs[out_cb]
        st = {}

        def p1():
            S.op("act", lambda e: e.activation(out=sqb[:, 0:nchunks, 0:n], in_=src3, func=AF.Square), reads=src_regs, writes=[R_sqb])

        def p2():
            pt, R_pt = misc_r.next()
            mm_group(pt[:, 0:n], lambda k: ones_b[:], lambda k: sqb[:, k, 0:n], nchunks, [R_onb, R_sqb], [R_pt])
            S.op("act", lambda e: e.activation(out=rs[:, c0:c0 + n], in_=pt[:, 0:n], func=AF.Sqrt, bias=EPS, scale=1.0 / denom),
                 reads=[R_pt], writes=[R_cb[out_cb][bi]])
            S.op("dve", lambda e: e.reciprocal(out=rs[:, c0:c0 + n], in_=rs[:, c0:c0 + n]),
                 reads=[R_cb[out_cb][bi]], writes=[R_cb[out_cb][bi]])
        return [p1, p2]

    def prenorm_pieces(gi, c0, n):
        bi = c0 // 256
        ps = rstd_pieces(xT[:, :, c0:c0 + n], [R_x[bi]], KC, c0, n, float(D), 9)

        def hops(cs):
            for c in cs:
                S.op("dve", lambda e, c=c: e.scalar_tensor_tensor(
                    out=hT[:, c, c0:c0 + n], in0=xT[:, c, c0:c0 + n], scalar=gainT[:, c, gi:gi + 1], in1=cbs[9][:, c0:c0 + n],
                    op0=ALU.mult, op1=ALU.mult), reads=[R_x[bi], R_gain, R_cb[9][bi]], writes=[R_h[bi]])
        return ps + [lambda: hops(range(0, 4)), lambda: hops(range(4, 8))]

    def postnorm_pieces(gi, c0, n):
        bi = c0 // 256
        ps = rstd_pieces(yacc[:, 0:KC, c0:c0 + n], [R_cb[c][bi] for c in range(KC)], KC, c0, n, float(D), 9)

        def xops(cs):
            for c in cs:
                S.op("dve", lambda e, c=c: e.tensor_tensor(out=yacc[:, c, c0:c0 + n], in0=yacc[:, c, c0:c0 + n],
                                                           in1=cbs[9][:, c0:c0 + n], op=ALU.mult),
                     reads=[R_cb[c][bi], R_cb[9][bi]], writes=[R_cb[c][bi]])
                S.op("dve", lambda e, c=c: e.scalar_tensor_tensor(
                    out=xT[:, c, c0:c0 + n], in0=yacc[:, c, c0:c0 + n], scalar=gpost[:, c, gi:gi + 1], in1=xT[:, c, c0:c0 + n],
                    op0=ALU.mult, op1=ALU.add), reads=[R_cb[c][bi], R_gpost, R_x[bi]], writes=[R_x[bi]])
        return ps + [lambda: xops(range(0, 4)), lambda: xops(range(4, 8))]

    epi_q = []

    def pump(k=1):
        for _ in range(k):
            if not epi_q:
                return
            epi_q.pop(0)[1]()

    def need_blk(bi):
        while any(b_ == bi for (b_, _) in epi_q):
            epi_q.pop(0)[1]()

    def flush_epi():
        while epi_q:
            epi_q.pop(0)[1]()

    def prenorm_blk(gi, c0, n):
        for p in prenorm_pieces(gi, c0, n):
            p()

    def rstd_blk(src3, src_regs, nchunks, c0, n, denom, out_cb):
        for p in rstd_pieces(src3, src_regs, nchunks, c0, n, denom, out_cb):
            p()

    def prenorm(gi, ncols):
        for (c0, n) in blocks_of(ncols):
            prenorm_blk(gi, c0, n)

    def rstd_of(src_fn, src_regs, nchunks, ncols, denom, out_cb):
        for (c0, n) in blocks_of(ncols):
            rstd_blk(src_fn(c0, n), src_regs, nchunks, c0, n, denom, out_cb)

    Y3 = pY[:].rearrange("p (m n) -> p m n", n=256)
    R_Y = R_bank[0:4]

    def ffn(fi, ncols, on_block_final):
        blocks = blocks_of(ncols)
        r0 = 0
        while r0 < DFF:
            ncc = 2 if r0 == 0 else 4
            first = (r0 == 0)
            final = (r0 + 128 * ncc >= DFF)
            G, RG = getw(("km", "wg", fi, ((r0, 128 * ncc),)))
            U, RU = getw(("km", "wu", fi, ((r0, 128 * ncc),)))
            Dn, RD = getw(("rows", "wd", fi, r0, ncc))
            r0 += 128 * ncc
            steps = [(c0, n, cc) for (c0, n) in blocks for cc in range(ncc)]
            atiles = {}

            def down(hf, c0, n, ccs):
                ops = []
                for cc in ccs:
                    at, Ra = atiles[(c0, cc)]
                    for m in range(4 * hf, 4 * hf + 4):
                        ops.append((Y3[:, m, 0:n], Dn[:, cc, m * 128:(m + 1) * 128], at[:, 0:n],
                                    cc == 0 and m % 2 == 0, cc == ncc - 1))

                def f(e, ops=ops):
                    ins = None
                    for (o, l, r_, st, sp) in ops:
                        ins = e.matmul(o, lhsT=l, rhs=r_, start=st, stop=sp, skip_group_check=True)
                    return ins
                S.op("pe", f, reads=[RD] + [atiles[(c0, cc)][1] for cc in ccs], writes=R_Y[2 * hf:2 * hf + 2])
                if ccs[-1] == ncc - 1:
                    ysl = yacc[:, 4 * hf:4 * hf + 4, c0:c0 + n]
                    psl = Y3[:, 4 * hf:4 * hf + 4, 0:n]
                    yregs = [R_cb[c][c0 // 256] for c in range(4 * hf, 4 * hf + 4)]
                    if first:
                        S.op("dve", lambda e: e.tensor_copy(out=ysl, in_=psl), reads=R_Y[2 * hf:2 * hf + 2], writes=yregs)
                    else:
                        S.op("dve", lambda e: e.tensor_tensor(out=ysl, in0=psl, in1=ysl, op=ALU.add),
                             reads=R_Y[2 * hf:2 * hf + 2] + yregs, writes=yregs)

            def retire(step):
                c0, n, cc = step
                down(0, c0, n, [cc])
                if cc == ncc - 1:
                    down(1, c0, n, list(range(ncc)))

            prev = None
            fin_q = []
            for (c0, n, cc) in steps:
                need_blk(c0 // 256)
                pgu, Rpg = gu_r.next()
                pg, pu = pgu[:, 0:256], pgu[:, 256:512]
                mm_group(pg[:, 0:n], lambda k: G[:, k, cc * 128:(cc + 1) * 128],
                         lambda k: hT[:, k, c0:c0 + n], KC, [RG, R_h[c0 // 256]], [Rpg])
                mm_group(pu[:, 0:n], lambda k: U[:, k, cc * 128:(cc + 1) * 128],
                         lambda k: hT[:, k, c0:c0 + n], KC, [RU, R_h[c0 // 256]], [Rpg])
                if fin_q and cc == 1:
                    on_block_final(*fin_q.pop(0))
                pump(2)
                sgt, Rsg = sg_r.next()
                at, Ra = a_r.next()
                atiles[(c0, cc)] = (at, Ra)
                S.op("act", lambda e, sgt=sgt, pg=pg, n=n: e.activation(out=sgt[:, 0:n], in_=pg[:, 0:n], func=AF.Silu),
                     reads=[Rpg], writes=[Rsg])
                S.op("dve", lambda e, at=at, sgt=sgt, pu=pu, n=n: e.tensor_tensor(out=at[:, 0:n], in0=sgt[:, 0:n],
                                                                                   in1=pu[:, 0:n], op=ALU.mult),
                     reads=[Rsg, Rpg], writes=[Ra])
                if prev is not None:
                    retire(prev)
                    if final and prev[2] == ncc - 1:
                        fin_q.append((prev[0], prev[1]))
                prev = (c0, n, cc)
            retire(prev)
            relw(RG, RU, RD)
            if final:
                fin_q.append((prev[0], prev[1]))
                while fin_q:
                    on_block_final(*fin_q.pop(0))

    def proj_fm(Wt, RW, coff, blocks, evac):
        for (c0, n) in blocks:
            need_blk(c0 // 256)
            pt, R_pt = gu_r.next()
            mm_group(pt[:, 0:n], lambda k: Wt[:, k, coff:coff + 128], lambda k, c0=c0, n=n: hT[:, k, c0:c0 + n],
                     KC, [RW, R_h[c0 // 256]], [R_pt])
            evac(pt, R_pt, c0, n)
            pump(1)

    def proj_tm(Wt, RW, coff, c0, ntok):
        need_blk(c0 // 256)
        pt, R_pt = gu_r.next()
        mm_group(pt[0:ntok, 0:128], lambda k: hT[:, k, c0:c0 + ntok], lambda k: Wt[:, k, coff:coff + 128],
                 KC, [RW, R_h], [R_pt])
        return pt, R_pt

    def out_proj(specs, ncols, on_block_final):
        Ws = [getw(spec) for spec in specs]
        alt = 0
        pending = None
        for (c0, n) in blocks_of(ncols):
            for mp, (Wt, RW) in enumerate(Ws):
                for mm in range(2):
                    m = 2 * mp + mm
                    pt, R_pt = gu_r.next()
                    mm_group(pt[:, 0:n], lambda k: Wt[:, k, mm * 128:(mm + 1) * 128],
                             lambda k: zT[:, k, c0:c0 + n], KC, [RW] + R_z, [R_pt])
                    if alt % 2 == 0:
                        S.op("act", lambda e, pt=pt, m=m, c0=c0, n=n: e.activation(out=yacc[:, m, c0:c0 + n], in_=pt[:, 0:n],
                                                                                  func=AF.Copy),
                             reads=[R_pt], writes=[R_cb[m][c0 // 256]])
                    else:
                        S.op("dve", lambda e, pt=pt, m=m, c0=c0, n=n: e.tensor_copy(out=yacc[:, m, c0:c0 + n], in_=pt[:, 0:n]),
                             reads=[R_pt], writes=[R_cb[m][c0 // 256]])
                    alt += 1
                    pump(1)
                if mp == 0 and pending is not None:
                    on_block_final(*pending)
                    pending = None
            pending = (c0, n)
        relw(*[rw for (_, rw) in Ws])
        on_block_final(*pending)

    def tiles_of(sb):
        tl = [(t * 128, 128, t) for t in range(SBF // 128)]
        if sb == 0:
            tl += [(META0, NMETA, 8), (SAMP0, NSAMP, 9)]
        return tl

    SCALE = 64 ** -0.5

    S2_r = Ring([(pY[:, 0:1024], R_bank[0:2]), (pY[:, 1024:2048], R_bank[2:4])])

    def attn_tile(j, qa, R_qa, qc0, nq, key_tiles, key_regs, mask):
        nkt = sum(k[2] for k in key_tiles)
        po, R_po = gu_r.next()
        po3 = po.rearrange("p (z q) -> p z q", q=128)
        for half in range(2):
            p0 = 64 * half
            s2, R_s2 = S2_r.next()
            s3 = s2.rearrange("p (z k) -> p z k", k=256)
            groups = []
            for zc in range(4):
                off = 0
                for ti, (kT, vt, nk) in enumerate(key_tiles):
                    groups.append((s3[0:nq, zc, off:off + nk], qa[zc][p0:p0 + 64, qc0:qc0 + nq], kT[p0:p0 + 64, 0:nk],
                                   zc % 2 == 0 and ti == 0, mask is None))
                    off += nk
                if mask is not None:
                    groups.append((s3[0:nq, zc, 0:nkt], ident_b[0:nq, 0:nq], mask[0:nq, 0:nkt], False, True))

            def fs(e, groups=groups):
                ins = None
                for (o, l, r_, st, sp) in groups:
                    ins = e.matmul(o, lhsT=l, rhs=r_, start=st, stop=sp, skip_group_check=True)
                return ins
            S.op("pe", fs, reads=R_qa + key_regs + [R_idb, R_mskb], writes=R_s2)
            sm, R_sm = sm_r.next()
            sm3 = sm.rearrange("p (z k) -> p z k", k=256)
            pn, R_pn = pn_r.next()
            pn3 = pn.rearrange("p (z k) -> p z k", k=256)
            pT, R_pT = pT_r.next()
            cl, R_cl = col_r.next()
            S.op("dve", lambda e, s3=s3, cl=cl: e.reduce_max(out=cl[0:nq, 0:4], in_=s3[0:nq, :, 0:nkt], axis=AX.X),
                 reads=R_s2, writes=[R_cl])
            nsk = negsink[0:nq, 8 * j + 4 * half:8 * j + 4 * half + 4]
            S.op("dve", lambda e, cl=cl, nsk=nsk: e.scalar_tensor_tensor(
                out=cl[0:nq, 4:8], in0=cl[0:nq, 0:4], scalar=-SCALE, in1=nsk, op0=ALU.mult, op1=ALU.min),
                reads=[R_cl, R_sink], writes=[R_cl])
            for zc in range(4):
                S.op("act", lambda e, s3=s3, sm3=sm3, cl=cl, zc=zc: e.activation(
                    out=sm3[0:nq, zc, 0:nkt], in_=s3[0:nq, zc, 0:nkt], func=AF.Exp, bias=cl[0:nq, 4 + zc:5 + zc], scale=SCALE),
                    reads=R_s2 + [R_cl], writes=[R_sm])
            S.op("dve", lambda e, cl=cl, nsk=nsk: e.tensor_tensor(out=cl[0:nq, 8:12], in0=cl[0:nq, 4:8], in1=nsk, op=ALU.subtract),
                 reads=[R_cl, R_sink], writes=[R_cl])
            S.op("act", lambda e, cl=cl: e.activation(out=cl[0:nq, 8:12], in_=cl[0:nq, 8:12], func=AF.Exp), reads=[R_cl], writes=[R_cl])
            S.op("dve", lambda e, sm3=sm3, cl=cl: e.reduce_sum(out=cl[0:nq, 12:16], in_=sm3[0:nq, :, 0:nkt], axis=AX.X),
                 reads=[R_sm], writes=[R_cl])
            S.op("dve", lambda e, cl=cl: e.tensor_tensor(out=cl[0:nq, 12:16], in0=cl[0:nq, 12:16], in1=cl[0:nq, 8:12], op=ALU.add),
                 reads=[R_cl], writes=[R_cl])
            S.op("dve", lambda e, cl=cl: e.reciprocal(out=cl[0:nq, 16:20], in_=cl[0:nq, 12:16]), reads=[R_cl], writes=[R_cl])
            S.op("dve", lambda e, sm3=sm3, pn3=pn3, cl=cl: e.tensor_tensor(
                out=pn3[0:nq, :, 0:nkt], in0=sm3[0:nq, :, 0:nkt], in1=cl[0:nq, 16:20].unsqueeze(2).to_broadcast([nq, 4, nkt]),
                op=ALU.mult), reads=[R_sm, R_cl], writes=[R_pn])
            pt, R_pt = gu_r.next()
            ptb = pt.bitcast(BF16).rearrange("p (g q) -> p g q", q=128)
            nkt_ = len(key_tiles)
            trs = []
            for zc in range(4):
                off = 0
                for ti, (kT, vt, nk) in enumerate(key_tiles):
                    trs.append((ptb[0:nk, zc * nkt_ + ti, 0:nq], pn3[0:nq, zc, off:off + nk]))
                    off += nk

            def ftr(e, trs=trs):
                ins = None
                for (o, i_) in trs:
                    ins = e.transpose(o, i_, ident_b[0:nq, 0:nq])
                return ins
            S.op("pe", ftr, reads=[R_pn, R_idb], writes=[R_pt])
            for ti, (kT, vt, nk) in enumerate(key_tiles):
                src_v = ptb[0:nk, ti:4 * nkt_:nkt_, 0:nq] if nkt_ > 1 else ptb[0:nk, 0:4, 0:nq]
                dst_v = pT[0:nk, ti:4 * nkt_:nkt_, 0:nq] if nkt_ > 1 else pT[0:nk, 0:4, 0:nq]
                S.op("act", lambda e, src_v=src_v, dst_v=dst_v: e.activation(out=dst_v, in_=src_v, func=AF.Copy),
                     reads=[R_pt], writes=[R_pT])
            pvs = []
            for zc in range(4):
                for ti, (kT, vt, nk) in enumerate(key_tiles):
                    pvs.append((po3[p0:p0 + 64, zc, 0:nq], vt[0:nk, p0:p0 + 64], pT[0:nk, zc * nkt_ + ti, 0:nq],
                                ti == 0, ti == nkt_ - 1))

            def fpv(e, pvs=pvs):
                ins = None
                for (o, l, r_, st, sp) in pvs:
                    ins = e.matmul(o, lhsT=l, rhs=r_, start=st, stop=sp, skip_group_check=True)
                return ins
            S.op("pe", fpv, reads=[R_pT] + key_regs, writes=[R_po])
            pump(1)
        S.op("act", lambda e: e.activation(out=zT[:, 0:4, qc0:qc0 + nq], in_=po3[:, :, 0:nq], func=AF.Copy),
             reads=[R_po], writes=R_z[0:4])

    BOFF = {"fr": 0, "meta": SBF + 2, "samp": SBF + 2 + NMETA + 2}
    CTI = {"fr": 0, "meta": 16, "samp": 17}

    def even_mixer(j, sb, ncols, on_block_final):
        last = (sb == n_sb - 1)
        blocks = blocks_of(ncols)
        tiles = tiles_of(sb)
        wn = "wie"
        qa = [bfv(c) for c in range(4)]
        R_qa = [R_cb[c] for c in range(4)]
        kf, R_kf = bfv(4), R_cb[4]
        for half2 in range(2):
            z0 = 2 * half2
            W, RW = getw(("km", wn, j, ((64 * z0, 64), (256 + 64 * z0, 64), (64 * (z0 + 1), 64), (256 + 64 * (z0 + 1), 64))))
            for zi in range(2):
                zc = z0 + zi
                proj_fm(W, RW, 128 * zi, blocks, lambda pt, R_pt, c0, n, zc=zc: S.op(
                    "act", lambda e: e.activation(out=qa[zc][:, c0:c0 + n], in_=pt[:, 0:n], func=AF.Copy),
                    reads=[R_pt], writes=[R_qa[zc]]))
            relw(RW)
        W, RW = getw(("km", wn, j, ((512, 256),)))
        proj_fm(W, RW, 0, blocks, lambda pt, R_pt, c0, n: S.op(
            "act", lambda e: e.activation(out=kf[:, c0:c0 + n], in_=pt[:, 0:n], func=AF.Copy), reads=[R_pt], writes=[R_kf]))
        for (c0, ntok, ti) in tiles:
            pt, R_pt = proj_tm(W, RW, 128, c0, ntok)
            S.op("dve", lambda e, pt=pt, ntok=ntok, ti=ti: e.tensor_copy(out=vtm[0:ntok, ti, :], in_=pt[0:ntok, 0:128]),
                 reads=[R_pt], writes=[R_vtm])
            if last and ti == 7:
                S.op("act", lambda e, pt=pt: e.activation(out=kvf[:, 1, :], in_=pt[:, 0:128], func=AF.Copy),
                     reads=[R_pt], writes=[R_kvf])
                S.dma("pool", lambda e: e.dma_start(out=ovp[j], in_=kvf[:, 1, :]), reads=[R_kvf], writes=[OUT])
                pk, R_pk = proj_tm(W, RW, 0, c0, ntok)
                S.op("act", lambda e, pk=pk: e.activation(out=kvf[:, 0, :], in_=pk[:, 0:128], func=AF.Copy),
                     reads=[R_pk], writes=[R_kvf])
                S.dma("pool", lambda e: e.dma_start(out=okp[j], in_=kvf[:, 0, :]), reads=[R_kvf], writes=[OUT])
            if ti == 9:
                S.op("act", lambda e, pt=pt: e.activation(out=kvf[0:NSAMP, 1, :], in_=pt[0:NSAMP, 0:128], func=AF.Copy),
                     reads=[R_pt], writes=[R_kvf])
                S.dma("pool", lambda e: e.dma_start(out=ovs[j, 96:128, :], in_=kvf[0:NSAMP, 1, :]), reads=[R_kvf], writes=[OUT])
                pk, R_pk = proj_tm(W, RW, 0, c0, ntok)
                S.op("act", lambda e, pk=pk: e.activation(out=kvf[0:NSAMP, 0, :], in_=pk[0:NSAMP, 0:128], func=AF.Copy),
                     reads=[R_pk], writes=[R_kvf])
                S.dma("pool", lambda e: e.dma_start(out=oks[j, 96:128, :], in_=kvf[0:NSAMP, 0, :]), reads=[R_kvf], writes=[OUT])
                S.dma("pool", lambda e: e.dma_start(out=oks[j, 0:96, :], in_=ck[j, 32:128, :]), writes=[OUT])
                S.dma("pool", lambda e: e.dma_start(out=ovs[j, 0:96, :], in_=cv[j, 32:128, :]), writes=[OUT])
        relw(RW)
        kp, R_kp = kprev[j]
        vp, R_vp = vprev[j]
        if sb == 0:
            attn_tile(j, qa, R_qa, META0, NMETA, [(kf[:, META0:META0 + NMETA], vtm[:, 8, :], NMETA)], [R_kf, R_vtm], None)
            S.op("dve", lambda e: e.tensor_copy(out=kp[:, 0:NMETA], in_=kf[:, META0:META0 + NMETA]), reads=[R_kf], writes=[R_kp])
            S.op("dve", lambda e: e.tensor_copy(out=vp[0:NMETA, :], in_=vtm[0:NMETA, 8, :]), reads=[R_vtm], writes=[R_vp])
        for t in range(SBF // 128):
            if t == 0:
                kts = [(kp[:], vp[:], 128), (kf[:, 0:128], vtm[:, 0, :], 128)]
                kregs = [R_kp, R_vp, R_kf, R_vtm]
            else:
                kts = [(kf[:, (t - 1) * 128:t * 128], vtm[:, t - 1, :], 128), (kf[:, t * 128:(t + 1) * 128], vtm[:, t, :], 128)]
                kregs = [R_kf, R_vtm]
            attn_tile(j, qa, R_qa, t * 128, 128, kts, kregs, mskb[:, 256:512] if (sb == 0 and t == 0) else mskb[:, 0:256])
        if not last:
            S.op("dve", lambda e: e.tensor_copy(out=kp[:], in_=kf[:, SBF - 128:SBF]), reads=[R_kf], writes=[R_kp])
            S.op("dve", lambda e: e.tensor_copy(out=vp[:], in_=vtm[:, 7, :]), reads=[R_vtm], writes=[R_vp])
        if sb == 0:
            _, stg, R_stg = take_stage()
            S.dma("pool", lambda e: e.dma_start(out=stg[:, 0:128], in_=ck[j]), writes=[R_stg])
            S.dma("pool", lambda e: e.dma_start(out=stg[:, 128:256], in_=cv[j]), writes=[R_stg])
            S.op("dve", lambda e: e.tensor_copy(out=cvb[:], in_=stg[:, 128:256]), reads=[R_stg], writes=[R_cvb])
            ckb, R_ckb = bfv(7)[:, 0:128], R_cb[7]
            S.op("dve", lambda e: e.tensor_copy(out=ckb, in_=stg[:, 0:128]), reads=[R_stg], writes=[R_ckb])
            pt, R_pt = gu_r.next()
            ptb = pt.bitcast(BF16)
            S.op("pe", lambda e: e.transpose(ptb[:, 0:128], ckb, ident_b[:]), reads=[R_ckb, R_idb], writes=[R_pt])
            S.op("act", lambda e: e.activation(out=ckT[:], in_=ptb[:, 0:128], func=AF.Copy), reads=[R_pt], writes=[R_ckT])
            attn_tile(j, qa, R_qa, SAMP0, NSAMP, [(ckT[:], cvb[:], 128), (kf[:, SAMP0:SAMP0 + NSAMP], vtm[:, 9, :], NSAMP)],
                      [R_ckT, R_cvb, R_kf, R_vtm], None)
        for h in range(4):
            hgrn_head(j, h, sb, ncols, blocks, tiles)
        out_proj([("woe", j, 256 * mp) for mp in range(4)], ncols, on_block_final)

    def hgrn_head(j, h, sb, ncols, blocks, tiles):
        last = (sb == n_sb - 1)
        qs, R_qs = bfv(0), R_cb[0]
        kin, R_kin = bfv(1), R_cb[1]
        gs, R_gs = bfv(2), R_cb[2]
        A, R_A = cbs[3], R_cb[3]
        Bx, R_B = cbs[4], R_cb[4]
        C, R_C = cbs[5], R_cb[5]
        oh, R_oh = cbs[6], R_cb[6]
        W1, RW1 = getw(("km", "wie", j, ((768 + 128 * h, 128), (1280 + 128 * h, 128))))
        W2, RW2 = getw(("km", "wie", j, ((1792 + 128 * h, 128), (2304 + 128 * h, 128))))
        proj_fm(W1, RW1, 0, blocks, lambda pt, R_pt, c0, n: S.op(
            "act", lambda e: e.activation(out=qs[:, c0:c0 + n], in_=pt[:, 0:n], func=AF.Silu), reads=[R_pt], writes=[R_qs]))
        proj_fm(W2, RW2, 128, blocks, lambda pt, R_pt, c0, n: S.op(
            "act", lambda e: e.activation(out=gs[:, c0:c0 + n], in_=pt[:, 0:n], func=AF.Silu), reads=[R_pt], writes=[R_gs]))
        proj_fm(W1, RW1, 128, blocks, lambda pt, R_pt, c0, n: S.op(
            "act", lambda e: e.activation(out=A[:, c0:c0 + n], in_=pt[:, 0:n], func=AF.Sigmoid), reads=[R_pt], writes=[R_A]))
        for (c0, ntok, ti) in tiles:
            pt, R_pt = proj_tm(W2, RW2, 0, c0, ntok)
            S.op("dve", lambda e, pt=pt, ntok=ntok, ti=ti: e.tensor_copy(out=vtm[0:ntok, ti, :], in_=pt[0:ntok, 0:128]),
                 reads=[R_pt], writes=[R_vtm])
        relw(RW1, RW2)
        S.op("dve", lambda e: e.tensor_scalar(out=A[:, 0:ncols], in0=A[:, 0:ncols], scalar1=omlb[:, h, j:j + 1],
                                              scalar2=lbv[:, h, j:j + 1], op0=ALU.mult, op1=ALU.add),
             reads=[R_A, R_omlb, R_lbv], writes=[R_A])
        S.op("dve", lambda e: e.tensor_scalar(out=kin[:, 0:ncols], in0=A[:, 0:ncols], scalar1=-1.0, scalar2=1.0,
                                              op0=ALU.mult, op1=ALU.add), reads=[R_A], writes=[R_kin])
        S.op("act", lambda e: e.activation(out=A[:, 0:ncols], in_=A[:, 0:ncols], func=AF.Ln), reads=[R_A], writes=[R_A])
        segs = [("fr", 0, SBF, 64)]
        if sb == 0:
            segs = [("meta", META0, NMETA, NMETA), ("fr", 0, SBF, 64), ("samp", SAMP0, NSAMP, NSAMP)]
        for (sn, col0, Ltot, Lc) in segs:
            o = BOFF[sn]
            nch = Ltot // Lc
            ci = CTI[sn]
            S.op("dve", lambda e, o=o: e.memset(Bx[:, o:o + 1], 0.0), writes=[R_B])
            S.op("dve", lambda e, o=o, col0=col0, Ltot=Ltot: e.tensor_tensor_scan(
                out=Bx[:, o + 1:o + 1 + Ltot], data0=ones_f[:, 0:1].to_broadcast([128, Ltot]), data1=A[:, col0:col0 + Ltot], initial=0.0,
                op0=ALU.mult, op1=ALU.add), reads=[R_onf, R_A, R_B], writes=[R_B])
            Bst = Bx[:, o:o + Ltot].rearrange("p (c l) -> p c l", l=Lc)[:, :, 0]
            Bin = Bx[:, o + 1:o + 1 + Ltot].rearrange("p (c l) -> p c l", l=Lc)
            Bmd = Bin[:, :, Lc // 2 - 1]
            Bls = Bin[:, :, Lc - 1]
            S.op("dve", lambda e, Bin=Bin, col0=col0, Ltot=Ltot, Lc=Lc, nch=nch: e.tensor_tensor(
                out=A[:, col0:col0 + Ltot].rearrange("p (c l) -> p c l", l=Lc), in0=Bin,
                in1=Bin[:, :, Lc // 2 - 1:Lc // 2].to_broadcast([128, nch, Lc]), op=ALU.subtract),
                reads=[R_B, R_A], writes=[R_A])
            S.op("dve", lambda e, Bmd=Bmd, Bst=Bst, ci=ci, nch=nch: e.tensor_tensor(
                out=ctab[:, 0, ci:ci + nch], in0=Bmd, in1=Bst, op=ALU.subtract), reads=[R_B], writes=[R_ctab])
            S.op("dve", lambda e, Bmd=Bmd, Bls=Bls, ci=ci, nch=nch: e.tensor_tensor(
                out=ctab[:, 1, ci:ci + nch], in0=Bls, in1=Bmd, op=ALU.subtract), reads=[R_B], writes=[R_ctab])
            S.op("dve", lambda e, Bst=Bst, Bls=Bls, ci=ci, nch=nch: e.tensor_tensor(
                out=ctab[:, 2, ci:ci + nch], in0=Bls, in1=Bst, op=ALU.subtract), reads=[R_B], writes=[R_ctab])
        nct = 18 if sb == 0 else 16
        S.op("act", lambda e: e.activation(out=ctab[:, :, 0:nct], in_=ctab[:, :, 0:nct], func=AF.Exp), reads=[R_ctab], writes=[R_ctab])
        S.op("act", lambda e: e.activation(out=C[:, 0:ncols], in_=A[:, 0:ncols], func=AF.Exp), reads=[R_A], writes=[R_C])
        S.op("dve", lambda e: e.tensor_tensor(out=qs[:, 0:ncols], in0=qs[:, 0:ncols], in1=C[:, 0:ncols], op=ALU.mult),
             reads=[R_qs, R_C], writes=[R_qs])
        S.op("act", lambda e: e.activation(out=C[:, 0:ncols], in_=A[:, 0:ncols], func=AF.Exp, scale=-1.0), reads=[R_A], writes=[R_C])
        S.op("dve", lambda e: e.tensor_tensor(out=kin[:, 0:ncols], in0=kin[:, 0:ncols], in1=C[:, 0:ncols], op=ALU.mult),
             reads=[R_kin, R_C], writes=[R_kin])
        if sb == 0:
            St, R_St = Ssm[h]
            S.dma("pool", lambda e: e.dma_start(out=St[:], in_=sh[j, h]), writes=[R_St])
        import os
        for (sn, col0, Ltot, Lc) in ([] if "hloop" in os.environ.get("KSKIP", "") else segs):
            St, R_St = Ssm[h] if sn == "samp" else Sst[j][h]
            nch = Ltot // Lc
            for c in range(nch):
                cs = col0 + c * Lc
                ci = CTI[sn] + c
                if sn == "fr":
                    ti, p0 = c // 2, 64 * (c % 2)
                    tcol0, ntile = (c // 2) * 128, 128
                else:
                    ti, p0 = (8 if sn == "meta" else 9), 0
                    tcol0, ntile = col0, Ltot
                if p0 == 0:
                    pt, R_pt = gu_r.next()
                    ptb = pt.bitcast(BF16)
                    ktm, R_ktm = ktm_r.next()
                    S.op("pe", lambda e, ptb=ptb, tcol0=tcol0, ntile=ntile: e.transpose(
                        ptb[0:ntile, 0:128], kin[:, tcol0:tcol0 + ntile], ident_b[:]), reads=[R_kin, R_idb], writes=[R_pt])
                    S.op("act", lambda e, ptb=ptb, ktm=ktm, ntile=ntile: e.activation(
                        out=ktm[0:ntile, :], in_=ptb[0:ntile, 0:128], func=AF.Copy), reads=[R_pt], writes=[R_ktm])
                pn_, R_pn_ = ya_r.next()
                mm_group(pn_[:, 0:128], lambda k, ktm=ktm: ktm[p0:p0 + Lc, :], lambda k: vtm[p0:p0 + Lc, ti, :], 1,
                         [R_ktm, R_vtm], [R_pn_])
                tmp, R_tmp = tmp_r.next()
                S.op("act", lambda e, pn_=pn_, tmp=tmp, ci=ci: e.activation(
                    out=tmp[:], in_=pn_[:, 0:128], func=AF.Copy, scale=ctab[:, 1, ci:ci + 1]),
                    reads=[R_pn_, R_ctab], writes=[R_tmp])
                ps, R_s = ya_r.next()
                mm_group(ps[p0:p0 + Lc, 0:Lc], lambda k: kin[:, cs:cs + Lc], lambda k: qs[:, cs:cs + Lc], 1,
                         [R_kin, R_qs], [R_s])
                pth, R_pth = pth_r.next()
                S.op("dve", lambda e, ps=ps, pth=pth, p0=p0, Lc=Lc: e.tensor_tensor(
                    out=pth[p0:p0 + Lc, 0:Lc], in0=ps[p0:p0 + Lc, 0:Lc], in1=mask_hg[p0:p0 + Lc, 0:Lc], op=ALU.mult),
                    reads=[R_s, R_cst], writes=[R_pth])
                Sb, R_Sb = Sb_r.next()
                S.op("dve", lambda e, Sb=Sb, St=St, ci=ci: e.tensor_scalar(
                    out=Sb[:], in0=St[:], scalar1=ctab[:, 0, ci:ci + 1], scalar2=None, op0=ALU.mult),
                    reads=[R_St, R_ctab], writes=[R_Sb])
                S.op("dve", lambda e, St=St, tmp=tmp, ci=ci: e.scalar_tensor_tensor(
                    out=St[:], in0=St[:], scalar=ctab[:, 2, ci:ci + 1], in1=tmp[:], op0=ALU.mult, op1=ALU.add),
                    reads=[R_St, R_ctab, R_tmp], writes=[R_St])
                po, R_po = yb_r.next()

                def fo(e, po=po, Sb=Sb, pth=pth, p0=p0, Lc=Lc, cs=cs, ti=ti):
                    e.matmul(po[:, 0:Lc], lhsT=Sb[:], rhs=qs[:, cs:cs + Lc], start=True, stop=False)
                    return e.matmul(po[:, 0:Lc], lhsT=vtm[p0:p0 + Lc, ti, :], rhs=pth[p0:p0 + Lc, 0:Lc], start=False, stop=True)
                S.op("pe", fo, reads=[R_Sb, R_qs, R_vtm, R_pth], writes=[R_po])
                S.op("act", lambda e, po=po, cs=cs, Lc=Lc: e.activation(out=oh[:, cs:cs + Lc], in_=po[:, 0:Lc], func=AF.Copy),
                     reads=[R_po], writes=[R_oh])
            if sn == "samp":
                S.dma("pool", lambda e, St=St: e.dma_start(out=ohs[j, h], in_=St[:]), reads=[R_St], writes=[OUT])
            if sn == "fr" and last:
                S.dma("pool", lambda e, St=St: e.dma_start(out=ohp[j, h], in_=St[:]), reads=[R_St], writes=[OUT])
        rstd_of(lambda c0, n: oh[:, c0:c0 + n].unsqueeze(1), [R_oh], 1, ncols, 128.0, 9)
        S.op("dve", lambda e: e.tensor_tensor(out=oh[:, 0:ncols], in0=oh[:, 0:ncols], in1=cbs[9][:, 0:ncols], op=ALU.mult),
             reads=[R_oh, R_cb[9]], writes=[R_oh])
        S.op("dve", lambda e: e.scalar_tensor_tensor(
            out=zT[:, 4 + h, 0:ncols], in0=oh[:, 0:ncols], scalar=hngT[:, h, j:j + 1], in1=gs[:, 0:ncols],
            op0=ALU.mult, op1=ALU.mult), reads=[R_oh, R_hng, R_gs], writes=[R_z[4 + h]])

    def odd_mixer(j, sb, ncols, on_block_final):
        last = (sb == n_sb - 1)
        blocks = blocks_of(ncols)
        bgb, R_bg = bfv(0), R_cb[0]
        cgf, R_cg = cbs[1], R_cb[1]
        uF, R_uF = cbs[2], R_cb[2]
        yv, R_yv = cbs[3], R_cb[3]
        uh, R_uh = uhalo[j]
        W2, RW2 = None, None
        for jc in range(KC):
            W1, RW1 = getw(("km", "wio", j, ((128 * jc, 128), (D + 128 * jc, 128))))
            if jc % 2 == 0:
                W2, RW2 = getw(("km", "wio", j, ((2 * D + 128 * jc, 256),)))
            proj_fm(W1, RW1, 0, blocks, lambda pt, R_pt, c0, n: S.op(
                "act", lambda e: e.activation(out=bgb[:, c0:c0 + n], in_=pt[:, 0:n], func=AF.Copy), reads=[R_pt], writes=[R_bg]))
            proj_fm(W1, RW1, 128, blocks, lambda pt, R_pt, c0, n: S.op(
                "act", lambda e: e.activation(out=cgf[:, c0:c0 + n], in_=pt[:, 0:n], func=AF.Copy), reads=[R_pt], writes=[R_cg]))

            def evac_u(pt, R_pt, c0, n):
                if c0 < SBF:
                    S.op("dve", lambda e: e.tensor_tensor(out=uF[:, 2 + c0:2 + c0 + n], in0=cgf[:, c0:c0 + n], in1=pt[:, 0:n],
                                                          op=ALU.mult), reads=[R_cg, R_pt], writes=[R_uF])
                else:
                    S.op("dve", lambda e: e.tensor_tensor(out=uaux[:, 2:2 + NMETA], in0=cgf[:, META0:META0 + NMETA],
                                                          in1=pt[:, 0:NMETA], op=ALU.mult), reads=[R_cg, R_pt], writes=[R_uaux])
                    S.op("dve", lambda e: e.tensor_tensor(out=uaux[:, 20:20 + NSAMP], in0=cgf[:, SAMP0:SAMP0 + NSAMP],
                                                          in1=pt[:, NMETA:NMETA + NSAMP], op=ALU.mult),
                         reads=[R_cg, R_pt], writes=[R_uaux])
            proj_fm(W2, RW2, 128 * (jc % 2), blocks, evac_u)
            relw(RW1)
            if jc % 2 == 1:
                relw(RW2)
            if sb == 0:
                S.op("dve", lambda e: e.memset(uaux[:, 0:2], 0.0), writes=[R_uaux])
                S.op("dve", lambda e, jc=jc: e.tensor_copy(out=uaux[:, 18:20], in_=ccT[:, jc, 2 * j:2 * j + 2]),
                     reads=[R_ccT], writes=[R_uaux])
                S.op("dve", lambda e: e.tensor_copy(out=uF[:, 0:2], in_=uaux[:, NMETA:NMETA + 2]), reads=[R_uaux], writes=[R_uF])
            else:
                S.op("dve", lambda e, jc=jc: e.tensor_copy(out=uF[:, 0:2], in_=uh[:, jc, :]), reads=[R_uh], writes=[R_uF])
            segs = [(uF, R_uF, 0, 0, SBF)]
            if sb == 0:
                segs += [(uaux, R_uaux, 0, META0, NMETA), (uaux, R_uaux, 18, SAMP0, NSAMP)]
            for (ub, R_ub, uo, yc0, N) in segs:
                w0 = cwT[:, jc, 3 * j + 0:3 * j + 1]
                w1 = cwT[:, jc, 3 * j + 1:3 * j + 2]
                w2 = cwT[:, jc, 3 * j + 2:3 * j + 3]
                S.op("dve", lambda e, ub=ub, uo=uo, yc0=yc0, N=N, w0=w0: e.tensor_scalar(
                    out=yv[:, yc0:yc0 + N], in0=ub[:, uo:uo + N], scalar1=w0, scalar2=None, op0=ALU.mult),
                    reads=[R_ub, R_cw], writes=[R_yv])
                S.op("dve", lambda e, ub=ub, uo=uo, yc0=yc0, N=N, w1=w1: e.scalar_tensor_tensor(
                    out=yv[:, yc0:yc0 + N], in0=ub[:, uo + 1:uo + 1 + N], scalar=w1, in1=yv[:, yc0:yc0 + N],
                    op0=ALU.mult, op1=ALU.add), reads=[R_ub, R_cw, R_yv], writes=[R_yv])
                S.op("dve", lambda e, ub=ub, uo=uo, yc0=yc0, N=N, w2=w2: e.scalar_tensor_tensor(
                    out=yv[:, yc0:yc0 + N], in0=ub[:, uo + 2:uo + 2 + N], scalar=w2, in1=yv[:, yc0:yc0 + N],
                    op0=ALU.mult, op1=ALU.add), reads=[R_ub, R_cw, R_yv], writes=[R_yv])
            S.op("dve", lambda e, jc=jc: e.tensor_tensor(out=zT[:, jc, 0:ncols], in0=bgb[:, 0:ncols], in1=yv[:, 0:ncols],
                                                         op=ALU.mult), reads=[R_bg, R_yv], writes=[R_z[jc]])
            S.op("dve", lambda e, jc=jc: e.tensor_copy(out=uh[:, jc, :], in_=uF[:, SBF:SBF + 2]), reads=[R_uF], writes=[R_uh])
            if sb == 0:
                S.op("dve", lambda e, jc=jc: e.tensor_copy(out=uSh[:, jc, :], in_=uaux[:, 18 + NSAMP:20 + NSAMP]),
                     reads=[R_uaux], writes=[R_uSh])
        if last:
            fm_to_rows(lambda c: uh[:, c, :], 2, KC, ocp[2 * j:2 * j + 2, :], [R_uh])
        if sb == 0:
            fm_to_rows(lambda c: uSh[:, c, :], 2, KC, ocs[2 * j:2 * j + 2, :], [R_uSh])
        out_proj([("km", "woo", j, ((256 * mp, 256),)) for mp in range(4)], ncols, on_block_final)

    MIXERS = {"even": even_mixer, "odd": odd_mixer}

    def main(max_stage=None):
        import os
        skip = os.environ.get("KSKIP", "")
        if "setup" in skip:
            S.dma("sp", lambda e: e.dma_start(out=cst[:], in_=consts[:, :]), writes=[R_cst])
        else:
            setup()
        if "io" in skip:
            return
        for sb in range(n_sb):
            ncols = NCM if sb == 0 else SBF
            load_x(sb)
            stages = [(l, s) for l in range(n_layers) for s in range(3)]
            if max_stage is not None:
                stages = stages[:max_stage]
            gidx = [l * 6 + 2 * s for (l, s) in stages]
            if stages:
                prenorm(gidx[0], ncols)
            for si, (l, s) in enumerate(stages):
                nxt = gidx[si + 1] if si + 1 < len(stages) else None

                import os
                late = ("late%d" % s) in os.environ.get("KSKIP", "")

                def epilogue(c0, n, g=gidx[si], nxt=nxt, late=late):
                    for p in postnorm_pieces(g + 1, c0, n):
                        epi_q.append((c0 // 256, p))
                    if nxt is not None and not late:
                        for p in prenorm_pieces(nxt, c0, n):
                            epi_q.append((c0 // 256, p))
                if s == 0:
                    ffn(2 * l, ncols, epilogue)
                elif s == 1:
                    MIXERS["even" if l % 2 == 0 else "odd"](l // 2, sb, ncols, epilogue)
                else:
                    ffn(2 * l + 1, ncols, epilogue)
                if late and nxt is not None:
                    flush_epi()
                    prenorm(nxt, ncols)
            flush_epi()
            store_y(sb)

    main(dbg_at)
    if plan is None:
        S.stack.close()
        return None, ws["rec"]
    S.finish([OUT])
    return S, ws["rec"]


IN_NAMES = ["xp", "xs", "ck", "cv", "sh", "ccv", "meta", "gains", "wg", "wu", "wd", "wie", "woe", "sinks", "lbl", "hng",
            "wio", "cw", "woo", "consts"]


def make_nc(**kw):
    nc0 = bass.Bass("TRN2", target_bir_lowering=False)
    _, plan = build_program(nc0, plan=None, **kw)
    nc = bass.Bass("TRN2", target_bir_lowering=False)
    S, rec = build_program(nc, plan=plan, **kw)
    assert rec == plan
    return nc, S


def core_inputs(inp, b, sbi, shared):
    m = dict(shared)
    m["xp"] = np.ascontiguousarray(inp["x_prompt"][b])
    m["xs"] = np.ascontiguousarray(inp["x_sample"][sbi])
    m["ck"] = np.ascontiguousarray(inp["cache_swa_k"][:, sbi]).reshape(2, 128, 128)
    m["cv"] = np.ascontiguousarray(inp["cache_swa_v"][:, sbi]).reshape(2, 128, 128)
    m["sh"] = np.ascontiguousarray(inp["state_hgrn"][:, sbi])
    m["ccv"] = np.ascontiguousarray(inp["cache_conv"][:, sbi]).reshape(4, D)
    return m


def shared_inputs(inp):
    f = lambda a: np.ascontiguousarray(np.asarray(a, dtype=np.float32))
    return {
        "meta": f(inp["meta_tokens"]), "gains": f(inp["norm_gains"]).reshape(24, D),
        "wg": f(inp["w_ffn_gate"]).reshape(8, D, DFF), "wu": f(inp["w_ffn_up"]).reshape(8, D, DFF),
        "wd": f(inp["w_ffn_down"]).reshape(8, DFF, D), "wie": f(inp["w_in_even"]), "woe": f(inp["w_out_even"]),
        "sinks": f(inp["attn_sinks"]).reshape(1, 16), "lbl": f(inp["hgrn_lb_logits"]),
        "hng": f(inp["hgrn_norm_gain"]).reshape(2, 512), "wio": f(inp["w_in_odd"]),
        "cw": f(inp["conv_w"]).reshape(6, D), "woo": f(inp["w_out_odd"]), "consts": host_consts(),
    }


_CACHE = {}


def kernel(**inputs):
    if "nc" not in _CACHE:
        _CACHE["nc"] = make_nc()[0]
    nc = _CACHE["nc"]
    inp = {k: np.asarray(v) for k, v in inputs.items()}
    sh = shared_inputs(inp)
    in_maps = [core_inputs(inp, c % 4, c, sh) for c in range(8)]
    res = run_bass_kernel_spmd(nc, in_maps, core_ids=list(range(8)))
    r = res.results
    f32 = np.float32
    y_prompt = np.stack([r[b]["yp"] for b in range(4)]).astype(f32)
    y_sample = np.stack([r[c]["ys"] for c in range(8)]).astype(f32)

    def gather(name, cores, shape):
        a = np.stack([r[c][name] for c in cores], axis=1)
        return np.ascontiguousarray(a.reshape(shape)).astype(f32)
    P, Q = range(4), range(8)
    swa_k_p = gather("okp", P, (2, 4, 128, 2, 64))
    swa_v_p = gather("ovp", P, (2, 4, 128, 2, 64))
    hgrn_p = gather("ohp", P, (2, 4, 4, 128, 128))
    conv_p = np.stack([r[c]["ocp"].reshape(2, 2, D) for c in P], axis=1).astype(f32)
    swa_k_s = gather("oks", Q, (2, 8, 128, 2, 64))
    swa_v_s = gather("ovs", Q, (2, 8, 128, 2, 64))
    hgrn_s = gather("ohs", Q, (2, 8, 4, 128, 128))
    conv_s = np.stack([r[c]["ocs"].reshape(2, 2, D) for c in Q], axis=1).astype(f32)
    return (y_prompt, y_sample, swa_k_p, swa_v_p, hgrn_p, conv_p, swa_k_s, swa_v_s, hgrn_s, conv_s)
```

```python
import numpy as np
from contextlib import ExitStack
import concourse.bass as bass
import concourse.mybir as mybir
from concourse.bass_utils import run_bass_kernel_spmd

F32 = mybir.dt.float32
BF16 = mybir.dt.bfloat16
AF = mybir.ActivationFunctionType
ALU = mybir.AluOpType
AX = mybir.AxisListType


class Reg:
    __slots__ = ("name", "last_w", "readers", "multi", "writers", "excl")

    def __init__(self, name, multi=False, excl=False):
        self.name = name
        self.last_w = None
        self.readers = []
        self.multi = multi
        self.writers = []
        self.excl = excl


class _Op:
    __slots__ = ("eng", "fn", "deps", "is_dma", "needs_inc", "sem", "val", "lane_prev", "idx")


EPOCH = 30000
NLANES = 32


class Sched:
    def __init__(self, nc):
        self.nc = nc
        self.stack = ExitStack()
        self.ops = []
        self.nsem = 0

    def sb(self, name, shape, dtype):
        return self.stack.enter_context(self.nc.sbuf_tensor(name, shape, dtype))

    def ps(self, name, shape, dtype):
        return self.stack.enter_context(self.nc.psum_tensor(name, shape, dtype))

    def _sem(self, name):
        self.nsem += 1
        return self.stack.enter_context(self.nc.semaphore(name))

    @staticmethod
    def _flat(regs):
        out = []
        for r in regs:
            if isinstance(r, (list, tuple)):
                out.extend(Sched._flat(r))
            else:
                out.append(r)
        return out

    def _add(self, eng, fn, reads, writes, is_dma):
        reads = self._flat(reads)
        writes = self._flat(writes)
        op = _Op()
        op.eng = eng
        op.fn = fn
        op.is_dma = is_dma
        op.needs_inc = False
        op.sem = None
        op.val = 0
        op.lane_prev = None
        op.idx = len(self.ops)
        deps = {}
        for r in reads:
            if r.multi:
                for w in r.writers:
                    deps[w.idx] = w
            else:
                if r.last_w is not None:
                    deps[r.last_w.idx] = r.last_w
                if r.excl:
                    for rd in r.readers:
                        if rd.eng != eng:
                            deps[rd.idx] = rd
        for w in writes:
            if w.multi:
                continue
            if w.last_w is not None:
                deps[w.last_w.idx] = w.last_w
            for rd in w.readers:
                deps[rd.idx] = rd
        for r in reads:
            if not r.multi:
                r.readers.append(op)
        for w in writes:
            if w.multi:
                w.writers.append(op)
            else:
                w.last_w = op
                w.readers = []
        final = []
        for i in sorted(deps):
            d = deps[i]
            if d is op:
                continue
            if (not is_dma) and (not d.is_dma) and d.eng == "pe" and eng == "pe":
                continue
            d.needs_inc = True
            final.append(d)
        op.deps = final
        self.ops.append(op)
        return op

    def op(self, eng, fn, reads=(), writes=()):
        return self._add(eng, fn, reads, writes, False)

    def dma(self, eng, fn, reads=(), writes=()):
        return self._add(eng, fn, reads, writes, True)

    def finish(self, outputs):
        self._add("sp", None, outputs, (), False)
        cnt = {}
        engsems = {}
        lanes = {}
        lane_cnt = {}
        lane_last = {}
        ndma_e = {}
        ndma = 0
        for op in self.ops:
            if op.is_dma:
                nd = ndma_e.get(op.eng, 0)
                ndma_e[op.eng] = nd + 1
                ndma += 1
                ln = (op.eng, nd % NLANES)
                n = lane_cnt.get(ln, 0)
                ep = n // (EPOCH // 16)
                lst = lanes.setdefault(ln, [])
                while len(lst) <= ep:
                    lst.append(self._sem("ln%s%d_%d" % (op.eng, ln[1], len(lst))))
                op.sem = lst[ep]
                op.val = (n - ep * (EPOCH // 16) + 1) * 16
                op.lane_prev = lane_last.get(ln)
                lane_last[ln] = op
                lane_cnt[ln] = n + 1
            elif op.needs_inc:
                n = cnt.get(op.eng, 0)
                ep = n // EPOCH
                lst = engsems.setdefault(op.eng, [])
                while len(lst) <= ep:
                    lst.append(self._sem("%s_%d" % (op.eng, len(lst))))
                op.sem = lst[ep]
                op.val = n - ep * EPOCH + 1
                cnt[op.eng] = n + 1
        by_eng = {}
        for op in self.ops:
            by_eng.setdefault(op.eng, []).append(op)
        self.stats = {k: len(v) for k, v in by_eng.items()}
        self.stats["ndma"] = ndma
        self.stats["nsem"] = self.nsem

        def emit(name, e):
            known = {}
            for op in by_eng.get(name, []):
                ws = list(op.deps)
                if op.lane_prev is not None:
                    ws.append(op.lane_prev)
                for d in ws:
                    k = id(d.sem)
                    if known.get(k, 0) < d.val:
                        e.wait_ge(d.sem, d.val)
                        known[k] = d.val
                if op.fn is not None:
                    ins = op.fn(e)
                    if op.is_dma:
                        ins.then_inc(op.sem, 16)
                    elif op.needs_inc:
                        ins.then_inc(op.sem, 1)

        with self.nc.Block() as block:
            @block.tensor
            def _(e):
                emit("pe", e)

            @block.scalar
            def _(e):
                emit("act", e)

            @block.vector
            def _(e):
                emit("dve", e)

            @block.gpsimd
            def _(e):
                emit("pool", e)

            @block.sync
            def _(e):
                emit("sp", e)
        self.stack.close()


D = 1024
KC = 8
DFF = 2816
SEQ = 4096
NMETA = 16
NSAMP = 32
SBF = 1024
NSB = SEQ // SBF
NCM = SBF + NMETA + NSAMP
CB = 1080
META0 = SBF
SAMP0 = SBF + NMETA
EPS = 1e-6
NEG = -30000.0
NCONST = 128 + 256 + 256 + 64


def host_consts():
    c = np.zeros((128, NCONST), np.float32)
    c[:, 0:128] = np.eye(128, dtype=np.float32)
    q = np.arange(128)[:, None]
    k = np.arange(256)[None, :]
    gen = np.where(q < 64, k < 192, k >= 64)
    c[:, 128:384] = np.where(gen, 0.0, NEG)
    first = np.where(k < 16, True, np.where(k < 128, False, np.where(q < 64, k < 192, True)))
    c[:, 384:640] = np.where(first, 0.0, NEG)
    s = (np.arange(128) % 64)[:, None]
    t = np.arange(64)[None, :]
    c[:, 640:704] = (s <= t).astype(np.float32)
    return c


class Ring:
    def __init__(self, items):
        self.items = items
        self.i = 0

    def next(self):
        it = self.items[self.i % len(self.items)]
        self.i += 1
        return it


LA_D, LA_C = 3, 3


def build_program(nc, plan=None, n_sb=NSB, n_layers=4, dbg_at=None):
    S = Sched(nc)
    T = {}

    def din(name, shape):
        T[name] = nc.dram_tensor(name, list(shape), F32, kind="ExternalInput").ap()
        return T[name]

    def dout(name, shape):
        T[name] = nc.dram_tensor(name, list(shape), F32, kind="ExternalOutput").ap()
        return T[name]

    xp = din("xp", [SEQ, D]); xs = din("xs", [NSAMP, D])
    ck = din("ck", [2, 128, 128]); cv = din("cv", [2, 128, 128])
    sh = din("sh", [2, 4, 128, 128]); ccv = din("ccv", [4, D])
    meta = din("meta", [NMETA, D]); gains = din("gains", [24, D])
    din("wg", [8, D, DFF]); din("wu", [8, D, DFF]); din("wd", [8, DFF, D])
    din("wie", [2, D, DFF]); din("woe", [2, D, D])
    sinks = din("sinks", [1, 16]); lbl = din("lbl", [2, 512]); hng = din("hng", [2, 512])
    din("wio", [2, D, 3 * D]); cw = din("cw", [6, D]); din("woo", [2, D, D])
    consts = din("consts", [128, NCONST])
    yp = dout("yp", [SEQ, D]); ys = dout("ys", [NSAMP, D])
    okp = dout("okp", [2, 128, 128]); ovp = dout("ovp", [2, 128, 128])
    ohp = dout("ohp", [2, 4, 128, 128]); ocp = dout("ocp", [4, D])
    oks = dout("oks", [2, 128, 128]); ovs = dout("ovs", [2, 128, 128])
    ohs = dout("ohs", [2, 4, 128, 128]); ocs = dout("ocs", [4, D])
    OUT = Reg("outputs", multi=True)
    dbgx = dout("dbgx", [128, KC * NCM]) if dbg_at is not None else None

    def sbt(name, shape, dt=F32):
        return S.sb(name, shape, dt), Reg(name)

    NBLK = 5
    xT = S.sb("xT", [128, KC, NCM], F32)
    hT = S.sb("hT", [128, KC, NCM], BF16)
    R_x = [Reg("x%d" % b) for b in range(NBLK)]
    R_h = [Reg("h%d" % b) for b in range(NBLK)]
    zT = S.sb("zT", [128, KC, NCM], BF16)
    R_z = [Reg("z%d" % c) for c in range(KC)]
    NCB = KC + 2
    yacc = S.sb("yacc", [128, KC, CB], F32)
    cbs = [yacc[:, i, :] for i in range(KC)] + [S.sb("cb%d" % i, [128, CB], F32)[:] for i in range(KC, NCB)]
    sqb, R_sqb = sbt("sqb", [128, KC, 256], BF16)
    R_cb = [[Reg("cb%d_%d" % (i, b)) for b in range(NBLK)] for i in range(NCB)]
    NSLOT, NSTG = 6, 2
    wsl = [S.sb("wsl%d" % i, [128, 4096], BF16) for i in range(NSLOT)]
    R_wsl = [Reg("wsl%d" % i) for i in range(NSLOT)]
    wst = [S.sb("wst%d" % i, [128, 1024], F32) for i in range(NSTG)]
    R_wst = [Reg("wst%d" % i) for i in range(NSTG)]
    cst, R_cst = sbt("cst", [128, NCONST])
    ident_f = cst[:, 0:128]
    mask_gen = cst[:, 128:384]
    mask_first = cst[:, 384:640]
    mask_hg = cst[:, 640:704]
    ident_b, R_idb = sbt("ident_b", [128, 128], BF16)
    ones_b, R_onb = sbt("ones_b", [128, 128], BF16)
    ones_f, R_onf = sbt("ones_f", [128, 1], F32)
    gainT, R_gain = sbt("gainT", [128, KC, 24])
    gpost, R_gpost = sbt("gpost", [128, KC, 24])
    cwT, R_cw = sbt("cwT", [128, KC, 6])
    sinkc, R_sink = sbt("sinkc", [128, 16])
    negsink = S.sb("negsink", [128, 16], F32)
    mskb, R_mskb = sbt("mskb", [128, 512], BF16)
    lbT, R_lbT = sbt("lbT", [128, 4, 2])
    lbv, R_lbv = sbt("lbv", [128, 4, 2])
    omlb, R_omlb = sbt("omlb", [128, 4, 2])
    hngT, R_hng = sbt("hngT", [128, 4, 2])
    kprev = [sbt("kprev%d" % j, [128, 128], BF16) for j in range(2)]
    vprev = [sbt("vprev%d" % j, [128, 128], BF16) for j in range(2)]
    vtm, R_vtm = sbt("vtm", [128, 10, 128], BF16)
    kvf, R_kvf = sbt("kvf", [128, 2, 128], F32)
    ckT, R_ckT = sbt("ckT", [128, 128], BF16)
    cvb, R_cvb = sbt("cvb", [128, 128], BF16)
    col_r = Ring([sbt("col%d" % i, [128, 20]) for i in range(2)])
    Sst = [[sbt("S%d_%d" % (j, h), [128, 128]) for h in range(4)] for j in range(2)]
    Ssm = [sbt("Ss%d" % h, [128, 128]) for h in range(4)]
    Sb_r = Ring([sbt("Sb%d" % i, [128, 128], BF16) for i in range(4)])
    tmp_r = Ring([sbt("tmpS%d" % i, [128, 128]) for i in range(3)])
    ktm_r = Ring([sbt("ktm%d" % i, [128, 128], BF16) for i in range(2)])
    pth_r = Ring([sbt("pth%d" % i, [128, 64], BF16) for i in range(4)])
    ctab, R_ctab = sbt("ctab", [128, 3, 20])
    uhalo = [sbt("uhalo%d" % j, [128, KC, 2]) for j in range(2)]
    uSh, R_uSh = sbt("uSh", [128, KC, 2])
    ccT, R_ccT = sbt("ccT", [128, KC, 4])
    uaux2 = [sbt("uaux%d" % i, [128, 64]) for i in range(2)]
    sg_r = Ring([sbt("sg%d" % i, [128, 256]) for i in range(3)])
    a_r = Ring([sbt("a%d" % i, [128, 256], BF16) for i in range(8)])

    pY = S.ps("pY", [128, 2048], F32)
    pX = S.ps("pX", [128, 2048], F32)
    bank = [pY[:, i * 512:(i + 1) * 512] for i in range(4)] + [pX[:, i * 512:(i + 1) * 512] for i in range(4)]
    R_bank = [Reg("bank%d" % b, excl=True) for b in range(8)]
    gu_r = Ring([(bank[b], R_bank[b]) for b in range(4, 8)])
    misc_r = gu_r
    ya_r = Ring([(bank[b], R_bank[b]) for b in range(0, 2)])
    yb_r = Ring([(bank[b], R_bank[b]) for b in range(2, 4)])
    stg_r = Ring(list(zip(wst, R_wst)))
    slot_r = Ring(list(zip(wsl, R_wsl)))

    def pieces_of(spec):
        kind = spec[0]
        if kind == "km":
            _, tn, idx, cols = spec
            wm = T[tn][idx].rearrange("(k p) c -> p k c", p=128)
            out, off = [], 0
            for c0, w in cols:
                out.append((lambda st3, off=off, w=w: st3[:, :, off:off + w], wm[:, :, c0:c0 + w]))
                off += w
            return out, 8, off
        if kind == "rows":
            _, tn, idx, r0, ncc = spec
            src = T[tn][idx][r0:r0 + 128 * ncc, :].rearrange("(c p) m -> p c m", p=128)
            return [(lambda st3: st3[:, :, :], src)], ncc, 1024
        if kind == "woe":
            _, j, m0 = spec
            wm = T["woe"][j]
            out = []
            for q in range(4):
                out.append((lambda st3, q=q: st3[0:64, q, :], wm[64 * q:64 * q + 64, m0:m0 + 256]))
                out.append((lambda st3, q=q: st3[64:128, q, :], wm[256 + 64 * q:256 + 64 * q + 64, m0:m0 + 256]))
            out.append((lambda st3: st3[:, 4:8, :],
                        wm[512:1024, :].rearrange("(k p) c -> p k c", p=128)[:, :, m0:m0 + 256]))
            return out, 8, 256
        raise ValueError(kind)

    ws = {"rec": [], "dma": 0, "cast": 0, "h": [], "st": [], "pend": {}}

    def take_stage():
        idx = stg_r.i % NSTG
        stg, R_stg = stg_r.next()
        return idx, stg, R_stg

    ws["busy"] = [False] * NSLOT
    ws["nslot"] = 0

    def _w_dma(spec):
        si = ws["nslot"] % NSLOT
        if ws["busy"][si]:
            return False
        ws["nslot"] += 1
        ws["busy"][si] = True
        pieces, k, w = pieces_of(spec)
        slot, R_slot = wsl[si], R_wsl[si]
        sl3 = slot[:, 0:k * w].rearrange("p (k w) -> p k w", w=w)
        for dst_fn, src in pieces:
            S.dma("pool", lambda e, d=dst_fn(sl3), s=src: e.dma_start(out=d, in_=s), writes=[R_slot])
        ws["h"].append((sl3, R_slot))
        ws["dma"] += 1
        return True

    def relw(*regs):
        for r in regs:
            ws["busy"][R_wsl.index(r)] = False

    def getw(spec):
        i = len(ws["rec"])
        ws["rec"].append(spec)
        if plan is None:
            assert _w_dma(spec)
            return ws["h"][i]
        assert plan[i] == spec, (i, plan[i], spec)
        while ws["dma"] < min(len(plan), i + 1 + LA_D):
            if not _w_dma(plan[ws["dma"]]):
                break
        assert ws["dma"] > i, "weight slot ring exhausted (missing relw?)"
        return ws["h"][i]

    def rows_to_fm(src_rows, R, nch, dst_fn, wregs):
        _, stg, R_stg = take_stage()
        S.dma("pool", lambda e: e.dma_start(out=stg[0:R, 0:nch * 128], in_=src_rows), writes=[R_stg])
        g = max(1, min(256 // R, nch))
        for c0 in range(0, nch, g):
            n = min(g, nch - c0)
            pt, R_pt = misc_r.next()

            def f(e, c0=c0, n=n, pt=pt):
                ins = None
                for c in range(n):
                    ins = e.transpose(pt[:, c * R:(c + 1) * R],
                                      stg[0:R, (c0 + c) * 128:(c0 + c + 1) * 128], ident_f[0:R, 0:R])
                return ins
            S.op("pe", f, reads=[R_stg, R_cst], writes=[R_pt])
            S.op("dve", lambda e, c0=c0, n=n, pt=pt: e.tensor_copy(
                out=dst_fn(c0, n), in_=pt[:, 0:n * R].rearrange("p (c r) -> p c r", r=R)),
                reads=[R_pt], writes=wregs)

    def fm_to_rows(src_fn, R, nch, dst_rows, rregs):
        _, stg, R_stg = take_stage()
        for c0 in range(0, nch, 2):
            n = min(2, nch - c0)
            pt, R_pt = misc_r.next()

            def f(e, c0=c0, n=n, pt=pt):
                ins = None
                for c in range(n):
                    ins = e.transpose(pt[0:R, c * 128:(c + 1) * 128], src_fn(c0 + c), ident_f)
                return ins
            S.op("pe", f, reads=rregs + [R_cst], writes=[R_pt])
            S.op("dve", lambda e, c0=c0, n=n, pt=pt: e.tensor_copy(
                out=stg[0:R, c0 * 128:(c0 + n) * 128], in_=pt[0:R, 0:n * 128]),
                reads=[R_pt], writes=[R_stg])
        S.dma("pool", lambda e: e.dma_start(out=dst_rows, in_=stg[0:R, 0:nch * 128]), reads=[R_stg], writes=[OUT])

    def bfv(i):
        return cbs[i].bitcast(BF16)

    sm_r = Ring([(cbs[5][:, 0:1024], R_cb[5]), (cbs[6][:, 0:1024], R_cb[6])])
    pn_r = Ring([(bfv(7)[:, 0:1024], R_cb[7]), (bfv(8)[:, 0:1024], R_cb[8])])
    pT_r = Ring([(bfv(7)[:, 1024:2048].rearrange("p (g q) -> p g q", q=128), R_cb[7]),
                 (bfv(8)[:, 1024:2048].rearrange("p (g q) -> p g q", q=128), R_cb[8])])

    def blocks_of(ncols):
        bl = [(c, 256) for c in range(0, SBF, 256)]
        if ncols > SBF:
            bl.append((SBF, ncols - SBF))
        return bl

    def mm_group(out, lhs_fn, rhs_fn, nk, reads, writes):
        ops = [(lhs_fn(k), rhs_fn(k)) for k in range(nk)]

        def f(e):
            ins = None
            for k, (l, r) in enumerate(ops):
                ins = e.matmul(out, lhsT=l, rhs=r, start=(k == 0), stop=(k == nk - 1))
            return ins
        S.op("pe", f, reads=reads, writes=writes)

    def setup():
        S.dma("sp", lambda e: e.dma_start(out=cst[:], in_=consts[:, :]), writes=[R_cst])
        S.op("dve", lambda e: e.tensor_copy(out=ident_b[:], in_=ident_f), reads=[R_cst], writes=[R_idb])
        S.op("dve", lambda e: e.memset(ones_b[:], 1.0), writes=[R_onb])
        S.op("dve", lambda e: e.memset(ones_f[:], 1.0), writes=[R_onf])
        rows_to_fm(gains[:, :], 24, KC, lambda c0, n: gainT[:, c0:c0 + n, :], [R_gain])
        S.op("dve", lambda e: e.tensor_scalar(out=gpost[:], in0=gainT[:], scalar1=0.5, scalar2=None, op0=ALU.mult),
             reads=[R_gain], writes=[R_gpost])
        g4 = gainT[:].rearrange("p c (l i) -> p c l i", i=6)
        gp4 = gpost[:].rearrange("p c (l i) -> p c l i", i=6)
        S.op("dve", lambda e: e.tensor_copy(out=gp4[:, :, :, 3], in_=g4[:, :, :, 3]), reads=[R_gain], writes=[R_gpost])
        rows_to_fm(cw[:, :], 6, KC, lambda c0, n: cwT[:, c0:c0 + n, :], [R_cw])
        rows_to_fm(ccv[:, :], 4, KC, lambda c0, n: ccT[:, c0:c0 + n, :], [R_ccT])
        rows_to_fm(lbl[:, :], 2, 4, lambda c0, n: lbT[:, c0:c0 + n, :], [R_lbT])
        rows_to_fm(hng[:, :], 2, 4, lambda c0, n: hngT[:, c0:c0 + n, :], [R_hng])
        S.dma("pool", lambda e: e.dma_start(out=sinkc[:], in_=sinks[0, :].partition_broadcast(128)), writes=[R_sink])
        S.op("dve", lambda e: e.tensor_scalar(out=negsink[:], in0=sinkc[:], scalar1=-1.0, scalar2=None, op0=ALU.mult),
             reads=[R_sink], writes=[R_sink])
        S.op("dve", lambda e: e.tensor_scalar(out=mskb[:], in0=cst[:, 128:640], scalar1=1.0 / (64 ** -0.5), scalar2=None, op0=ALU.mult),
             reads=[R_cst], writes=[R_mskb])
        S.op("dve", lambda e: e.memset(lbv[:], 0.0), writes=[R_lbv])
        S.op("dve", lambda e: e.tensor_tensor(out=lbv[:, :, 1], in0=lbT[:, :, 1], in1=lbT[:, :, 0], op=ALU.subtract),
             reads=[R_lbT], writes=[R_lbv])
        S.op("act", lambda e: e.activation(out=lbv[:, :, 1], in_=lbv[:, :, 1], func=AF.Sigmoid), reads=[R_lbv], writes=[R_lbv])
        S.op("dve", lambda e: e.tensor_scalar(out=omlb[:], in0=lbv[:], scalar1=-1.0, scalar2=1.0, op0=ALU.mult, op1=ALU.add),
             reads=[R_lbv], writes=[R_omlb])
        for j in range(2):
            S.op("dve", lambda e, j=j: e.memset(kprev[j][0][:], 0.0), writes=[kprev[j][1]])
            S.op("dve", lambda e, j=j: e.memset(vprev[j][0][:], 0.0), writes=[vprev[j][1]])
            for h in range(4):
                S.op("dve", lambda e, j=j, h=h: e.memset(Sst[j][h][0][:], 0.0), writes=[Sst[j][h][1]])

    def load_x(sb):
        for t in range(SBF // 128):
            r0 = sb * SBF + t * 128
            rows_to_fm(xp[r0:r0 + 128, :], 128, KC,
                       lambda c0, n, t=t: xT[:, c0:c0 + n, t * 128:(t + 1) * 128], [R_x])
        if sb == 0:
            rows_to_fm(meta[:, :], NMETA, KC, lambda c0, n: xT[:, c0:c0 + n, META0:META0 + NMETA], [R_x])
            rows_to_fm(xs[:, :], NSAMP, KC, lambda c0, n: xT[:, c0:c0 + n, SAMP0:SAMP0 + NSAMP], [R_x])

    def store_y(sb):
        for t in range(SBF // 128):
            r0 = sb * SBF + t * 128
            fm_to_rows(lambda c, t=t: xT[:, c, t * 128:(t + 1) * 128], 128, KC, yp[r0:r0 + 128, :], [R_x])
        if sb == 0:
            fm_to_rows(lambda c: xT[:, c, SAMP0:SAMP0 + NSAMP], NSAMP, KC, ys[:, :], [R_x])

    def rstd_pieces(src3, src_regs, nchunks, c0, n, denom, out_cb):
        bi = c0 // 256
        rs = cbs[out_cb]
        st = {}

        def p1():
            S.op("act", lambda e: e.activation(out=sqb[:, 0:nchunks, 0:n], in_=src3, func=AF.Square), reads=src_regs, writes=[R_sqb])

        def p2():
            pt, R_pt = misc_r.next()
            mm_group(pt[:, 0:n], lambda k: ones_b[:], lambda k: sqb[:, k, 0:n], nchunks, [R_onb, R_sqb], [R_pt])
            S.op("act", lambda e: e.activation(out=rs[:, c0:c0 + n], in_=pt[:, 0:n], func=AF.Sqrt, bias=EPS, scale=1.0 / denom),
                 reads=[R_pt], writes=[R_cb[out_cb][bi]])
            S.op("dve", lambda e: e.reciprocal(out=rs[:, c0:c0 + n], in_=rs[:, c0:c0 + n]),
                 reads=[R_cb[out_cb][bi]], writes=[R_cb[out_cb][bi]])
        return [p1, p2]

    def prenorm_pieces(gi, c0, n):
        bi = c0 // 256
        ps = rstd_pieces(xT[:, :, c0:c0 + n], [R_x[bi]], KC, c0, n, float(D), 9)

        def hops(cs):
            for c in cs:
                S.op("dve", lambda e, c=c: e.scalar_tensor_tensor(
                    out=hT[:, c, c0:c0 + n], in0=xT[:, c, c0:c0 + n], scalar=gainT[:, c, gi:gi + 1], in1=cbs[9][:, c0:c0 + n],
                    op0=ALU.mult, op1=ALU.mult), reads=[R_x[bi], R_gain, R_cb[9][bi]], writes=[R_h[bi]])
        return ps + [lambda: hops(range(0, 4)), lambda: hops(range(4, 8))]

    def postnorm_pieces(gi, c0, n):
        bi = c0 // 256
        ps = rstd_pieces(yacc[:, 0:KC, c0:c0 + n], [R_cb[c][bi] for c in range(KC)], KC, c0, n, float(D), 9)

        def xops(cs):
            for c in cs:
                S.op("dve", lambda e, c=c: e.tensor_tensor(out=yacc[:, c, c0:c0 + n], in0=yacc[:, c, c0:c0 + n],
                                                           in1=cbs[9][:, c0:c0 + n], op=ALU.mult),
                     reads=[R_cb[c][bi], R_cb[9][bi]], writes=[R_cb[c][bi]])
                S.op("dve", lambda e, c=c: e.scalar_tensor_tensor(
                    out=xT[:, c, c0:c0 + n], in0=yacc[:, c, c0:c0 + n], scalar=gpost[:, c, gi:gi + 1], in1=xT[:, c, c0:c0 + n],
                    op0=ALU.mult, op1=ALU.add), reads=[R_cb[c][bi], R_gpost, R_x[bi]], writes=[R_x[bi]])
        return ps + [lambda: xops(range(0, 4)), lambda: xops(range(4, 8))]

    epi_q = []

    def pump(k=1):
        for _ in range(k):
            if not epi_q:
                return
            epi_q.pop(0)[1]()

    def need_blk(bi):
        while any(b_ == bi for (b_, _) in epi_q):
            epi_q.pop(0)[1]()

    def flush_epi():
        while epi_q:
            epi_q.pop(0)[1]()

    def prenorm_blk(gi, c0, n):
        for p in prenorm_pieces(gi, c0, n):
            p()

    def rstd_blk(src3, src_regs, nchunks, c0, n, denom, out_cb):
        for p in rstd_pieces(src3, src_regs, nchunks, c0, n, denom, out_cb):
            p()

    def prenorm(gi, ncols):
        for (c0, n) in blocks_of(ncols):
            prenorm_blk(gi, c0, n)

    def rstd_of(src_fn, src_regs, nchunks, ncols, denom, out_cb):
        for (c0, n) in blocks_of(ncols):
            rstd_blk(src_fn(c0, n), src_regs, nchunks, c0, n, denom, out_cb)

    Y3 = pY[:].rearrange("p (m n) -> p m n", n=256)
    R_Y = R_bank[0:4]

    def ffn(fi, ncols, on_block_final):
        blocks = blocks_of(ncols)
        r0 = 0
        while r0 < DFF:
            ncc = 2 if r0 == 0 else 4
            first = (r0 == 0)
            final = (r0 + 128 * ncc >= DFF)
            G, RG = getw(("km", "wg", fi, ((r0, 128 * ncc),)))
            U, RU = getw(("km", "wu", fi, ((r0, 128 * ncc),)))
            Dn, RD = getw(("rows", "wd", fi, r0, ncc))
            r0 += 128 * ncc
            steps = [(c0, n, cc) for (c0, n) in blocks for cc in range(ncc)]
            atiles = {}

            def down(hf, c0, n, ccs):
                ops = []
                for cc in ccs:
                    at, Ra = atiles[(c0, cc)]
                    for m in range(4 * hf, 4 * hf + 4):
                        ops.append((Y3[:, m, 0:n], Dn[:, cc, m * 128:(m + 1) * 128], at[:, 0:n],
                                    cc == 0 and m % 2 == 0, cc == ncc - 1))

                def f(e, ops=ops):
                    ins = None
                    for (o, l, r_, st, sp) in ops:
                        ins = e.matmul(o, lhsT=l, rhs=r_, start=st, stop=sp, skip_group_check=True)
                    return ins
                S.op("pe", f, reads=[RD] + [atiles[(c0, cc)][1] for cc in ccs], writes=R_Y[2 * hf:2 * hf + 2])
                if ccs[-1] == ncc - 1:
                    ysl = yacc[:, 4 * hf:4 * hf + 4, c0:c0 + n]
                    psl = Y3[:, 4 * hf:4 * hf + 4, 0:n]
                    yregs = [R_cb[c][c0 // 256] for c in range(4 * hf, 4 * hf + 4)]
                    if first:
                        S.op("dve", lambda e: e.tensor_copy(out=ysl, in_=psl), reads=R_Y[2 * hf:2 * hf + 2], writes=yregs)
                    else:
                        S.op("dve", lambda e: e.tensor_tensor(out=ysl, in0=psl, in1=ysl, op=ALU.add),
                             reads=R_Y[2 * hf:2 * hf + 2] + yregs, writes=yregs)

            def retire(step):
                c0, n, cc = step
                down(0, c0, n, [cc])
                if cc == ncc - 1:
                    down(1, c0, n, list(range(ncc)))

            prev = None
            fin_q = []
            for (c0, n, cc) in steps:
                need_blk(c0 // 256)
                pgu, Rpg = gu_r.next()
                pg, pu = pgu[:, 0:256], pgu[:, 256:512]
                mm_group(pg[:, 0:n], lambda k: G[:, k, cc * 128:(cc + 1) * 128],
                         lambda k: hT[:, k, c0:c0 + n], KC, [RG, R_h[c0 // 256]], [Rpg])
                mm_group(pu[:, 0:n], lambda k: U[:, k, cc * 128:(cc + 1) * 128],
                         lambda k: hT[:, k, c0:c0 + n], KC, [RU, R_h[c0 // 256]], [Rpg])
                if fin_q and cc == 1:
                    on_block_final(*fin_q.pop(0))
                pump(2)
                sgt, Rsg = sg_r.next()
                at, Ra = a_r.next()
                atiles[(c0, cc)] = (at, Ra)
                S.op("act", lambda e, sgt=sgt, pg=pg, n=n: e.activation(out=sgt[:, 0:n], in_=pg[:, 0:n], func=AF.Silu),
                     reads=[Rpg], writes=[Rsg])
                S.op("dve", lambda e, at=at, sgt=sgt, pu=pu, n=n: e.tensor_tensor(out=at[:, 0:n], in0=sgt[:, 0:n],
                                                                                   in1=pu[:, 0:n], op=ALU.mult),
                     reads=[Rsg, Rpg], writes=[Ra])
                if prev is not None:
                    retire(prev)
                    if final and prev[2] == ncc - 1:
                        fin_q.append((prev[0], prev[1]))
                prev = (c0, n, cc)
            retire(prev)
            relw(RG, RU, RD)
            if final:
                fin_q.append((prev[0], prev[1]))
                while fin_q:
                    on_block_final(*fin_q.pop(0))

    def proj_fm(Wt, RW, coff, blocks, evac):
        for (c0, n) in blocks:
            need_blk(c0 // 256)
            pt, R_pt = gu_r.next()
            mm_group(pt[:, 0:n], lambda k: Wt[:, k, coff:coff + 128], lambda k, c0=c0, n=n: hT[:, k, c0:c0 + n],
                     KC, [RW, R_h[c0 // 256]], [R_pt])
            evac(pt, R_pt, c0, n)
            pump(1)

    def proj_tm(Wt, RW, coff, c0, ntok):
        need_blk(c0 // 256)
        pt, R_pt = gu_r.next()
        mm_group(pt[0:ntok, 0:128], lambda k: hT[:, k, c0:c0 + ntok], lambda k: Wt[:, k, coff:coff + 128],
                 KC, [RW, R_h], [R_pt])
        return pt, R_pt

    def out_proj(specs, ncols, on_block_final):
        Ws = [getw(spec) for spec in specs]
        alt = 0
        pending = None
        for (c0, n) in blocks_of(ncols):
            for mp, (Wt, RW) in enumerate(Ws):
                for mm in range(2):
                    m = 2 * mp + mm
                    pt, R_pt = gu_r.next()
                    mm_group(pt[:, 0:n], lambda k: Wt[:, k, mm * 128:(mm + 1) * 128],
                             lambda k: zT[:, k, c0:c0 + n], KC, [RW] + R_z, [R_pt])
                    if alt % 2 == 0:
                        S.op("act", lambda e, pt=pt, m=m, c0=c0, n=n: e.activation(out=yacc[:, m, c0:c0 + n], in_=pt[:, 0:n],
                                                                                  func=AF.Copy),
                             reads=[R_pt], writes=[R_cb[m][c0 // 256]])
                    else:
                        S.op("dve", lambda e, pt=pt, m=m, c0=c0, n=n: e.tensor_copy(out=yacc[:, m, c0:c0 + n], in_=pt[:, 0:n]),
                             reads=[R_pt], writes=[R_cb[m][c0 // 256]])
                    alt += 1
                    pump(1)
                if mp == 0 and pending is not None:
                    on_block_final(*pending)
                    pending = None
            pending = (c0, n)
        relw(*[rw for (_, rw) in Ws])
        on_block_final(*pending)

    def tiles_of(sb):
        tl = [(t * 128, 128, t) for t in range(SBF // 128)]
        if sb == 0:
            tl += [(META0, NMETA, 8), (SAMP0, NSAMP, 9)]
        return tl

    SCALE = 64 ** -0.5

    S2_r = Ring([(pY[:, 0:1024], R_bank[0:2]), (pY[:, 1024:2048], R_bank[2:4])])

    def attn_tile(j, qa, R_qa, qc0, nq, key_tiles, key_regs, mask):
        nkt = sum(k[2] for k in key_tiles)
        po, R_po = gu_r.next()
        po3 = po.rearrange("p (z q) -> p z q", q=128)
        for half in range(2):
            p0 = 64 * half
            s2, R_s2 = S2_r.next()
            s3 = s2.rearrange("p (z k) -> p z k", k=256)
            groups = []
            for zc in range(4):
                off = 0
                for ti, (kT, vt, nk) in enumerate(key_tiles):
                    groups.append((s3[0:nq, zc, off:off + nk], qa[zc][p0:p0 + 64, qc0:qc0 + nq], kT[p0:p0 + 64, 0:nk],
                                   zc % 2 == 0 and ti == 0, mask is None))
                    off += nk
                if mask is not None:
                    groups.append((s3[0:nq, zc, 0:nkt], ident_b[0:nq, 0:nq], mask[0:nq, 0:nkt], False, True))

            def fs(e, groups=groups):
                ins = None
                for (o, l, r_, st, sp) in groups:
                    ins = e.matmul(o, lhsT=l, rhs=r_, start=st, stop=sp, skip_group_check=True)
                return ins
            S.op("pe", fs, reads=R_qa + key_regs + [R_idb, R_mskb], writes=R_s2)
            sm, R_sm = sm_r.next()
            sm3 = sm.rearrange("p (z k) -> p z k", k=256)
            pn, R_pn = pn_r.next()
            pn3 = pn.rearrange("p (z k) -> p z k", k=256)
            pT, R_pT = pT_r.next()
            cl, R_cl = col_r.next()
            S.op("dve", lambda e, s3=s3, cl=cl: e.reduce_max(out=cl[0:nq, 0:4], in_=s3[0:nq, :, 0:nkt], axis=AX.X),
                 reads=R_s2, writes=[R_cl])
            nsk = negsink[0:nq, 8 * j + 4 * half:8 * j + 4 * half + 4]
            S.op("dve", lambda e, cl=cl, nsk=nsk: e.scalar_tensor_tensor(
                out=cl[0:nq, 4:8], in0=cl[0:nq, 0:4], scalar=-SCALE, in1=nsk, op0=ALU.mult, op1=ALU.min),
                reads=[R_cl, R_sink], writes=[R_cl])
            for zc in range(4):
                S.op("act", lambda e, s3=s3, sm3=sm3, cl=cl, zc=zc: e.activation(
                    out=sm3[0:nq, zc, 0:nkt], in_=s3[0:nq, zc, 0:nkt], func=AF.Exp, bias=cl[0:nq, 4 + zc:5 + zc], scale=SCALE),
                    reads=R_s2 + [R_cl], writes=[R_sm])
            S.op("dve", lambda e, cl=cl, nsk=nsk: e.tensor_tensor(out=cl[0:nq, 8:12], in0=cl[0:nq, 4:8], in1=nsk, op=ALU.subtract),
                 reads=[R_cl, R_sink], writes=[R_cl])
            S.op("act", lambda e, cl=cl: e.activation(out=cl[0:nq, 8:12], in_=cl[0:nq, 8:12], func=AF.Exp), reads=[R_cl], writes=[R_cl])
            S.op("dve", lambda e, sm3=sm3, cl=cl: e.reduce_sum(out=cl[0:nq, 12:16], in_=sm3[0:nq, :, 0:nkt], axis=AX.X),
                 reads=[R_sm], writes=[R_cl])
            S.op("dve", lambda e, cl=cl: e.tensor_tensor(out=cl[0:nq, 12:16], in0=cl[0:nq, 12:16], in1=cl[0:nq, 8:12], op=ALU.add),
                 reads=[R_cl], writes=[R_cl])
            S.op("dve", lambda e, cl=cl: e.reciprocal(out=cl[0:nq, 16:20], in_=cl[0:nq, 12:16]), reads=[R_cl], writes=[R_cl])
            S.op("dve", lambda e, sm3=sm3, pn3=pn3, cl=cl: e.tensor_tensor(
                out=pn3[0:nq, :, 0:nkt], in0=sm3[0:nq, :, 0:nkt], in1=cl[0:nq, 16:20].unsqueeze(2).to_broadcast([nq, 4, nkt]),
                op=ALU.mult), reads=[R_sm, R_cl], writes=[R_pn])
            pt, R_pt = gu_r.next()
            ptb = pt.bitcast(BF16).rearrange("p (g q) -> p g q", q=128)
            nkt_ = len(key_tiles)
            trs = []
            for zc in range(4):
                off = 0
                for ti, (kT, vt, nk) in enumerate(key_tiles):
                    trs.append((ptb[0:nk, zc * nkt_ + ti, 0:nq], pn3[0:nq, zc, off:off + nk]))
                    off += nk

            def ftr(e, trs=trs):
                ins = None
                for (o, i_) in trs:
                    ins = e.transpose(o, i_, ident_b[0:nq, 0:nq])
                return ins
            S.op("pe", ftr, reads=[R_pn, R_idb], writes=[R_pt])
            for ti, (kT, vt, nk) in enumerate(key_tiles):
                src_v = ptb[0:nk, ti:4 * nkt_:nkt_, 0:nq] if nkt_ > 1 else ptb[0:nk, 0:4, 0:nq]
                dst_v = pT[0:nk, ti:4 * nkt_:nkt_, 0:nq] if nkt_ > 1 else pT[0:nk, 0:4, 0:nq]
                S.op("act", lambda e, src_v=src_v, dst_v=dst_v: e.activation(out=dst_v, in_=src_v, func=AF.Copy),
                     reads=[R_pt], writes=[R_pT])
            pvs = []
            for zc in range(4):
                for ti, (kT, vt, nk) in enumerate(key_tiles):
                    pvs.append((po3[p0:p0 + 64, zc, 0:nq], vt[0:nk, p0:p0 + 64], pT[0:nk, zc * nkt_ + ti, 0:nq],
                                ti == 0, ti == nkt_ - 1))

            def fpv(e, pvs=pvs):
                ins = None
                for (o, l, r_, st, sp) in pvs:
                    ins = e.matmul(o, lhsT=l, rhs=r_, start=st, stop=sp, skip_group_check=True)
                return ins
            S.op("pe", fpv, reads=[R_pT] + key_regs, writes=[R_po])
            pump(1)
        S.op("act", lambda e: e.activation(out=zT[:, 0:4, qc0:qc0 + nq], in_=po3[:, :, 0:nq], func=AF.Copy),
             reads=[R_po], writes=R_z[0:4])

    BOFF = {"fr": 0, "meta": SBF + 2, "samp": SBF + 2 + NMETA + 2}
    CTI = {"fr": 0, "meta": 16, "samp": 17}

    def even_mixer(j, sb, ncols, on_block_final):
        last = (sb == n_sb - 1)
        blocks = blocks_of(ncols)
        tiles = tiles_of(sb)
        wn = "wie"
        qa = [bfv(c) for c in range(4)]
        R_qa = [R_cb[c] for c in range(4)]
        kf, R_kf = bfv(4), R_cb[4]
        for half2 in range(2):
            z0 = 2 * half2
            W, RW = getw(("km", wn, j, ((64 * z0, 64), (256 + 64 * z0, 64), (64 * (z0 + 1), 64), (256 + 64 * (z0 + 1), 64))))
            for zi in range(2):
                zc = z0 + zi
                proj_fm(W, RW, 128 * zi, blocks, lambda pt, R_pt, c0, n, zc=zc: S.op(
                    "act", lambda e: e.activation(out=qa[zc][:, c0:c0 + n], in_=pt[:, 0:n], func=AF.Copy),
                    reads=[R_pt], writes=[R_qa[zc]]))
            relw(RW)
        W, RW = getw(("km", wn, j, ((512, 256),)))
        proj_fm(W, RW, 0, blocks, lambda pt, R_pt, c0, n: S.op(
            "act", lambda e: e.activation(out=kf[:, c0:c0 + n], in_=pt[:, 0:n], func=AF.Copy), reads=[R_pt], writes=[R_kf]))
        for (c0, ntok, ti) in tiles:
            pt, R_pt = proj_tm(W, RW, 128, c0, ntok)
            S.op("dve", lambda e, pt=pt, ntok=ntok, ti=ti: e.tensor_copy(out=vtm[0:ntok, ti, :], in_=pt[0:ntok, 0:128]),
                 reads=[R_pt], writes=[R_vtm])
            if last and ti == 7:
                S.op("act", lambda e, pt=pt: e.activation(out=kvf[:, 1, :], in_=pt[:, 0:128], func=AF.Copy),
                     reads=[R_pt], writes=[R_kvf])
                S.dma("pool", lambda e: e.dma_start(out=ovp[j], in_=kvf[:, 1, :]), reads=[R_kvf], writes=[OUT])
                pk, R_pk = proj_tm(W, RW, 0, c0, ntok)
                S.op("act", lambda e, pk=pk: e.activation(out=kvf[:, 0, :], in_=pk[:, 0:128], func=AF.Copy),
                     reads=[R_pk], writes=[R_kvf])
                S.dma("pool", lambda e: e.dma_start(out=okp[j], in_=kvf[:, 0, :]), reads=[R_kvf], writes=[OUT])
            if ti == 9:
                S.op("act", lambda e, pt=pt: e.activation(out=kvf[0:NSAMP, 1, :], in_=pt[0:NSAMP, 0:128], func=AF.Copy),
                     reads=[R_pt], writes=[R_kvf])
                S.dma("pool", lambda e: e.dma_start(out=ovs[j, 96:128, :], in_=kvf[0:NSAMP, 1, :]), reads=[R_kvf], writes=[OUT])
                pk, R_pk = proj_tm(W, RW, 0, c0, ntok)
                S.op("act", lambda e, pk=pk: e.activation(out=kvf[0:NSAMP, 0, :], in_=pk[0:NSAMP, 0:128], func=AF.Copy),
                     reads=[R_pk], writes=[R_kvf])
                S.dma("pool", lambda e: e.dma_start(out=oks[j, 96:128, :], in_=kvf[0:NSAMP, 0, :]), reads=[R_kvf], writes=[OUT])
                S.dma("pool", lambda e: e.dma_start(out=oks[j, 0:96, :], in_=ck[j, 32:128, :]), writes=[OUT])
                S.dma("pool", lambda e: e.dma_start(out=ovs[j, 0:96, :], in_=cv[j, 32:128, :]), writes=[OUT])
        relw(RW)
        kp, R_kp = kprev[j]
        vp, R_vp = vprev[j]
        if sb == 0:
            attn_tile(j, qa, R_qa, META0, NMETA, [(kf[:, META0:META0 + NMETA], vtm[:, 8, :], NMETA)], [R_kf, R_vtm], None)
            S.op("dve", lambda e: e.tensor_copy(out=kp[:, 0:NMETA], in_=kf[:, META0:META0 + NMETA]), reads=[R_kf], writes=[R_kp])
            S.op("dve", lambda e: e.tensor_copy(out=vp[0:NMETA, :], in_=vtm[0:NMETA, 8, :]), reads=[R_vtm], writes=[R_vp])
        for t in range(SBF // 128):
            if t == 0:
                kts = [(kp[:], vp[:], 128), (kf[:, 0:128], vtm[:, 0, :], 128)]
                kregs = [R_kp, R_vp, R_kf, R_vtm]
            else:
                kts = [(kf[:, (t - 1) * 128:t * 128], vtm[:, t - 1, :], 128), (kf[:, t * 128:(t + 1) * 128], vtm[:, t, :], 128)]
                kregs = [R_kf, R_vtm]
            attn_tile(j, qa, R_qa, t * 128, 128, kts, kregs, mskb[:, 256:512] if (sb == 0 and t == 0) else mskb[:, 0:256])
        if not last:
            S.op("dve", lambda e: e.tensor_copy(out=kp[:], in_=kf[:, SBF - 128:SBF]), reads=[R_kf], writes=[R_kp])
            S.op("dve", lambda e: e.tensor_copy(out=vp[:], in_=vtm[:, 7, :]), reads=[R_vtm], writes=[R_vp])
        if sb == 0:
            _, stg, R_stg = take_stage()
            S.dma("pool", lambda e: e.dma_start(out=stg[:, 0:128], in_=ck[j]), writes=[R_stg])
            S.dma("pool", lambda e: e.dma_start(out=stg[:, 128:256], in_=cv[j]), writes=[R_stg])
            S.op("dve", lambda e: e.tensor_copy(out=cvb[:], in_=stg[:, 128:256]), reads=[R_stg], writes=[R_cvb])
            ckb, R_ckb = bfv(7)[:, 0:128], R_cb[7]
            S.op("dve", lambda e: e.tensor_copy(out=ckb, in_=stg[:, 0:128]), reads=[R_stg], writes=[R_ckb])
            pt, R_pt = gu_r.next()
            ptb = pt.bitcast(BF16)
            S.op("pe", lambda e: e.transpose(ptb[:, 0:128], ckb, ident_b[:]), reads=[R_ckb, R_idb], writes=[R_pt])
            S.op("act", lambda e: e.activation(out=ckT[:], in_=ptb[:, 0:128], func=AF.Copy), reads=[R_pt], writes=[R_ckT])
            attn_tile(j, qa, R_qa, SAMP0, NSAMP, [(ckT[:], cvb[:], 128), (kf[:, SAMP0:SAMP0 + NSAMP], vtm[:, 9, :], NSAMP)],
                      [R_ckT, R_cvb, R_kf, R_vtm], None)
        for h in range(4):
            hgrn_head(j, h, sb, ncols, blocks, tiles)
        out_proj([("woe", j, 256 * mp) for mp in range(4)], ncols, on_block_final)

    def hgrn_head(j, h, sb, ncols, blocks, tiles):
        last = (sb == n_sb - 1)
        qs, R_qs = bfv(0), R_cb[0]
        kin, R_kin = bfv(1), R_cb[1]
        gs, R_gs = bfv(2), R_cb[2]
        A, R_A = cbs[3], R_cb[3]
        Bx, R_B = cbs[4], R_cb[4]
        C, R_C = cbs[5], R_cb[5]
        oh, R_oh = cbs[6], R_cb[6]
        W1, RW1 = getw(("km", "wie", j, ((768 + 128 * h, 128), (1280 + 128 * h, 128))))
        W2, RW2 = getw(("km", "wie", j, ((1792 + 128 * h, 128), (2304 + 128 * h, 128))))
        proj_fm(W1, RW1, 0, blocks, lambda pt, R_pt, c0, n: S.op(
            "act", lambda e: e.activation(out=qs[:, c0:c0 + n], in_=pt[:, 0:n], func=AF.Silu), reads=[R_pt], writes=[R_qs]))
        proj_fm(W2, RW2, 128, blocks, lambda pt, R_pt, c0, n: S.op(
            "act", lambda e: e.activation(out=gs[:, c0:c0 + n], in_=pt[:, 0:n], func=AF.Silu), reads=[R_pt], writes=[R_gs]))
        proj_fm(W1, RW1, 128, blocks, lambda pt, R_pt, c0, n: S.op(
            "act", lambda e: e.activation(out=A[:, c0:c0 + n], in_=pt[:, 0:n], func=AF.Sigmoid), reads=[R_pt], writes=[R_A]))
        for (c0, ntok, ti) in tiles:
            pt, R_pt = proj_tm(W2, RW2, 0, c0, ntok)
            S.op("dve", lambda e, pt=pt, ntok=ntok, ti=ti: e.tensor_copy(out=vtm[0:ntok, ti, :], in_=pt[0:ntok, 0:128]),
                 reads=[R_pt], writes=[R_vtm])
        relw(RW1, RW2)
        S.op("dve", lambda e: e.tensor_scalar(out=A[:, 0:ncols], in0=A[:, 0:ncols], scalar1=omlb[:, h, j:j + 1],
                                              scalar2=lbv[:, h, j:j + 1], op0=ALU.mult, op1=ALU.add),
             reads=[R_A, R_omlb, R_lbv], writes=[R_A])
        S.op("dve", lambda e: e.tensor_scalar(out=kin[:, 0:ncols], in0=A[:, 0:ncols], scalar1=-1.0, scalar2=1.0,
                                              op0=ALU.mult, op1=ALU.add), reads=[R_A], writes=[R_kin])
        S.op("act", lambda e: e.activation(out=A[:, 0:ncols], in_=A[:, 0:ncols], func=AF.Ln), reads=[R_A], writes=[R_A])
        segs = [("fr", 0, SBF, 64)]
        if sb == 0:
            segs = [("meta", META0, NMETA, NMETA), ("fr", 0, SBF, 64), ("samp", SAMP0, NSAMP, NSAMP)]
        for (sn, col0, Ltot, Lc) in segs:
            o = BOFF[sn]
            nch = Ltot // Lc
            ci = CTI[sn]
            S.op("dve", lambda e, o=o: e.memset(Bx[:, o:o + 1], 0.0), writes=[R_B])
            S.op("dve", lambda e, o=o, col0=col0, Ltot=Ltot: e.tensor_tensor_scan(
                out=Bx[:, o + 1:o + 1 + Ltot], data0=ones_f[:, 0:1].to_broadcast([128, Ltot]), data1=A[:, col0:col0 + Ltot], initial=0.0,
                op0=ALU.mult, op1=ALU.add), reads=[R_onf, R_A, R_B], writes=[R_B])
            Bst = Bx[:, o:o + Ltot].rearrange("p (c l) -> p c l", l=Lc)[:, :, 0]
            Bin = Bx[:, o + 1:o + 1 + Ltot].rearrange("p (c l) -> p c l", l=Lc)
            Bmd = Bin[:, :, Lc // 2 - 1]
            Bls = Bin[:, :, Lc - 1]
            S.op("dve", lambda e, Bin=Bin, col0=col0, Ltot=Ltot, Lc=Lc, nch=nch: e.tensor_tensor(
                out=A[:, col0:col0 + Ltot].rearrange("p (c l) -> p c l", l=Lc), in0=Bin,
                in1=Bin[:, :, Lc // 2 - 1:Lc // 2].to_broadcast([128, nch, Lc]), op=ALU.subtract),
                reads=[R_B, R_A], writes=[R_A])
            S.op("dve", lambda e, Bmd=Bmd, Bst=Bst, ci=ci, nch=nch: e.tensor_tensor(
                out=ctab[:, 0, ci:ci + nch], in0=Bmd, in1=Bst, op=ALU.subtract), reads=[R_B], writes=[R_ctab])
            S.op("dve", lambda e, Bmd=Bmd, Bls=Bls, ci=ci, nch=nch: e.tensor_tensor(
                out=ctab[:, 1, ci:ci + nch], in0=Bls, in1=Bmd, op=ALU.subtract), reads=[R_B], writes=[R_ctab])
            S.op("dve", lambda e, Bst=Bst, Bls=Bls, ci=ci, nch=nch: e.tensor_tensor(
                out=ctab[:, 2, ci:ci + nch], in0=Bls, in1=Bst, op=ALU.subtract), reads=[R_B], writes=[R_ctab])
        nct = 18 if sb == 0 else 16
        S.op("act", lambda e: e.activation(out=ctab[:, :, 0:nct], in_=ctab[:, :, 0:nct], func=AF.Exp), reads=[R_ctab], writes=[R_ctab])
        S.op("act", lambda e: e.activation(out=C[:, 0:ncols], in_=A[:, 0:ncols], func=AF.Exp), reads=[R_A], writes=[R_C])
        S.op("dve", lambda e: e.tensor_tensor(out=qs[:, 0:ncols], in0=qs[:, 0:ncols], in1=C[:, 0:ncols], op=ALU.mult),
             reads=[R_qs, R_C], writes=[R_qs])
        S.op("act", lambda e: e.activation(out=C[:, 0:ncols], in_=A[:, 0:ncols], func=AF.Exp, scale=-1.0), reads=[R_A], writes=[R_C])
        S.op("dve", lambda e: e.tensor_tensor(out=kin[:, 0:ncols], in0=kin[:, 0:ncols], in1=C[:, 0:ncols], op=ALU.mult),
             reads=[R_kin, R_C], writes=[R_kin])
        if sb == 0:
            St, R_St = Ssm[h]
            S.dma("pool", lambda e: e.dma_start(out=St[:], in_=sh[j, h]), writes=[R_St])
        chunks = []
        for (sn, col0, Ltot, Lc) in segs:
            for c in range(Ltot // Lc):
                if sn == "fr":
                    chunks.append((sn, col0 + c * Lc, Lc, CTI[sn] + c, c // 2, 64 * (c % 2), (c // 2) * 128, 128, c == Ltot // Lc - 1))
                else:
                    chunks.append((sn, col0, Lc, CTI[sn], 8 if sn == "meta" else 9, 0, col0, Ltot, True))
        ktm_cur = {}

        def state_part(ch):
            sn, cs, Lc, ci, ti, p0, tcol0, ntile, seg_last = ch
            St, R_St = Ssm[h] if sn == "samp" else Sst[j][h]
            if p0 == 0:
                pt, R_pt = gu_r.next()
                ptb = pt.bitcast(BF16)
                ktm, R_ktm = ktm_r.next()
                ktm_cur["k"] = (ktm, R_ktm)
                S.op("pe", lambda e: e.transpose(ptb[0:ntile, 0:128], kin[:, tcol0:tcol0 + ntile], ident_b[:]),
                     reads=[R_kin, R_idb], writes=[R_pt])
                S.op("act", lambda e: e.activation(out=ktm[0:ntile, :], in_=ptb[0:ntile, 0:128], func=AF.Copy),
                     reads=[R_pt], writes=[R_ktm])
            ktm, R_ktm = ktm_cur["k"]
            Sb, R_Sb = Sb_r.next()
            S.op("dve", lambda e: e.tensor_scalar(out=Sb[:], in0=St[:], scalar1=ctab[:, 0, ci:ci + 1], scalar2=None, op0=ALU.mult),
                 reads=[R_St, R_ctab], writes=[R_Sb])
            pn_, R_pn_ = ya_r.next()
            mm_group(pn_[:, 0:128], lambda k: ktm[p0:p0 + Lc, :], lambda k: vtm[p0:p0 + Lc, ti, :], 1, [R_ktm, R_vtm], [R_pn_])
            tmp, R_tmp = tmp_r.next()
            S.op("act", lambda e: e.activation(out=tmp[:], in_=pn_[:, 0:128], func=AF.Copy, scale=ctab[:, 1, ci:ci + 1]),
                 reads=[R_pn_, R_ctab], writes=[R_tmp])
            S.op("dve", lambda e: e.scalar_tensor_tensor(out=St[:], in0=St[:], scalar=ctab[:, 2, ci:ci + 1], in1=tmp[:],
                                                         op0=ALU.mult, op1=ALU.add), reads=[R_St, R_ctab, R_tmp], writes=[R_St])
            if seg_last and sn == "samp":
                S.dma("pool", lambda e: e.dma_start(out=ohs[j, h], in_=St[:]), reads=[R_St], writes=[OUT])
            if seg_last and sn == "fr" and last:
                S.dma("pool", lambda e: e.dma_start(out=ohp[j, h], in_=St[:]), reads=[R_St], writes=[OUT])
            return (Sb, R_Sb)

        def out_part(ch, Sbt):
            sn, cs, Lc, ci, ti, p0, tcol0, ntile, seg_last = ch
            Sb, R_Sb = Sbt
            ps, R_s = yb_r.next()
            mm_group(ps[p0:p0 + Lc, 0:Lc], lambda k: kin[:, cs:cs + Lc], lambda k: qs[:, cs:cs + Lc], 1, [R_kin, R_qs], [R_s])
            pth, R_pth = pth_r.next()
            S.op("dve", lambda e: e.tensor_tensor(out=pth[p0:p0 + Lc, 0:Lc], in0=ps[p0:p0 + Lc, 0:Lc],
                                                  in1=mask_hg[p0:p0 + Lc, 0:Lc], op=ALU.mult), reads=[R_s, R_cst], writes=[R_pth])
            po, R_po = gu_r.next()

            def fo(e):
                e.matmul(po[:, 0:Lc], lhsT=Sb[:], rhs=qs[:, cs:cs + Lc], start=True, stop=False)
                return e.matmul(po[:, 0:Lc], lhsT=vtm[p0:p0 + Lc, ti, :], rhs=pth[p0:p0 + Lc, 0:Lc], start=False, stop=True)
            S.op("pe", fo, reads=[R_Sb, R_qs, R_vtm, R_pth], writes=[R_po])
            S.op("act", lambda e: e.activation(out=oh[:, cs:cs + Lc], in_=po[:, 0:Lc], func=AF.Copy), reads=[R_po], writes=[R_oh])

        prev = None
        for ch in chunks:
            sbt_ = state_part(ch)
            if prev is not None:
                out_part(*prev)
                pump(1)
            prev = (ch, sbt_)
        out_part(*prev)
        rstd_of(lambda c0, n: oh[:, c0:c0 + n].unsqueeze(1), [R_oh], 1, ncols, 128.0, 9)
        S.op("dve", lambda e: e.tensor_tensor(out=oh[:, 0:ncols], in0=oh[:, 0:ncols], in1=cbs[9][:, 0:ncols], op=ALU.mult),
             reads=[R_oh, R_cb[9]], writes=[R_oh])
        S.op("dve", lambda e: e.scalar_tensor_tensor(
            out=zT[:, 4 + h, 0:ncols], in0=oh[:, 0:ncols], scalar=hngT[:, h, j:j + 1], in1=gs[:, 0:ncols],
            op0=ALU.mult, op1=ALU.mult), reads=[R_oh, R_hng, R_gs], writes=[R_z[4 + h]])

    def odd_mixer(j, sb, ncols, on_block_final):
        last = (sb == n_sb - 1)
        blocks = blocks_of(ncols)
        uh, R_uh = uhalo[j]
        Wst = {}

        def bufs(jc):
            o = 4 * (jc % 2)
            return (bfv(o), R_cb[o]), (cbs[o + 1], R_cb[o + 1]), (cbs[o + 2], R_cb[o + 2]), (cbs[o + 3], R_cb[o + 3])

        def proj(jc):
            (bgb, R_bg), (cgf, R_cg), (uF, R_uF), (yv, R_yv) = bufs(jc)
            uaux, R_uaux = uaux2[jc % 2]
            W1, RW1 = getw(("km", "wio", j, ((128 * jc, 128), (D + 128 * jc, 128))))
            if jc % 2 == 0:
                Wst["w2"] = getw(("km", "wio", j, ((2 * D + 128 * jc, 256),)))
            W2, RW2 = Wst["w2"]
            proj_fm(W1, RW1, 0, blocks, lambda pt, R_pt, c0, n: S.op(
                "act", lambda e: e.activation(out=bgb[:, c0:c0 + n], in_=pt[:, 0:n], func=AF.Copy), reads=[R_pt], writes=[R_bg]))
            proj_fm(W1, RW1, 128, blocks, lambda pt, R_pt, c0, n: S.op(
                "act", lambda e: e.activation(out=cgf[:, c0:c0 + n], in_=pt[:, 0:n], func=AF.Copy), reads=[R_pt], writes=[R_cg]))

            def evac_u(pt, R_pt, c0, n):
                if c0 < SBF:
                    S.op("dve", lambda e: e.tensor_tensor(out=uF[:, 2 + c0:2 + c0 + n], in0=cgf[:, c0:c0 + n], in1=pt[:, 0:n],
                                                          op=ALU.mult), reads=[R_cg, R_pt], writes=[R_uF])
                else:
                    S.op("dve", lambda e: e.tensor_tensor(out=uaux[:, 2:2 + NMETA], in0=cgf[:, META0:META0 + NMETA],
                                                          in1=pt[:, 0:NMETA], op=ALU.mult), reads=[R_cg, R_pt], writes=[R_uaux])
                    S.op("dve", lambda e: e.tensor_tensor(out=uaux[:, 20:20 + NSAMP], in0=cgf[:, SAMP0:SAMP0 + NSAMP],
                                                          in1=pt[:, NMETA:NMETA + NSAMP], op=ALU.mult),
                         reads=[R_cg, R_pt], writes=[R_uaux])
            proj_fm(W2, RW2, 128 * (jc % 2), blocks, evac_u)
            relw(RW1)
            if jc % 2 == 1:
                relw(RW2)

        def conv(jc):
            (bgb, R_bg), (cgf, R_cg), (uF, R_uF), (yv, R_yv) = bufs(jc)
            uaux, R_uaux = uaux2[jc % 2]
            if sb == 0:
                S.op("dve", lambda e: e.memset(uaux[:, 0:2], 0.0), writes=[R_uaux])
                S.op("dve", lambda e: e.tensor_copy(out=uaux[:, 18:20], in_=ccT[:, jc, 2 * j:2 * j + 2]),
                     reads=[R_ccT], writes=[R_uaux])
                S.op("dve", lambda e: e.tensor_copy(out=uF[:, 0:2], in_=uaux[:, NMETA:NMETA + 2]), reads=[R_uaux], writes=[R_uF])
            else:
                S.op("dve", lambda e: e.tensor_copy(out=uF[:, 0:2], in_=uh[:, jc, :]), reads=[R_uh], writes=[R_uF])
            segs = [(uF, R_uF, 0, 0, SBF)]
            if sb == 0:
                segs += [(uaux, R_uaux, 0, META0, NMETA), (uaux, R_uaux, 18, SAMP0, NSAMP)]
            for (ub, R_ub, uo, yc0, N) in segs:
                w0 = cwT[:, jc, 3 * j + 0:3 * j + 1]
                w1 = cwT[:, jc, 3 * j + 1:3 * j + 2]
                w2 = cwT[:, jc, 3 * j + 2:3 * j + 3]
                S.op("dve", lambda e, ub=ub, uo=uo, yc0=yc0, N=N, w0=w0: e.tensor_scalar(
                    out=yv[:, yc0:yc0 + N], in0=ub[:, uo:uo + N], scalar1=w0, scalar2=None, op0=ALU.mult),
                    reads=[R_ub, R_cw], writes=[R_yv])
                S.op("dve", lambda e, ub=ub, uo=uo, yc0=yc0, N=N, w1=w1: e.scalar_tensor_tensor(
                    out=yv[:, yc0:yc0 + N], in0=ub[:, uo + 1:uo + 1 + N], scalar=w1, in1=yv[:, yc0:yc0 + N],
                    op0=ALU.mult, op1=ALU.add), reads=[R_ub, R_cw, R_yv], writes=[R_yv])
                S.op("dve", lambda e, ub=ub, uo=uo, yc0=yc0, N=N, w2=w2: e.scalar_tensor_tensor(
                    out=yv[:, yc0:yc0 + N], in0=ub[:, uo + 2:uo + 2 + N], scalar=w2, in1=yv[:, yc0:yc0 + N],
                    op0=ALU.mult, op1=ALU.add), reads=[R_ub, R_cw, R_yv], writes=[R_yv])
            S.op("dve", lambda e: e.tensor_tensor(out=zT[:, jc, 0:ncols], in0=bgb[:, 0:ncols], in1=yv[:, 0:ncols], op=ALU.mult),
                 reads=[R_bg, R_yv], writes=[R_z[jc]])
            S.op("dve", lambda e: e.tensor_copy(out=uh[:, jc, :], in_=uF[:, SBF:SBF + 2]), reads=[R_uF], writes=[R_uh])
            if sb == 0:
                S.op("dve", lambda e: e.tensor_copy(out=uSh[:, jc, :], in_=uaux[:, 18 + NSAMP:20 + NSAMP]),
                     reads=[R_uaux], writes=[R_uSh])

        proj(0)
        for jc in range(KC):
            if jc + 1 < KC:
                proj(jc + 1)
            conv(jc)
        if last:
            fm_to_rows(lambda c: uh[:, c, :], 2, KC, ocp[2 * j:2 * j + 2, :], [R_uh])
        if sb == 0:
            fm_to_rows(lambda c: uSh[:, c, :], 2, KC, ocs[2 * j:2 * j + 2, :], [R_uSh])
        out_proj([("km", "woo", j, ((256 * mp, 256),)) for mp in range(4)], ncols, on_block_final)

    MIXERS = {"even": even_mixer, "odd": odd_mixer}

    def main(max_stage=None):
        import os
        skip = os.environ.get("KSKIP", "")
        if "setup" in skip:
            S.dma("sp", lambda e: e.dma_start(out=cst[:], in_=consts[:, :]), writes=[R_cst])
        else:
            setup()
        if "io" in skip:
            return
        for sb in range(n_sb):
            ncols = NCM if sb == 0 else SBF
            load_x(sb)
            stages = [(l, s) for l in range(n_layers) for s in range(3)]
            if max_stage is not None:
                stages = stages[:max_stage]
            gidx = [l * 6 + 2 * s for (l, s) in stages]
            if stages:
                prenorm(gidx[0], ncols)
            for si, (l, s) in enumerate(stages):
                nxt = gidx[si + 1] if si + 1 < len(stages) else None

                import os
                late = ("late%d" % s) in os.environ.get("KSKIP", "")

                def epilogue(c0, n, g=gidx[si], nxt=nxt, late=late):
                    for p in postnorm_pieces(g + 1, c0, n):
                        epi_q.append((c0 // 256, p))
                    if nxt is not None and not late:
                        for p in prenorm_pieces(nxt, c0, n):
                            epi_q.append((c0 // 256, p))
                if s == 0:
                    ffn(2 * l, ncols, epilogue)
                elif s == 1:
                    MIXERS["even" if l % 2 == 0 else "odd"](l // 2, sb, ncols, epilogue)
                else:
                    ffn(2 * l + 1, ncols, epilogue)
                if late and nxt is not None:
                    flush_epi()
                    prenorm(nxt, ncols)
            flush_epi()
            store_y(sb)

    main(dbg_at)
    if plan is None:
        S.stack.close()
        return None, ws["rec"]
    S.finish([OUT])
    return S, ws["rec"]


IN_NAMES = ["xp", "xs", "ck", "cv", "sh", "ccv", "meta", "gains", "wg", "wu", "wd", "wie", "woe", "sinks", "lbl", "hng",
            "wio", "cw", "woo", "consts"]


def make_nc(**kw):
    nc0 = bass.Bass("TRN2", target_bir_lowering=False)
    _, plan = build_program(nc0, plan=None, **kw)
    nc = bass.Bass("TRN2", target_bir_lowering=False)
    S, rec = build_program(nc, plan=plan, **kw)
    assert rec == plan
    return nc, S


def core_inputs(inp, b, sbi, shared):
    m = dict(shared)
    m["xp"] = np.ascontiguousarray(inp["x_prompt"][b])
    m["xs"] = np.ascontiguousarray(inp["x_sample"][sbi])
    m["ck"] = np.ascontiguousarray(inp["cache_swa_k"][:, sbi]).reshape(2, 128, 128)
    m["cv"] = np.ascontiguousarray(inp["cache_swa_v"][:, sbi]).reshape(2, 128, 128)
    m["sh"] = np.ascontiguousarray(inp["state_hgrn"][:, sbi])
    m["ccv"] = np.ascontiguousarray(inp["cache_conv"][:, sbi]).reshape(4, D)
    return m


def shared_inputs(inp):
    f = lambda a: np.ascontiguousarray(np.asarray(a, dtype=np.float32))
    return {
        "meta": f(inp["meta_tokens"]), "gains": f(inp["norm_gains"]).reshape(24, D),
        "wg": f(inp["w_ffn_gate"]).reshape(8, D, DFF), "wu": f(inp["w_ffn_up"]).reshape(8, D, DFF),
        "wd": f(inp["w_ffn_down"]).reshape(8, DFF, D), "wie": f(inp["w_in_even"]), "woe": f(inp["w_out_even"]),
        "sinks": f(inp["attn_sinks"]).reshape(1, 16), "lbl": f(inp["hgrn_lb_logits"]),
        "hng": f(inp["hgrn_norm_gain"]).reshape(2, 512), "wio": f(inp["w_in_odd"]),
        "cw": f(inp["conv_w"]).reshape(6, D), "woo": f(inp["w_out_odd"]), "consts": host_consts(),
    }


_CACHE = {}


def kernel(**inputs):
    if "nc" not in _CACHE:
        _CACHE["nc"] = make_nc()[0]
    nc = _CACHE["nc"]
    inp = {k: np.asarray(v) for k, v in inputs.items()}
    sh = shared_inputs(inp)
    in_maps = [core_inputs(inp, c % 4, c, sh) for c in range(8)]
    res = run_bass_kernel_spmd(nc, in_maps, core_ids=list(range(8)))
    r = res.results
    f32 = np.float32
    y_prompt = np.stack([r[b]["yp"] for b in range(4)]).astype(f32)
    y_sample = np.stack([r[c]["ys"] for c in range(8)]).astype(f32)

    def gather(name, cores, shape):
        a = np.stack([r[c][name] for c in cores], axis=1)
        return np.ascontiguousarray(a.reshape(shape)).astype(f32)
    P, Q = range(4), range(8)
    swa_k_p = gather("okp", P, (2, 4, 128, 2, 64))
    swa_v_p = gather("ovp", P, (2, 4, 128, 2, 64))
    hgrn_p = gather("ohp", P, (2, 4, 4, 128, 128))
    conv_p = np.stack([r[c]["ocp"].reshape(2, 2, D) for c in P], axis=1).astype(f32)
    swa_k_s = gather("oks", Q, (2, 8, 128, 2, 64))
    swa_v_s = gather("ovs", Q, (2, 8, 128, 2, 64))
    hgrn_s = gather("ohs", Q, (2, 8, 4, 128, 128))
    conv_s = np.stack([r[c]["ocs"].reshape(2, 2, D) for c in Q], axis=1).astype(f32)
    return (y_prompt, y_sample, swa_k_p, swa_v_p, hgrn_p, conv_p, swa_k_s, swa_v_s, hgrn_s, conv_s)
```

```python
import numpy as np
from contextlib import ExitStack
import concourse.bass as bass
import concourse.mybir as mybir
from concourse.bass_utils import run_bass_kernel_spmd

F32 = mybir.dt.float32
BF16 = mybir.dt.bfloat16
AF = mybir.ActivationFunctionType
ALU = mybir.AluOpType
AX = mybir.AxisListType


class Reg:
    __slots__ = ("name", "last_w", "readers", "multi", "writers", "excl")

    def __init__(self, name, multi=False, excl=False):
        self.name = name
        self.last_w = None
        self.readers = []
        self.multi = multi
        self.writers = []
        self.excl = excl


class _Op:
    __slots__ = ("eng", "fn", "deps", "is_dma", "needs_inc", "sem", "val", "lane_prev", "idx")


EPOCH = 30000
NLANES = 32


class Sched:
    def __init__(self, nc):
        self.nc = nc
        self.stack = ExitStack()
        self.ops = []
        self.nsem = 0

    def sb(self, name, shape, dtype):
        return self.stack.enter_context(self.nc.sbuf_tensor(name, shape, dtype))

    def ps(self, name, shape, dtype):
        return self.stack.enter_context(self.nc.psum_tensor(name, shape, dtype))

    def _sem(self, name):
        self.nsem += 1
        return self.stack.enter_context(self.nc.semaphore(name))

    @staticmethod
    def _flat(regs):
        out = []
        for r in regs:
            if isinstance(r, (list, tuple)):
                out.extend(Sched._flat(r))
            else:
                out.append(r)
        return out

    def _add(self, eng, fn, reads, writes, is_dma):
        reads = self._flat(reads)
        writes = self._flat(writes)
        op = _Op()
        op.eng = eng
        op.fn = fn
        op.is_dma = is_dma
        op.needs_inc = False
        op.sem = None
        op.val = 0
        op.lane_prev = None
        op.idx = len(self.ops)
        deps = {}
        for r in reads:
            if r.multi:
                for w in r.writers:
                    deps[w.idx] = w
            else:
                if r.last_w is not None:
                    deps[r.last_w.idx] = r.last_w
                if r.excl:
                    for rd in r.readers:
                        if rd.eng != eng:
                            deps[rd.idx] = rd
        for w in writes:
            if w.multi:
                continue
            if w.last_w is not None:
                deps[w.last_w.idx] = w.last_w
            for rd in w.readers:
                deps[rd.idx] = rd
        for r in reads:
            if not r.multi:
                r.readers.append(op)
        for w in writes:
            if w.multi:
                w.writers.append(op)
            else:
                w.last_w = op
                w.readers = []
        final = []
        for i in sorted(deps):
            d = deps[i]
            if d is op:
                continue
            if (not is_dma) and (not d.is_dma) and d.eng == "pe" and eng == "pe":
                continue
            d.needs_inc = True
            final.append(d)
        op.deps = final
        self.ops.append(op)
        return op

    def op(self, eng, fn, reads=(), writes=()):
        return self._add(eng, fn, reads, writes, False)

    def dma(self, eng, fn, reads=(), writes=()):
        return self._add(eng, fn, reads, writes, True)

    def finish(self, outputs):
        self._add("sp", None, outputs, (), False)
        cnt = {}
        engsems = {}
        lanes = {}
        lane_cnt = {}
        lane_last = {}
        ndma_e = {}
        ndma = 0
        for op in self.ops:
            if op.is_dma:
                nd = ndma_e.get(op.eng, 0)
                ndma_e[op.eng] = nd + 1
                ndma += 1
                ln = (op.eng, nd % NLANES)
                n = lane_cnt.get(ln, 0)
                ep = n // (EPOCH // 16)
                lst = lanes.setdefault(ln, [])
                while len(lst) <= ep:
                    lst.append(self._sem("ln%s%d_%d" % (op.eng, ln[1], len(lst))))
                op.sem = lst[ep]
                op.val = (n - ep * (EPOCH // 16) + 1) * 16
                op.lane_prev = lane_last.get(ln)
                lane_last[ln] = op
                lane_cnt[ln] = n + 1
            elif op.needs_inc:
                n = cnt.get(op.eng, 0)
                ep = n // EPOCH
                lst = engsems.setdefault(op.eng, [])
                while len(lst) <= ep:
                    lst.append(self._sem("%s_%d" % (op.eng, len(lst))))
                op.sem = lst[ep]
                op.val = n - ep * EPOCH + 1
                cnt[op.eng] = n + 1
        by_eng = {}
        for op in self.ops:
            by_eng.setdefault(op.eng, []).append(op)
        self.stats = {k: len(v) for k, v in by_eng.items()}
        self.stats["ndma"] = ndma
        self.stats["nsem"] = self.nsem

        def emit(name, e):
            known = {}
            for op in by_eng.get(name, []):
                ws = list(op.deps)
                if op.lane_prev is not None:
                    ws.append(op.lane_prev)
                for d in ws:
                    k = id(d.sem)
                    if known.get(k, 0) < d.val:
                        e.wait_ge(d.sem, d.val)
                        known[k] = d.val
                if op.fn is not None:
                    ins = op.fn(e)
                    if op.is_dma:
                        ins.then_inc(op.sem, 16)
                    elif op.needs_inc:
                        ins.then_inc(op.sem, 1)

        with self.nc.Block() as block:
            @block.tensor
            def _(e):
                emit("pe", e)

            @block.scalar
            def _(e):
                emit("act", e)

            @block.vector
            def _(e):
                emit("dve", e)

            @block.gpsimd
            def _(e):
                emit("pool", e)

            @block.sync
            def _(e):
                emit("sp", e)
        self.stack.close()


D = 1024
KC = 8
DFF = 2816
SEQ = 4096
NMETA = 16
NSAMP = 32
SBF = 1024
NSB = SEQ // SBF
NCM = SBF + NMETA + NSAMP
CB = 1080
META0 = SBF
SAMP0 = SBF + NMETA
EPS = 1e-6
NEG = -30000.0
NCONST = 128 + 256 + 256 + 64


def host_consts():
    c = np.zeros((128, NCONST), np.float32)
    c[:, 0:128] = np.eye(128, dtype=np.float32)
    q = np.arange(128)[:, None]
    k = np.arange(256)[None, :]
    gen = np.where(q < 64, k < 192, k >= 64)
    c[:, 128:384] = np.where(gen, 0.0, NEG)
    first = np.where(k < 16, True, np.where(k < 128, False, np.where(q < 64, k < 192, True)))
    c[:, 384:640] = np.where(first, 0.0, NEG)
    s = (np.arange(128) % 64)[:, None]
    t = np.arange(64)[None, :]
    c[:, 640:704] = (s <= t).astype(np.float32)
    return c


class Ring:
    def __init__(self, items):
        self.items = items
        self.i = 0

    def next(self):
        it = self.items[self.i % len(self.items)]
        self.i += 1
        return it


LA_D, LA_C = 3, 3


def build_program(nc, plan=None, n_sb=NSB, n_layers=4, dbg_at=None):
    S = Sched(nc)
    T = {}

    def din(name, shape):
        T[name] = nc.dram_tensor(name, list(shape), F32, kind="ExternalInput").ap()
        return T[name]

    def dout(name, shape):
        T[name] = nc.dram_tensor(name, list(shape), F32, kind="ExternalOutput").ap()
        return T[name]

    xp = din("xp", [SEQ, D]); xs = din("xs", [NSAMP, D])
    ck = din("ck", [2, 128, 128]); cv = din("cv", [2, 128, 128])
    sh = din("sh", [2, 4, 128, 128]); ccv = din("ccv", [4, D])
    meta = din("meta", [NMETA, D]); gains = din("gains", [24, D])
    din("wg", [8, D, DFF]); din("wu", [8, D, DFF]); din("wd", [8, DFF, D])
    din("wie", [2, D, DFF]); din("woe", [2, D, D])
    sinks = din("sinks", [1, 16]); lbl = din("lbl", [2, 512]); hng = din("hng", [2, 512])
    din("wio", [2, D, 3 * D]); cw = din("cw", [6, D]); din("woo", [2, D, D])
    consts = din("consts", [128, NCONST])
    yp = dout("yp", [SEQ, D]); ys = dout("ys", [NSAMP, D])
    okp = dout("okp", [2, 128, 128]); ovp = dout("ovp", [2, 128, 128])
    ohp = dout("ohp", [2, 4, 128, 128]); ocp = dout("ocp", [4, D])
    oks = dout("oks", [2, 128, 128]); ovs = dout("ovs", [2, 128, 128])
    ohs = dout("ohs", [2, 4, 128, 128]); ocs = dout("ocs", [4, D])
    OUT = Reg("outputs", multi=True)
    dbgx = dout("dbgx", [128, KC * NCM]) if dbg_at is not None else None

    def sbt(name, shape, dt=F32):
        return S.sb(name, shape, dt), Reg(name)

    NBLK = 5
    xT = S.sb("xT", [128, KC, NCM], F32)
    hT = S.sb("hT", [128, KC, NCM], BF16)
    R_x = [Reg("x%d" % b) for b in range(NBLK)]
    R_h = [Reg("h%d" % b) for b in range(NBLK)]
    zT = S.sb("zT", [128, KC, NCM], BF16)
    R_z = [Reg("z%d" % c) for c in range(KC)]
    NCB = KC + 2
    yacc = S.sb("yacc", [128, KC, CB], F32)
    cbs = [yacc[:, i, :] for i in range(KC)] + [S.sb("cb%d" % i, [128, CB], F32)[:] for i in range(KC, NCB)]
    sqb, R_sqb = sbt("sqb", [128, KC, 256], BF16)
    R_cb = [[Reg("cb%d_%d" % (i, b)) for b in range(NBLK)] for i in range(NCB)]
    NSLOT, NSTG = 6, 2
    wsl = [S.sb("wsl%d" % i, [128, 4096], BF16) for i in range(NSLOT)]
    R_wsl = [Reg("wsl%d" % i) for i in range(NSLOT)]
    wst = [S.sb("wst%d" % i, [128, 1024], F32) for i in range(NSTG)]
    R_wst = [Reg("wst%d" % i) for i in range(NSTG)]
    cst, R_cst = sbt("cst", [128, NCONST])
    ident_f = cst[:, 0:128]
    mask_gen = cst[:, 128:384]
    mask_first = cst[:, 384:640]
    mask_hg = cst[:, 640:704]
    ident_b, R_idb = sbt("ident_b", [128, 128], BF16)
    ones_b, R_onb = sbt("ones_b", [128, 128], BF16)
    ones_f, R_onf = sbt("ones_f", [128, 1], F32)
    gainT, R_gain = sbt("gainT", [128, KC, 24])
    gpost, R_gpost = sbt("gpost", [128, KC, 24])
    cwT, R_cw = sbt("cwT", [128, KC, 6])
    sinkc, R_sink = sbt("sinkc", [128, 16])
    negsink = S.sb("negsink", [128, 16], F32)
    mskb, R_mskb = sbt("mskb", [128, 512], BF16)
    lbT, R_lbT = sbt("lbT", [128, 4, 2])
    lbv, R_lbv = sbt("lbv", [128, 4, 2])
    omlb, R_omlb = sbt("omlb", [128, 4, 2])
    hngT, R_hng = sbt("hngT", [128, 4, 2])
    kprev = [sbt("kprev%d" % j, [128, 128], BF16) for j in range(2)]
    vprev = [sbt("vprev%d" % j, [128, 128], BF16) for j in range(2)]
    vtm, R_vtm = sbt("vtm", [128, 10, 128], BF16)
    kvf, R_kvf = sbt("kvf", [128, 2, 128], F32)
    ckT, R_ckT = sbt("ckT", [128, 128], BF16)
    cvb, R_cvb = sbt("cvb", [128, 128], BF16)
    col_r = Ring([sbt("col%d" % i, [128, 20]) for i in range(2)])
    Sst = [[sbt("S%d_%d" % (j, h), [128, 128]) for h in range(4)] for j in range(2)]
    Ssm = [sbt("Ss%d" % h, [128, 128]) for h in range(4)]
    Sb_r = Ring([sbt("Sb%d" % i, [128, 128], BF16) for i in range(4)])
    tmp_r = Ring([sbt("tmpS%d" % i, [128, 128]) for i in range(3)])
    ktm_r = Ring([sbt("ktm%d" % i, [128, 128], BF16) for i in range(2)])
    pth_r = Ring([sbt("pth%d" % i, [128, 64], BF16) for i in range(4)])
    ctab, R_ctab = sbt("ctab", [128, 3, 20])
    uhalo = [sbt("uhalo%d" % j, [128, KC, 2]) for j in range(2)]
    uSh, R_uSh = sbt("uSh", [128, KC, 2])
    ccT, R_ccT = sbt("ccT", [128, KC, 4])
    uaux2 = [sbt("uaux%d" % i, [128, 64]) for i in range(2)]
    sg_r = Ring([sbt("sg%d" % i, [128, 256]) for i in range(3)])
    a_r = Ring([sbt("a%d" % i, [128, 256], BF16) for i in range(8)])

    pY = S.ps("pY", [128, 2048], F32)
    pX = S.ps("pX", [128, 2048], F32)
    bank = [pY[:, i * 512:(i + 1) * 512] for i in range(4)] + [pX[:, i * 512:(i + 1) * 512] for i in range(4)]
    R_bank = [Reg("bank%d" % b, excl=True) for b in range(8)]
    gu_r = Ring([(bank[b], R_bank[b]) for b in range(4, 8)])
    misc_r = gu_r
    ya_r = Ring([(bank[b], R_bank[b]) for b in range(0, 2)])
    yb_r = Ring([(bank[b], R_bank[b]) for b in range(2, 4)])
    stg_r = Ring(list(zip(wst, R_wst)))
    slot_r = Ring(list(zip(wsl, R_wsl)))

    def pieces_of(spec):
        kind = spec[0]
        if kind == "km":
            _, tn, idx, cols = spec
            wm = T[tn][idx].rearrange("(k p) c -> p k c", p=128)
            out, off = [], 0
            for c0, w in cols:
                out.append((lambda st3, off=off, w=w: st3[:, :, off:off + w], wm[:, :, c0:c0 + w]))
                off += w
            return out, 8, off
        if kind == "rows":
            _, tn, idx, r0, ncc = spec
            src = T[tn][idx][r0:r0 + 128 * ncc, :].rearrange("(c p) m -> p c m", p=128)
            return [(lambda st3: st3[:, :, :], src)], ncc, 1024
        if kind == "woe":
            _, j, m0 = spec
            wm = T["woe"][j]
            out = []
            for q in range(4):
                out.append((lambda st3, q=q: st3[0:64, q, :], wm[64 * q:64 * q + 64, m0:m0 + 256]))
                out.append((lambda st3, q=q: st3[64:128, q, :], wm[256 + 64 * q:256 + 64 * q + 64, m0:m0 + 256]))
            out.append((lambda st3: st3[:, 4:8, :],
                        wm[512:1024, :].rearrange("(k p) c -> p k c", p=128)[:, :, m0:m0 + 256]))
            return out, 8, 256
        raise ValueError(kind)

    ws = {"rec": [], "dma": 0, "cast": 0, "h": [], "st": [], "pend": {}}

    def take_stage():
        idx = stg_r.i % NSTG
        stg, R_stg = stg_r.next()
        return idx, stg, R_stg

    ws["busy"] = [False] * NSLOT
    ws["nslot"] = 0

    def _w_dma(spec):
        si = ws["nslot"] % NSLOT
        if ws["busy"][si]:
            return False
        ws["nslot"] += 1
        ws["busy"][si] = True
        pieces, k, w = pieces_of(spec)
        slot, R_slot = wsl[si], R_wsl[si]
        sl3 = slot[:, 0:k * w].rearrange("p (k w) -> p k w", w=w)
        for dst_fn, src in pieces:
            S.dma("pool", lambda e, d=dst_fn(sl3), s=src: e.dma_start(out=d, in_=s), writes=[R_slot])
        ws["h"].append((sl3, R_slot))
        ws["dma"] += 1
        return True

    def relw(*regs):
        for r in regs:
            ws["busy"][R_wsl.index(r)] = False

    def getw(spec):
        i = len(ws["rec"])
        ws["rec"].append(spec)
        if plan is None:
            assert _w_dma(spec)
            return ws["h"][i]
        assert plan[i] == spec, (i, plan[i], spec)
        while ws["dma"] < min(len(plan), i + 1 + LA_D):
            if not _w_dma(plan[ws["dma"]]):
                break
        assert ws["dma"] > i, "weight slot ring exhausted (missing relw?)"
        return ws["h"][i]

    def rows_to_fm(src_rows, R, nch, dst_fn, wregs):
        _, stg, R_stg = take_stage()
        S.dma("pool", lambda e: e.dma_start(out=stg[0:R, 0:nch * 128], in_=src_rows), writes=[R_stg])
        g = max(1, min(256 // R, nch))
        for c0 in range(0, nch, g):
            n = min(g, nch - c0)
            pt, R_pt = misc_r.next()

            def f(e, c0=c0, n=n, pt=pt):
                ins = None
                for c in range(n):
                    ins = e.transpose(pt[:, c * R:(c + 1) * R],
                                      stg[0:R, (c0 + c) * 128:(c0 + c + 1) * 128], ident_f[0:R, 0:R])
                return ins
            S.op("pe", f, reads=[R_stg, R_cst], writes=[R_pt])
            S.op("dve", lambda e, c0=c0, n=n, pt=pt: e.tensor_copy(
                out=dst_fn(c0, n), in_=pt[:, 0:n * R].rearrange("p (c r) -> p c r", r=R)),
                reads=[R_pt], writes=wregs)

    def fm_to_rows(src_fn, R, nch, dst_rows, rregs):
        _, stg, R_stg = take_stage()
        for c0 in range(0, nch, 2):
            n = min(2, nch - c0)
            pt, R_pt = misc_r.next()

            def f(e, c0=c0, n=n, pt=pt):
                ins = None
                for c in range(n):
                    ins = e.transpose(pt[0:R, c * 128:(c + 1) * 128], src_fn(c0 + c), ident_f)
                return ins
            S.op("pe", f, reads=rregs + [R_cst], writes=[R_pt])
            S.op("dve", lambda e, c0=c0, n=n, pt=pt: e.tensor_copy(
                out=stg[0:R, c0 * 128:(c0 + n) * 128], in_=pt[0:R, 0:n * 128]),
                reads=[R_pt], writes=[R_stg])
        S.dma("pool", lambda e: e.dma_start(out=dst_rows, in_=stg[0:R, 0:nch * 128]), reads=[R_stg], writes=[OUT])

    def bfv(i):
        return cbs[i].bitcast(BF16)

    sm_r = Ring([(cbs[5][:, 0:1024], R_cb[5]), (cbs[6][:, 0:1024], R_cb[6])])
    pn_r = Ring([(bfv(7)[:, 0:1024], R_cb[7]), (bfv(8)[:, 0:1024], R_cb[8])])
    pT_r = Ring([(bfv(7)[:, 1024:2048].rearrange("p (g q) -> p g q", q=128), R_cb[7]),
                 (bfv(8)[:, 1024:2048].rearrange("p (g q) -> p g q", q=128), R_cb[8])])

    def blocks_of(ncols):
        bl = [(c, 256) for c in range(0, SBF, 256)]
        if ncols > SBF:
            bl.append((SBF, ncols - SBF))
        return bl

    def mm_group(out, lhs_fn, rhs_fn, nk, reads, writes):
        ops = [(lhs_fn(k), rhs_fn(k)) for k in range(nk)]

        def f(e):
            ins = None
            for k, (l, r) in enumerate(ops):
                ins = e.matmul(out, lhsT=l, rhs=r, start=(k == 0), stop=(k == nk - 1))
            return ins
        S.op("pe", f, reads=reads, writes=writes)

    def setup():
        S.dma("sp", lambda e: e.dma_start(out=cst[:], in_=consts[:, :]), writes=[R_cst])
        S.op("dve", lambda e: e.tensor_copy(out=ident_b[:], in_=ident_f), reads=[R_cst], writes=[R_idb])
        S.op("dve", lambda e: e.memset(ones_b[:], 1.0), writes=[R_onb])
        S.op("dve", lambda e: e.memset(ones_f[:], 1.0), writes=[R_onf])
        rows_to_fm(gains[:, :], 24, KC, lambda c0, n: gainT[:, c0:c0 + n, :], [R_gain])
        S.op("dve", lambda e: e.tensor_scalar(out=gpost[:], in0=gainT[:], scalar1=0.5, scalar2=None, op0=ALU.mult),
             reads=[R_gain], writes=[R_gpost])
        g4 = gainT[:].rearrange("p c (l i) -> p c l i", i=6)
        gp4 = gpost[:].rearrange("p c (l i) -> p c l i", i=6)
        S.op("dve", lambda e: e.tensor_copy(out=gp4[:, :, :, 3], in_=g4[:, :, :, 3]), reads=[R_gain], writes=[R_gpost])
        rows_to_fm(cw[:, :], 6, KC, lambda c0, n: cwT[:, c0:c0 + n, :], [R_cw])
        rows_to_fm(ccv[:, :], 4, KC, lambda c0, n: ccT[:, c0:c0 + n, :], [R_ccT])
        rows_to_fm(lbl[:, :], 2, 4, lambda c0, n: lbT[:, c0:c0 + n, :], [R_lbT])
        rows_to_fm(hng[:, :], 2, 4, lambda c0, n: hngT[:, c0:c0 + n, :], [R_hng])
        S.dma("pool", lambda e: e.dma_start(out=sinkc[:], in_=sinks[0, :].partition_broadcast(128)), writes=[R_sink])
        S.op("dve", lambda e: e.tensor_scalar(out=negsink[:], in0=sinkc[:], scalar1=-1.0, scalar2=None, op0=ALU.mult),
             reads=[R_sink], writes=[R_sink])
        S.op("dve", lambda e: e.tensor_scalar(out=mskb[:], in0=cst[:, 128:640], scalar1=1.0 / (64 ** -0.5), scalar2=None, op0=ALU.mult),
             reads=[R_cst], writes=[R_mskb])
        S.op("dve", lambda e: e.memset(lbv[:], 0.0), writes=[R_lbv])
        S.op("dve", lambda e: e.tensor_tensor(out=lbv[:, :, 1], in0=lbT[:, :, 1], in1=lbT[:, :, 0], op=ALU.subtract),
             reads=[R_lbT], writes=[R_lbv])
        S.op("act", lambda e: e.activation(out=lbv[:, :, 1], in_=lbv[:, :, 1], func=AF.Sigmoid), reads=[R_lbv], writes=[R_lbv])
        S.op("dve", lambda e: e.tensor_scalar(out=omlb[:], in0=lbv[:], scalar1=-1.0, scalar2=1.0, op0=ALU.mult, op1=ALU.add),
             reads=[R_lbv], writes=[R_omlb])
        for j in range(2):
            S.op("dve", lambda e, j=j: e.memset(kprev[j][0][:], 0.0), writes=[kprev[j][1]])
            S.op("dve", lambda e, j=j: e.memset(vprev[j][0][:], 0.0), writes=[vprev[j][1]])
            for h in range(4):
                S.op("dve", lambda e, j=j, h=h: e.memset(Sst[j][h][0][:], 0.0), writes=[Sst[j][h][1]])

    def load_x(sb):
        for t in range(SBF // 128):
            r0 = sb * SBF + t * 128
            rows_to_fm(xp[r0:r0 + 128, :], 128, KC,
                       lambda c0, n, t=t: xT[:, c0:c0 + n, t * 128:(t + 1) * 128], [R_x])
        if sb == 0:
            rows_to_fm(meta[:, :], NMETA, KC, lambda c0, n: xT[:, c0:c0 + n, META0:META0 + NMETA], [R_x])
            rows_to_fm(xs[:, :], NSAMP, KC, lambda c0, n: xT[:, c0:c0 + n, SAMP0:SAMP0 + NSAMP], [R_x])

    def store_y(sb):
        for t in range(SBF // 128):
            r0 = sb * SBF + t * 128
            fm_to_rows(lambda c, t=t: xT[:, c, t * 128:(t + 1) * 128], 128, KC, yp[r0:r0 + 128, :], [R_x])
        if sb == 0:
            fm_to_rows(lambda c: xT[:, c, SAMP0:SAMP0 + NSAMP], NSAMP, KC, ys[:, :], [R_x])

    def rstd_pieces(src3, src_regs, nchunks, c0, n, denom, out_cb):
        bi = c0 // 256
        rs = cbs[out_cb]
        st = {}

        def p1():
            S.op("act", lambda e: e.activation(out=sqb[:, 0:nchunks, 0:n], in_=src3, func=AF.Square), reads=src_regs, writes=[R_sqb])

        def p2():
            pt, R_pt = misc_r.next()
            mm_group(pt[:, 0:n], lambda k: ones_b[:], lambda k: sqb[:, k, 0:n], nchunks, [R_onb, R_sqb], [R_pt])
            S.op("act", lambda e: e.activation(out=rs[:, c0:c0 + n], in_=pt[:, 0:n], func=AF.Sqrt, bias=EPS, scale=1.0 / denom),
                 reads=[R_pt], writes=[R_cb[out_cb][bi]])
            S.op("dve", lambda e: e.reciprocal(out=rs[:, c0:c0 + n], in_=rs[:, c0:c0 + n]),
                 reads=[R_cb[out_cb][bi]], writes=[R_cb[out_cb][bi]])
        return [p1, p2]

    def prenorm_pieces(gi, c0, n):
        bi = c0 // 256
        ps = rstd_pieces(xT[:, :, c0:c0 + n], [R_x[bi]], KC, c0, n, float(D), 9)

        def hops(cs):
            for c in cs:
                S.op("dve", lambda e, c=c: e.scalar_tensor_tensor(
                    out=hT[:, c, c0:c0 + n], in0=xT[:, c, c0:c0 + n], scalar=gainT[:, c, gi:gi + 1], in1=cbs[9][:, c0:c0 + n],
                    op0=ALU.mult, op1=ALU.mult), reads=[R_x[bi], R_gain, R_cb[9][bi]], writes=[R_h[bi]])
        return ps + [lambda: hops(range(0, 4)), lambda: hops(range(4, 8))]

    def postnorm_pieces(gi, c0, n):
        bi = c0 // 256
        ps = rstd_pieces(yacc[:, 0:KC, c0:c0 + n], [R_cb[c][bi] for c in range(KC)], KC, c0, n, float(D), 9)

        def xops(cs):
            for c in cs:
                S.op("dve", lambda e, c=c: e.tensor_tensor(out=yacc[:, c, c0:c0 + n], in0=yacc[:, c, c0:c0 + n],
                                                           in1=cbs[9][:, c0:c0 + n], op=ALU.mult),
                     reads=[R_cb[c][bi], R_cb[9][bi]], writes=[R_cb[c][bi]])
                S.op("dve", lambda e, c=c: e.scalar_tensor_tensor(
                    out=xT[:, c, c0:c0 + n], in0=yacc[:, c, c0:c0 + n], scalar=gpost[:, c, gi:gi + 1], in1=xT[:, c, c0:c0 + n],
                    op0=ALU.mult, op1=ALU.add), reads=[R_cb[c][bi], R_gpost, R_x[bi]], writes=[R_x[bi]])
        return ps + [lambda: xops(range(0, 4)), lambda: xops(range(4, 8))]

    epi_q = []

    def pump(k=1):
        for _ in range(k):
            if not epi_q:
                return
            epi_q.pop(0)[1]()

    def need_blk(bi):
        while any(b_ == bi for (b_, _) in epi_q):
            epi_q.pop(0)[1]()

    def flush_epi():
        while epi_q:
            epi_q.pop(0)[1]()

    def prenorm_blk(gi, c0, n):
        for p in prenorm_pieces(gi, c0, n):
            p()

    def rstd_blk(src3, src_regs, nchunks, c0, n, denom, out_cb):
        for p in rstd_pieces(src3, src_regs, nchunks, c0, n, denom, out_cb):
            p()

    def prenorm(gi, ncols):
        for (c0, n) in blocks_of(ncols):
            prenorm_blk(gi, c0, n)

    def rstd_of(src_fn, src_regs, nchunks, ncols, denom, out_cb):
        for (c0, n) in blocks_of(ncols):
            rstd_blk(src_fn(c0, n), src_regs, nchunks, c0, n, denom, out_cb)

    Y3 = pY[:].rearrange("p (m n) -> p m n", n=256)
    R_Y = R_bank[0:4]

    def ffn(fi, ncols, on_block_final):
        blocks = blocks_of(ncols)
        r0 = 0
        while r0 < DFF:
            ncc = 2 if r0 == 0 else 4
            first = (r0 == 0)
            final = (r0 + 128 * ncc >= DFF)
            G, RG = getw(("km", "wg", fi, ((r0, 128 * ncc),)))
            U, RU = getw(("km", "wu", fi, ((r0, 128 * ncc),)))
            Dn, RD = getw(("rows", "wd", fi, r0, ncc))
            r0 += 128 * ncc
            steps = [(c0, n, cc) for (c0, n) in blocks for cc in range(ncc)]
            atiles = {}

            def down(hf, c0, n, ccs):
                ops = []
                for cc in ccs:
                    at, Ra = atiles[(c0, cc)]
                    for m in range(4 * hf, 4 * hf + 4):
                        ops.append((Y3[:, m, 0:n], Dn[:, cc, m * 128:(m + 1) * 128], at[:, 0:n],
                                    cc == 0 and m % 2 == 0, cc == ncc - 1))

                def f(e, ops=ops):
                    ins = None
                    for (o, l, r_, st, sp) in ops:
                        ins = e.matmul(o, lhsT=l, rhs=r_, start=st, stop=sp, skip_group_check=True)
                    return ins
                S.op("pe", f, reads=[RD] + [atiles[(c0, cc)][1] for cc in ccs], writes=R_Y[2 * hf:2 * hf + 2])
                if ccs[-1] == ncc - 1:
                    ysl = yacc[:, 4 * hf:4 * hf + 4, c0:c0 + n]
                    psl = Y3[:, 4 * hf:4 * hf + 4, 0:n]
                    yregs = [R_cb[c][c0 // 256] for c in range(4 * hf, 4 * hf + 4)]
                    if first:
                        S.op("dve", lambda e: e.tensor_copy(out=ysl, in_=psl), reads=R_Y[2 * hf:2 * hf + 2], writes=yregs)
                    else:
                        S.op("dve", lambda e: e.tensor_tensor(out=ysl, in0=psl, in1=ysl, op=ALU.add),
                             reads=R_Y[2 * hf:2 * hf + 2] + yregs, writes=yregs)

            def retire(step):
                c0, n, cc = step
                down(0, c0, n, [cc])
                if cc == ncc - 1:
                    down(1, c0, n, list(range(ncc)))

            prev = None
            fin_q = []
            for (c0, n, cc) in steps:
                need_blk(c0 // 256)
                pgu, Rpg = gu_r.next()
                pg, pu = pgu[:, 0:256], pgu[:, 256:512]
                mm_group(pg[:, 0:n], lambda k: G[:, k, cc * 128:(cc + 1) * 128],
                         lambda k: hT[:, k, c0:c0 + n], KC, [RG, R_h[c0 // 256]], [Rpg])
                mm_group(pu[:, 0:n], lambda k: U[:, k, cc * 128:(cc + 1) * 128],
                         lambda k: hT[:, k, c0:c0 + n], KC, [RU, R_h[c0 // 256]], [Rpg])
                if fin_q and cc == 1:
                    on_block_final(*fin_q.pop(0))
                pump(2)
                sgt, Rsg = sg_r.next()
                at, Ra = a_r.next()
                atiles[(c0, cc)] = (at, Ra)
                S.op("act", lambda e, sgt=sgt, pg=pg, n=n: e.activation(out=sgt[:, 0:n], in_=pg[:, 0:n], func=AF.Silu),
                     reads=[Rpg], writes=[Rsg])
                S.op("dve", lambda e, at=at, sgt=sgt, pu=pu, n=n: e.tensor_tensor(out=at[:, 0:n], in0=sgt[:, 0:n],
                                                                                   in1=pu[:, 0:n], op=ALU.mult),
                     reads=[Rsg, Rpg], writes=[Ra])
                if prev is not None:
                    retire(prev)
                    if final and prev[2] == ncc - 1:
                        fin_q.append((prev[0], prev[1]))
                prev = (c0, n, cc)
            retire(prev)
            relw(RG, RU, RD)
            if final:
                fin_q.append((prev[0], prev[1]))
                while fin_q:
                    on_block_final(*fin_q.pop(0))

    def proj_fm(Wt, RW, coff, blocks, evac):
        for (c0, n) in blocks:
            need_blk(c0 // 256)
            pt, R_pt = gu_r.next()
            mm_group(pt[:, 0:n], lambda k: Wt[:, k, coff:coff + 128], lambda k, c0=c0, n=n: hT[:, k, c0:c0 + n],
                     KC, [RW, R_h[c0 // 256]], [R_pt])
            evac(pt, R_pt, c0, n)
            pump(1)

    def proj_tm(Wt, RW, coff, c0, ntok):
        need_blk(c0 // 256)
        pt, R_pt = gu_r.next()
        mm_group(pt[0:ntok, 0:128], lambda k: hT[:, k, c0:c0 + ntok], lambda k: Wt[:, k, coff:coff + 128],
                 KC, [RW, R_h], [R_pt])
        return pt, R_pt

    def out_proj(specs, ncols, on_block_final):
        Ws = [getw(spec) for spec in specs]
        alt = 0
        pending = None
        for (c0, n) in blocks_of(ncols):
            for mp, (Wt, RW) in enumerate(Ws):
                for mm in range(2):
                    m = 2 * mp + mm
                    pt, R_pt = gu_r.next()
                    mm_group(pt[:, 0:n], lambda k: Wt[:, k, mm * 128:(mm + 1) * 128],
                             lambda k: zT[:, k, c0:c0 + n], KC, [RW] + R_z, [R_pt])
                    if alt % 2 == 0:
                        S.op("act", lambda e, pt=pt, m=m, c0=c0, n=n: e.activation(out=yacc[:, m, c0:c0 + n], in_=pt[:, 0:n],
                                                                                  func=AF.Copy),
                             reads=[R_pt], writes=[R_cb[m][c0 // 256]])
                    else:
                        S.op("dve", lambda e, pt=pt, m=m, c0=c0, n=n: e.tensor_copy(out=yacc[:, m, c0:c0 + n], in_=pt[:, 0:n]),
                             reads=[R_pt], writes=[R_cb[m][c0 // 256]])
                    alt += 1
                    pump(1)
                if mp == 0 and pending is not None:
                    on_block_final(*pending)
                    pending = None
            pending = (c0, n)
        relw(*[rw for (_, rw) in Ws])
        on_block_final(*pending)

    def tiles_of(sb):
        tl = [(t * 128, 128, t) for t in range(SBF // 128)]
        if sb == 0:
            tl += [(META0, NMETA, 8), (SAMP0, NSAMP, 9)]
        return tl

    SCALE = 64 ** -0.5

    S2_r = Ring([(pY[:, 0:1024], R_bank[0:2]), (pY[:, 1024:2048], R_bank[2:4])])

    def attn_parts(j, qa, R_qa, qc0, nq, key_tiles, key_regs, mask):
        nkt = sum(k[2] for k in key_tiles)
        st = {}

        def get_po():
            if "po" not in st:
                po, R_po = gu_r.next()
                st["po"] = (po.rearrange("p (z q) -> p z q", q=128), R_po)
            return st["po"]
        parts = []
        for half in range(2):
            parts.append(_attn_half(j, qa, R_qa, qc0, nq, key_tiles, key_regs, mask, nkt, half, get_po))

        def evac():
            po3, R_po = get_po()
            S.op("act", lambda e: e.activation(out=zT[:, 0:4, qc0:qc0 + nq], in_=po3[:, :, 0:nq], func=AF.Copy),
                 reads=[R_po], writes=R_z[0:4])
        return parts, evac

    def _attn_half(j, qa, R_qa, qc0, nq, key_tiles, key_regs, mask, nkt, half, get_po):
        hs = {}

        def partA():
            p0 = 64 * half
            s2, R_s2 = S2_r.next()
            s3 = s2.rearrange("p (z k) -> p z k", k=256)
            groups = []
            for zc in range(4):
                off = 0
                for ti, (kT, vt, nk) in enumerate(key_tiles):
                    groups.append((s3[0:nq, zc, off:off + nk], qa[zc][p0:p0 + 64, qc0:qc0 + nq], kT[p0:p0 + 64, 0:nk],
                                   zc % 2 == 0 and ti == 0, mask is None))
                    off += nk
                if mask is not None:
                    groups.append((s3[0:nq, zc, 0:nkt], ident_b[0:nq, 0:nq], mask[0:nq, 0:nkt], False, True))

            def fs(e, groups=groups):
                ins = None
                for (o, l, r_, st, sp) in groups:
                    ins = e.matmul(o, lhsT=l, rhs=r_, start=st, stop=sp, skip_group_check=True)
                return ins
            S.op("pe", fs, reads=R_qa + key_regs + [R_idb, R_mskb], writes=R_s2)
            sm, R_sm = sm_r.next()
            sm3 = sm.rearrange("p (z k) -> p z k", k=256)
            pn, R_pn = pn_r.next()
            pn3 = pn.rearrange("p (z k) -> p z k", k=256)
            pT, R_pT = pT_r.next()
            cl, R_cl = col_r.next()
            S.op("dve", lambda e, s3=s3, cl=cl: e.reduce_max(out=cl[0:nq, 0:4], in_=s3[0:nq, :, 0:nkt], axis=AX.X),
                 reads=R_s2, writes=[R_cl])
            nsk = negsink[0:nq, 8 * j + 4 * half:8 * j + 4 * half + 4]
            S.op("dve", lambda e, cl=cl, nsk=nsk: e.scalar_tensor_tensor(
                out=cl[0:nq, 4:8], in0=cl[0:nq, 0:4], scalar=-SCALE, in1=nsk, op0=ALU.mult, op1=ALU.min),
                reads=[R_cl, R_sink], writes=[R_cl])
            for zc in range(4):
                S.op("act", lambda e, s3=s3, sm3=sm3, cl=cl, zc=zc: e.activation(
                    out=sm3[0:nq, zc, 0:nkt], in_=s3[0:nq, zc, 0:nkt], func=AF.Exp, bias=cl[0:nq, 4 + zc:5 + zc], scale=SCALE),
                    reads=R_s2 + [R_cl], writes=[R_sm])
            S.op("dve", lambda e, cl=cl, nsk=nsk: e.tensor_tensor(out=cl[0:nq, 8:12], in0=cl[0:nq, 4:8], in1=nsk, op=ALU.subtract),
                 reads=[R_cl, R_sink], writes=[R_cl])
            S.op("act", lambda e, cl=cl: e.activation(out=cl[0:nq, 8:12], in_=cl[0:nq, 8:12], func=AF.Exp), reads=[R_cl], writes=[R_cl])
            S.op("dve", lambda e, sm3=sm3, cl=cl: e.reduce_sum(out=cl[0:nq, 12:16], in_=sm3[0:nq, :, 0:nkt], axis=AX.X),
                 reads=[R_sm], writes=[R_cl])
            S.op("dve", lambda e, cl=cl: e.tensor_tensor(out=cl[0:nq, 12:16], in0=cl[0:nq, 12:16], in1=cl[0:nq, 8:12], op=ALU.add),
                 reads=[R_cl], writes=[R_cl])
            S.op("dve", lambda e, cl=cl: e.reciprocal(out=cl[0:nq, 16:20], in_=cl[0:nq, 12:16]), reads=[R_cl], writes=[R_cl])
            S.op("dve", lambda e, sm3=sm3, pn3=pn3, cl=cl: e.tensor_tensor(
                out=pn3[0:nq, :, 0:nkt], in0=sm3[0:nq, :, 0:nkt], in1=cl[0:nq, 16:20].unsqueeze(2).to_broadcast([nq, 4, nkt]),
                op=ALU.mult), reads=[R_sm, R_cl], writes=[R_pn])
            hs["v"] = (pn3, R_pn, pT, R_pT)

        def partB():
            p0 = 64 * half
            pn3, R_pn, pT, R_pT = hs["v"]
            po3, R_po = get_po()
            pt, R_pt = gu_r.next()
            ptb = pt.bitcast(BF16).rearrange("p (g q) -> p g q", q=128)
            nkt_ = len(key_tiles)
            trs = []
            for zc in range(4):
                off = 0
                for ti, (kT, vt, nk) in enumerate(key_tiles):
                    trs.append((ptb[0:nk, zc * nkt_ + ti, 0:nq], pn3[0:nq, zc, off:off + nk]))
                    off += nk

            def ftr(e, trs=trs):
                ins = None
                for (o, i_) in trs:
                    ins = e.transpose(o, i_, ident_b[0:nq, 0:nq])
                return ins
            S.op("pe", ftr, reads=[R_pn, R_idb], writes=[R_pt])
            for ti, (kT, vt, nk) in enumerate(key_tiles):
                src_v = ptb[0:nk, ti:4 * nkt_:nkt_, 0:nq] if nkt_ > 1 else ptb[0:nk, 0:4, 0:nq]
                dst_v = pT[0:nk, ti:4 * nkt_:nkt_, 0:nq] if nkt_ > 1 else pT[0:nk, 0:4, 0:nq]
                S.op("act", lambda e, src_v=src_v, dst_v=dst_v: e.activation(out=dst_v, in_=src_v, func=AF.Copy),
                     reads=[R_pt], writes=[R_pT])
            pvs = []
            for zc in range(4):
                for ti, (kT, vt, nk) in enumerate(key_tiles):
                    pvs.append((po3[p0:p0 + 64, zc, 0:nq], vt[0:nk, p0:p0 + 64], pT[0:nk, zc * nkt_ + ti, 0:nq],
                                ti == 0, ti == nkt_ - 1))

            def fpv(e, pvs=pvs):
                ins = None
                for (o, l, r_, st, sp) in pvs:
                    ins = e.matmul(o, lhsT=l, rhs=r_, start=st, stop=sp, skip_group_check=True)
                return ins
            S.op("pe", fpv, reads=[R_pT] + key_regs, writes=[R_po])
            pump(1)
        return partA, partB

    def attn_pipeline(j, qa, R_qa, jobs):
        seq = []
        for job in jobs:
            parts, evac = attn_parts(j, qa, R_qa, *job)
            seq.append((parts[0], None))
            seq.append((parts[1], evac))
        prev = None
        for (pa, pb), evac in seq:
            pa()
            if prev is not None:
                prev[0]()
                if prev[1] is not None:
                    prev[1]()
            prev = (pb, evac)
        prev[0]()
        prev[1]()

    BOFF = {"fr": 0, "meta": SBF + 2, "samp": SBF + 2 + NMETA + 2}
    CTI = {"fr": 0, "meta": 16, "samp": 17}

    def even_mixer(j, sb, ncols, on_block_final):
        last = (sb == n_sb - 1)
        blocks = blocks_of(ncols)
        tiles = tiles_of(sb)
        wn = "wie"
        qa = [bfv(c) for c in range(4)]
        R_qa = [R_cb[c] for c in range(4)]
        kf, R_kf = bfv(4), R_cb[4]
        for half2 in range(2):
            z0 = 2 * half2
            W, RW = getw(("km", wn, j, ((64 * z0, 64), (256 + 64 * z0, 64), (64 * (z0 + 1), 64), (256 + 64 * (z0 + 1), 64))))
            for zi in range(2):
                zc = z0 + zi
                proj_fm(W, RW, 128 * zi, blocks, lambda pt, R_pt, c0, n, zc=zc: S.op(
                    "act", lambda e: e.activation(out=qa[zc][:, c0:c0 + n], in_=pt[:, 0:n], func=AF.Copy),
                    reads=[R_pt], writes=[R_qa[zc]]))
            relw(RW)
        W, RW = getw(("km", wn, j, ((512, 256),)))
        proj_fm(W, RW, 0, blocks, lambda pt, R_pt, c0, n: S.op(
            "act", lambda e: e.activation(out=kf[:, c0:c0 + n], in_=pt[:, 0:n], func=AF.Copy), reads=[R_pt], writes=[R_kf]))
        for (c0, ntok, ti) in tiles:
            pt, R_pt = proj_tm(W, RW, 128, c0, ntok)
            S.op("dve", lambda e, pt=pt, ntok=ntok, ti=ti: e.tensor_copy(out=vtm[0:ntok, ti, :], in_=pt[0:ntok, 0:128]),
                 reads=[R_pt], writes=[R_vtm])
            if last and ti == 7:
                S.op("act", lambda e, pt=pt: e.activation(out=kvf[:, 1, :], in_=pt[:, 0:128], func=AF.Copy),
                     reads=[R_pt], writes=[R_kvf])
                S.dma("pool", lambda e: e.dma_start(out=ovp[j], in_=kvf[:, 1, :]), reads=[R_kvf], writes=[OUT])
                pk, R_pk = proj_tm(W, RW, 0, c0, ntok)
                S.op("act", lambda e, pk=pk: e.activation(out=kvf[:, 0, :], in_=pk[:, 0:128], func=AF.Copy),
                     reads=[R_pk], writes=[R_kvf])
                S.dma("pool", lambda e: e.dma_start(out=okp[j], in_=kvf[:, 0, :]), reads=[R_kvf], writes=[OUT])
            if ti == 9:
                S.op("act", lambda e, pt=pt: e.activation(out=kvf[0:NSAMP, 1, :], in_=pt[0:NSAMP, 0:128], func=AF.Copy),
                     reads=[R_pt], writes=[R_kvf])
                S.dma("pool", lambda e: e.dma_start(out=ovs[j, 96:128, :], in_=kvf[0:NSAMP, 1, :]), reads=[R_kvf], writes=[OUT])
                pk, R_pk = proj_tm(W, RW, 0, c0, ntok)
                S.op("act", lambda e, pk=pk: e.activation(out=kvf[0:NSAMP, 0, :], in_=pk[0:NSAMP, 0:128], func=AF.Copy),
                     reads=[R_pk], writes=[R_kvf])
                S.dma("pool", lambda e: e.dma_start(out=oks[j, 96:128, :], in_=kvf[0:NSAMP, 0, :]), reads=[R_kvf], writes=[OUT])
                S.dma("pool", lambda e: e.dma_start(out=oks[j, 0:96, :], in_=ck[j, 32:128, :]), writes=[OUT])
                S.dma("pool", lambda e: e.dma_start(out=ovs[j, 0:96, :], in_=cv[j, 32:128, :]), writes=[OUT])
        relw(RW)
        kp, R_kp = kprev[j]
        vp, R_vp = vprev[j]
        jobs = []
        if sb == 0:
            jobs.append((META0, NMETA, [(kf[:, META0:META0 + NMETA], vtm[:, 8, :], NMETA)], [R_kf, R_vtm], None))
            S.op("dve", lambda e: e.tensor_copy(out=kp[:, 0:NMETA], in_=kf[:, META0:META0 + NMETA]), reads=[R_kf], writes=[R_kp])
            S.op("dve", lambda e: e.tensor_copy(out=vp[0:NMETA, :], in_=vtm[0:NMETA, 8, :]), reads=[R_vtm], writes=[R_vp])
        for t in range(SBF // 128):
            if t == 0:
                kts = [(kp[:], vp[:], 128), (kf[:, 0:128], vtm[:, 0, :], 128)]
                kregs = [R_kp, R_vp, R_kf, R_vtm]
            else:
                kts = [(kf[:, (t - 1) * 128:t * 128], vtm[:, t - 1, :], 128), (kf[:, t * 128:(t + 1) * 128], vtm[:, t, :], 128)]
                kregs = [R_kf, R_vtm]
            jobs.append((t * 128, 128, kts, kregs, mskb[:, 256:512] if (sb == 0 and t == 0) else mskb[:, 0:256]))
        if sb != 0:
            attn_pipeline(j, qa, R_qa, jobs)
        if not last and sb != 0:
            S.op("dve", lambda e: e.tensor_copy(out=kp[:], in_=kf[:, SBF - 128:SBF]), reads=[R_kf], writes=[R_kp])
            S.op("dve", lambda e: e.tensor_copy(out=vp[:], in_=vtm[:, 7, :]), reads=[R_vtm], writes=[R_vp])
        if sb == 0:
            _, stg, R_stg = take_stage()
            S.dma("pool", lambda e: e.dma_start(out=stg[:, 0:128], in_=ck[j]), writes=[R_stg])
            S.dma("pool", lambda e: e.dma_start(out=stg[:, 128:256], in_=cv[j]), writes=[R_stg])
            S.op("dve", lambda e: e.tensor_copy(out=cvb[:], in_=stg[:, 128:256]), reads=[R_stg], writes=[R_cvb])
            ckb, R_ckb = bfv(7)[:, 0:128], R_cb[7]
            S.op("dve", lambda e: e.tensor_copy(out=ckb, in_=stg[:, 0:128]), reads=[R_stg], writes=[R_ckb])
            pt, R_pt = gu_r.next()
            ptb = pt.bitcast(BF16)
            S.op("pe", lambda e: e.transpose(ptb[:, 0:128], ckb, ident_b[:]), reads=[R_ckb, R_idb], writes=[R_pt])
            S.op("act", lambda e: e.activation(out=ckT[:], in_=ptb[:, 0:128], func=AF.Copy), reads=[R_pt], writes=[R_ckT])
            jobs.append((SAMP0, NSAMP, [(ckT[:], cvb[:], 128), (kf[:, SAMP0:SAMP0 + NSAMP], vtm[:, 9, :], NSAMP)],
                         [R_ckT, R_cvb, R_kf, R_vtm], None))
            attn_pipeline(j, qa, R_qa, jobs)
            if not last:
                S.op("dve", lambda e: e.tensor_copy(out=kp[:], in_=kf[:, SBF - 128:SBF]), reads=[R_kf], writes=[R_kp])
                S.op("dve", lambda e: e.tensor_copy(out=vp[:], in_=vtm[:, 7, :]), reads=[R_vtm], writes=[R_vp])
        for h in range(4):
            hgrn_head(j, h, sb, ncols, blocks, tiles)
        out_proj([("woe", j, 256 * mp) for mp in range(4)], ncols, on_block_final)

    def hgrn_head(j, h, sb, ncols, blocks, tiles):
        last = (sb == n_sb - 1)
        qs, R_qs = bfv(0), R_cb[0]
        kin, R_kin = bfv(1), R_cb[1]
        gs, R_gs = bfv(2), R_cb[2]
        A, R_A = cbs[3], R_cb[3]
        Bx, R_B = cbs[4], R_cb[4]
        C, R_C = cbs[5], R_cb[5]
        oh, R_oh = cbs[6], R_cb[6]
        W1, RW1 = getw(("km", "wie", j, ((768 + 128 * h, 128), (1280 + 128 * h, 128))))
        W2, RW2 = getw(("km", "wie", j, ((1792 + 128 * h, 128), (2304 + 128 * h, 128))))
        proj_fm(W1, RW1, 0, blocks, lambda pt, R_pt, c0, n: S.op(
            "act", lambda e: e.activation(out=qs[:, c0:c0 + n], in_=pt[:, 0:n], func=AF.Silu), reads=[R_pt], writes=[R_qs]))
        proj_fm(W2, RW2, 128, blocks, lambda pt, R_pt, c0, n: S.op(
            "act", lambda e: e.activation(out=gs[:, c0:c0 + n], in_=pt[:, 0:n], func=AF.Silu), reads=[R_pt], writes=[R_gs]))
        proj_fm(W1, RW1, 128, blocks, lambda pt, R_pt, c0, n: S.op(
            "act", lambda e: e.activation(out=A[:, c0:c0 + n], in_=pt[:, 0:n], func=AF.Sigmoid), reads=[R_pt], writes=[R_A]))
        for (c0, ntok, ti) in tiles:
            pt, R_pt = proj_tm(W2, RW2, 0, c0, ntok)
            S.op("dve", lambda e, pt=pt, ntok=ntok, ti=ti: e.tensor_copy(out=vtm[0:ntok, ti, :], in_=pt[0:ntok, 0:128]),
                 reads=[R_pt], writes=[R_vtm])
        relw(RW1, RW2)
        S.op("dve", lambda e: e.tensor_scalar(out=A[:, 0:ncols], in0=A[:, 0:ncols], scalar1=omlb[:, h, j:j + 1],
                                              scalar2=lbv[:, h, j:j + 1], op0=ALU.mult, op1=ALU.add),
             reads=[R_A, R_omlb, R_lbv], writes=[R_A])
        S.op("dve", lambda e: e.tensor_scalar(out=kin[:, 0:ncols], in0=A[:, 0:ncols], scalar1=-1.0, scalar2=1.0,
                                              op0=ALU.mult, op1=ALU.add), reads=[R_A], writes=[R_kin])
        S.op("act", lambda e: e.activation(out=A[:, 0:ncols], in_=A[:, 0:ncols], func=AF.Ln), reads=[R_A], writes=[R_A])
        segs = [("fr", 0, SBF, 64)]
        if sb == 0:
            segs = [("meta", META0, NMETA, NMETA), ("fr", 0, SBF, 64), ("samp", SAMP0, NSAMP, NSAMP)]
        for (sn, col0, Ltot, Lc) in segs:
            o = BOFF[sn]
            nch = Ltot // Lc
            ci = CTI[sn]
            S.op("dve", lambda e, o=o: e.memset(Bx[:, o:o + 1], 0.0), writes=[R_B])
            S.op("dve", lambda e, o=o, col0=col0, Ltot=Ltot: e.tensor_tensor_scan(
                out=Bx[:, o + 1:o + 1 + Ltot], data0=ones_f[:, 0:1].to_broadcast([128, Ltot]), data1=A[:, col0:col0 + Ltot], initial=0.0,
                op0=ALU.mult, op1=ALU.add), reads=[R_onf, R_A, R_B], writes=[R_B])
            Bst = Bx[:, o:o + Ltot].rearrange("p (c l) -> p c l", l=Lc)[:, :, 0]
            Bin = Bx[:, o + 1:o + 1 + Ltot].rearrange("p (c l) -> p c l", l=Lc)
            Bmd = Bin[:, :, Lc // 2 - 1]
            Bls = Bin[:, :, Lc - 1]
            S.op("dve", lambda e, Bin=Bin, col0=col0, Ltot=Ltot, Lc=Lc, nch=nch: e.tensor_tensor(
                out=A[:, col0:col0 + Ltot].rearrange("p (c l) -> p c l", l=Lc), in0=Bin,
                in1=Bin[:, :, Lc // 2 - 1:Lc // 2].to_broadcast([128, nch, Lc]), op=ALU.subtract),
                reads=[R_B, R_A], writes=[R_A])
            S.op("dve", lambda e, Bmd=Bmd, Bst=Bst, ci=ci, nch=nch: e.tensor_tensor(
                out=ctab[:, 0, ci:ci + nch], in0=Bmd, in1=Bst, op=ALU.subtract), reads=[R_B], writes=[R_ctab])
            S.op("dve", lambda e, Bmd=Bmd, Bls=Bls, ci=ci, nch=nch: e.tensor_tensor(
                out=ctab[:, 1, ci:ci + nch], in0=Bls, in1=Bmd, op=ALU.subtract), reads=[R_B], writes=[R_ctab])
            S.op("dve", lambda e, Bst=Bst, Bls=Bls, ci=ci, nch=nch: e.tensor_tensor(
                out=ctab[:, 2, ci:ci + nch], in0=Bls, in1=Bst, op=ALU.subtract), reads=[R_B], writes=[R_ctab])
        nct = 18 if sb == 0 else 16
        S.op("act", lambda e: e.activation(out=ctab[:, :, 0:nct], in_=ctab[:, :, 0:nct], func=AF.Exp), reads=[R_ctab], writes=[R_ctab])
        S.op("act", lambda e: e.activation(out=C[:, 0:ncols], in_=A[:, 0:ncols], func=AF.Exp), reads=[R_A], writes=[R_C])
        S.op("dve", lambda e: e.tensor_tensor(out=qs[:, 0:ncols], in0=qs[:, 0:ncols], in1=C[:, 0:ncols], op=ALU.mult),
             reads=[R_qs, R_C], writes=[R_qs])
        S.op("act", lambda e: e.activation(out=C[:, 0:ncols], in_=A[:, 0:ncols], func=AF.Exp, scale=-1.0), reads=[R_A], writes=[R_C])
        S.op("dve", lambda e: e.tensor_tensor(out=kin[:, 0:ncols], in0=kin[:, 0:ncols], in1=C[:, 0:ncols], op=ALU.mult),
             reads=[R_kin, R_C], writes=[R_kin])
        if sb == 0:
            St, R_St = Ssm[h]
            S.dma("pool", lambda e: e.dma_start(out=St[:], in_=sh[j, h]), writes=[R_St])
        chunks = []
        for (sn, col0, Ltot, Lc) in segs:
            for c in range(Ltot // Lc):
                if sn == "fr":
                    chunks.append((sn, col0 + c * Lc, Lc, CTI[sn] + c, c // 2, 64 * (c % 2), (c // 2) * 128, 128, c == Ltot // Lc - 1))
                else:
                    chunks.append((sn, col0, Lc, CTI[sn], 8 if sn == "meta" else 9, 0, col0, Ltot, True))
        ktm_cur = {}

        def state_part(ch):
            sn, cs, Lc, ci, ti, p0, tcol0, ntile, seg_last = ch
            St, R_St = Ssm[h] if sn == "samp" else Sst[j][h]
            if p0 == 0:
                pt, R_pt = gu_r.next()
                ptb = pt.bitcast(BF16)
                ktm, R_ktm = ktm_r.next()
                ktm_cur["k"] = (ktm, R_ktm)
                S.op("pe", lambda e: e.transpose(ptb[0:ntile, 0:128], kin[:, tcol0:tcol0 + ntile], ident_b[:]),
                     reads=[R_kin, R_idb], writes=[R_pt])
                S.op("act", lambda e: e.activation(out=ktm[0:ntile, :], in_=ptb[0:ntile, 0:128], func=AF.Copy),
                     reads=[R_pt], writes=[R_ktm])
            ktm, R_ktm = ktm_cur["k"]
            Sb, R_Sb = Sb_r.next()
            S.op("dve", lambda e: e.tensor_scalar(out=Sb[:], in0=St[:], scalar1=ctab[:, 0, ci:ci + 1], scalar2=None, op0=ALU.mult),
                 reads=[R_St, R_ctab], writes=[R_Sb])
            pn_, R_pn_ = ya_r.next()
            mm_group(pn_[:, 0:128], lambda k: ktm[p0:p0 + Lc, :], lambda k: vtm[p0:p0 + Lc, ti, :], 1, [R_ktm, R_vtm], [R_pn_])
            tmp, R_tmp = tmp_r.next()
            S.op("act", lambda e: e.activation(out=tmp[:], in_=pn_[:, 0:128], func=AF.Copy, scale=ctab[:, 1, ci:ci + 1]),
                 reads=[R_pn_, R_ctab], writes=[R_tmp])
            S.op("dve", lambda e: e.scalar_tensor_tensor(out=St[:], in0=St[:], scalar=ctab[:, 2, ci:ci + 1], in1=tmp[:],
                                                         op0=ALU.mult, op1=ALU.add), reads=[R_St, R_ctab, R_tmp], writes=[R_St])
            if seg_last and sn == "samp":
                S.dma("pool", lambda e: e.dma_start(out=ohs[j, h], in_=St[:]), reads=[R_St], writes=[OUT])
            if seg_last and sn == "fr" and last:
                S.dma("pool", lambda e: e.dma_start(out=ohp[j, h], in_=St[:]), reads=[R_St], writes=[OUT])
            return (Sb, R_Sb)

        def out_part(ch, Sbt):
            sn, cs, Lc, ci, ti, p0, tcol0, ntile, seg_last = ch
            Sb, R_Sb = Sbt
            ps, R_s = yb_r.next()
            mm_group(ps[p0:p0 + Lc, 0:Lc], lambda k: kin[:, cs:cs + Lc], lambda k: qs[:, cs:cs + Lc], 1, [R_kin, R_qs], [R_s])
            pth, R_pth = pth_r.next()
            S.op("dve", lambda e: e.tensor_tensor(out=pth[p0:p0 + Lc, 0:Lc], in0=ps[p0:p0 + Lc, 0:Lc],
                                                  in1=mask_hg[p0:p0 + Lc, 0:Lc], op=ALU.mult), reads=[R_s, R_cst], writes=[R_pth])
            po, R_po = gu_r.next()

            def fo(e):
                e.matmul(po[:, 0:Lc], lhsT=Sb[:], rhs=qs[:, cs:cs + Lc], start=True, stop=False)
                return e.matmul(po[:, 0:Lc], lhsT=vtm[p0:p0 + Lc, ti, :], rhs=pth[p0:p0 + Lc, 0:Lc], start=False, stop=True)
            S.op("pe", fo, reads=[R_Sb, R_qs, R_vtm, R_pth], writes=[R_po])
            S.op("act", lambda e: e.activation(out=oh[:, cs:cs + Lc], in_=po[:, 0:Lc], func=AF.Copy), reads=[R_po], writes=[R_oh])

        prev = None
        for ch in chunks:
            sbt_ = state_part(ch)
            if prev is not None:
                out_part(*prev)
                pump(1)
            prev = (ch, sbt_)
        out_part(*prev)
        rstd_of(lambda c0, n: oh[:, c0:c0 + n].unsqueeze(1), [R_oh], 1, ncols, 128.0, 9)
        S.op("dve", lambda e: e.tensor_tensor(out=oh[:, 0:ncols], in0=oh[:, 0:ncols], in1=cbs[9][:, 0:ncols], op=ALU.mult),
             reads=[R_oh, R_cb[9]], writes=[R_oh])
        S.op("dve", lambda e: e.scalar_tensor_tensor(
            out=zT[:, 4 + h, 0:ncols], in0=oh[:, 0:ncols], scalar=hngT[:, h, j:j + 1], in1=gs[:, 0:ncols],
            op0=ALU.mult, op1=ALU.mult), reads=[R_oh, R_hng, R_gs], writes=[R_z[4 + h]])

    def odd_mixer(j, sb, ncols, on_block_final):
        last = (sb == n_sb - 1)
        blocks = blocks_of(ncols)
        uh, R_uh = uhalo[j]
        Wst = {}

        def bufs(jc):
            o = 4 * (jc % 2)
            return (bfv(o), R_cb[o]), (cbs[o + 1], R_cb[o + 1]), (cbs[o + 2], R_cb[o + 2]), (cbs[o + 3], R_cb[o + 3])

        def proj(jc):
            (bgb, R_bg), (cgf, R_cg), (uF, R_uF), (yv, R_yv) = bufs(jc)
            uaux, R_uaux = uaux2[jc % 2]
            W1, RW1 = getw(("km", "wio", j, ((128 * jc, 128), (D + 128 * jc, 128))))
            if jc % 2 == 0:
                Wst["w2"] = getw(("km", "wio", j, ((2 * D + 128 * jc, 256),)))
            W2, RW2 = Wst["w2"]
            proj_fm(W1, RW1, 0, blocks, lambda pt, R_pt, c0, n: S.op(
                "act", lambda e: e.activation(out=bgb[:, c0:c0 + n], in_=pt[:, 0:n], func=AF.Copy), reads=[R_pt], writes=[R_bg]))
            proj_fm(W1, RW1, 128, blocks, lambda pt, R_pt, c0, n: S.op(
                "act", lambda e: e.activation(out=cgf[:, c0:c0 + n], in_=pt[:, 0:n], func=AF.Copy), reads=[R_pt], writes=[R_cg]))

            def evac_u(pt, R_pt, c0, n):
                if c0 < SBF:
                    S.op("dve", lambda e: e.tensor_tensor(out=uF[:, 2 + c0:2 + c0 + n], in0=cgf[:, c0:c0 + n], in1=pt[:, 0:n],
                                                          op=ALU.mult), reads=[R_cg, R_pt], writes=[R_uF])
                else:
                    S.op("dve", lambda e: e.tensor_tensor(out=uaux[:, 2:2 + NMETA], in0=cgf[:, META0:META0 + NMETA],
                                                          in1=pt[:, 0:NMETA], op=ALU.mult), reads=[R_cg, R_pt], writes=[R_uaux])
                    S.op("dve", lambda e: e.tensor_tensor(out=uaux[:, 20:20 + NSAMP], in0=cgf[:, SAMP0:SAMP0 + NSAMP],
                                                          in1=pt[:, NMETA:NMETA + NSAMP], op=ALU.mult),
                         reads=[R_cg, R_pt], writes=[R_uaux])
            proj_fm(W2, RW2, 128 * (jc % 2), blocks, evac_u)
            relw(RW1)
            if jc % 2 == 1:
                relw(RW2)

        def conv(jc):
            (bgb, R_bg), (cgf, R_cg), (uF, R_uF), (yv, R_yv) = bufs(jc)
            uaux, R_uaux = uaux2[jc % 2]
            if sb == 0:
                S.op("dve", lambda e: e.memset(uaux[:, 0:2], 0.0), writes=[R_uaux])
                S.op("dve", lambda e: e.tensor_copy(out=uaux[:, 18:20], in_=ccT[:, jc, 2 * j:2 * j + 2]),
                     reads=[R_ccT], writes=[R_uaux])
                S.op("dve", lambda e: e.tensor_copy(out=uF[:, 0:2], in_=uaux[:, NMETA:NMETA + 2]), reads=[R_uaux], writes=[R_uF])
            else:
                S.op("dve", lambda e: e.tensor_copy(out=uF[:, 0:2], in_=uh[:, jc, :]), reads=[R_uh], writes=[R_uF])
            segs = [(uF, R_uF, 0, 0, SBF)]
            if sb == 0:
                segs += [(uaux, R_uaux, 0, META0, NMETA), (uaux, R_uaux, 18, SAMP0, NSAMP)]
            for (ub, R_ub, uo, yc0, N) in segs:
                w0 = cwT[:, jc, 3 * j + 0:3 * j + 1]
                w1 = cwT[:, jc, 3 * j + 1:3 * j + 2]
                w2 = cwT[:, jc, 3 * j + 2:3 * j + 3]
                S.op("dve", lambda e, ub=ub, uo=uo, yc0=yc0, N=N, w0=w0: e.tensor_scalar(
                    out=yv[:, yc0:yc0 + N], in0=ub[:, uo:uo + N], scalar1=w0, scalar2=None, op0=ALU.mult),
                    reads=[R_ub, R_cw], writes=[R_yv])
                S.op("dve", lambda e, ub=ub, uo=uo, yc0=yc0, N=N, w1=w1: e.scalar_tensor_tensor(
                    out=yv[:, yc0:yc0 + N], in0=ub[:, uo + 1:uo + 1 + N], scalar=w1, in1=yv[:, yc0:yc0 + N],
                    op0=ALU.mult, op1=ALU.add), reads=[R_ub, R_cw, R_yv], writes=[R_yv])
                S.op("dve", lambda e, ub=ub, uo=uo, yc0=yc0, N=N, w2=w2: e.scalar_tensor_tensor(
                    out=yv[:, yc0:yc0 + N], in0=ub[:, uo + 2:uo + 2 + N], scalar=w2, in1=yv[:, yc0:yc0 + N],
                    op0=ALU.mult, op1=ALU.add), reads=[R_ub, R_cw, R_yv], writes=[R_yv])
            S.op("dve", lambda e: e.tensor_tensor(out=zT[:, jc, 0:ncols], in0=bgb[:, 0:ncols], in1=yv[:, 0:ncols], op=ALU.mult),
                 reads=[R_bg, R_yv], writes=[R_z[jc]])
            S.op("dve", lambda e: e.tensor_copy(out=uh[:, jc, :], in_=uF[:, SBF:SBF + 2]), reads=[R_uF], writes=[R_uh])
            if sb == 0:
                S.op("dve", lambda e: e.tensor_copy(out=uSh[:, jc, :], in_=uaux[:, 18 + NSAMP:20 + NSAMP]),
                     reads=[R_uaux], writes=[R_uSh])

        proj(0)
        for jc in range(KC):
            if jc + 1 < KC:
                proj(jc + 1)
            conv(jc)
        if last:
            fm_to_rows(lambda c: uh[:, c, :], 2, KC, ocp[2 * j:2 * j + 2, :], [R_uh])
        if sb == 0:
            fm_to_rows(lambda c: uSh[:, c, :], 2, KC, ocs[2 * j:2 * j + 2, :], [R_uSh])
        out_proj([("km", "woo", j, ((256 * mp, 256),)) for mp in range(4)], ncols, on_block_final)

    MIXERS = {"even": even_mixer, "odd": odd_mixer}

    def main(max_stage=None):
        import os
        skip = os.environ.get("KSKIP", "")
        if "setup" in skip:
            S.dma("sp", lambda e: e.dma_start(out=cst[:], in_=consts[:, :]), writes=[R_cst])
        else:
            setup()
        if "io" in skip:
            return
        for sb in range(n_sb):
            ncols = NCM if sb == 0 else SBF
            load_x(sb)
            stages = [(l, s) for l in range(n_layers) for s in range(3)]
            if max_stage is not None:
                stages = stages[:max_stage]
            gidx = [l * 6 + 2 * s for (l, s) in stages]
            if stages:
                prenorm(gidx[0], ncols)
            for si, (l, s) in enumerate(stages):
                nxt = gidx[si + 1] if si + 1 < len(stages) else None

                import os
                late = ("late%d" % s) in os.environ.get("KSKIP", "")

                def epilogue(c0, n, g=gidx[si], nxt=nxt, late=late):
                    for p in postnorm_pieces(g + 1, c0, n):
                        epi_q.append((c0 // 256, p))
                    if nxt is not None and not late:
                        for p in prenorm_pieces(nxt, c0, n):
                            epi_q.append((c0 // 256, p))
                if s == 0:
                    ffn(2 * l, ncols, epilogue)
                elif s == 1:
                    MIXERS["even" if l % 2 == 0 else "odd"](l // 2, sb, ncols, epilogue)
                else:
                    ffn(2 * l + 1, ncols, epilogue)
                if late and nxt is not None:
                    flush_epi()
                    prenorm(nxt, ncols)
            flush_epi()
            store_y(sb)

    main(dbg_at)
    if plan is None:
        S.stack.close()
        return None, ws["rec"]
    S.finish([OUT])
    return S, ws["rec"]


IN_NAMES = ["xp", "xs", "ck", "cv", "sh", "ccv", "meta", "gains", "wg", "wu", "wd", "wie", "woe", "sinks", "lbl", "hng",
            "wio", "cw", "woo", "consts"]


def make_nc(**kw):
    nc0 = bass.Bass("TRN2", target_bir_lowering=False)
    _, plan = build_program(nc0, plan=None, **kw)
    nc = bass.Bass("TRN2", target_bir_lowering=False)
    S, rec = build_program(nc, plan=plan, **kw)
    assert rec == plan
    return nc, S


def core_inputs(inp, b, sbi, shared):
    m = dict(shared)
    m["xp"] = np.ascontiguousarray(inp["x_prompt"][b])
    m["xs"] = np.ascontiguousarray(inp["x_sample"][sbi])
    m["ck"] = np.ascontiguousarray(inp["cache_swa_k"][:, sbi]).reshape(2, 128, 128)
    m["cv"] = np.ascontiguousarray(inp["cache_swa_v"][:, sbi]).reshape(2, 128, 128)
    m["sh"] = np.ascontiguousarray(inp["state_hgrn"][:, sbi])
    m["ccv"] = np.ascontiguousarray(inp["cache_conv"][:, sbi]).reshape(4, D)
    return m


def shared_inputs(inp):
    f = lambda a: np.ascontiguousarray(np.asarray(a, dtype=np.float32))
    return {
        "meta": f(inp["meta_tokens"]), "gains": f(inp["norm_gains"]).reshape(24, D),
        "wg": f(inp["w_ffn_gate"]).reshape(8, D, DFF), "wu": f(inp["w_ffn_up"]).reshape(8, D, DFF),
        "wd": f(inp["w_ffn_down"]).reshape(8, DFF, D), "wie": f(inp["w_in_even"]), "woe": f(inp["w_out_even"]),
        "sinks": f(inp["attn_sinks"]).reshape(1, 16), "lbl": f(inp["hgrn_lb_logits"]),
        "hng": f(inp["hgrn_norm_gain"]).reshape(2, 512), "wio": f(inp["w_in_odd"]),
        "cw": f(inp["conv_w"]).reshape(6, D), "woo": f(inp["w_out_odd"]), "consts": host_consts(),
    }


_CACHE = {}


def kernel(**inputs):
    if "nc" not in _CACHE:
        _CACHE["nc"] = make_nc()[0]
    nc = _CACHE["nc"]
    inp = {k: np.asarray(v) for k, v in inputs.items()}
    sh = shared_inputs(inp)
    in_maps = [core_inputs(inp, c % 4, c, sh) for c in range(8)]
    res = run_bass_kernel_spmd(nc, in_maps, core_ids=list(range(8)))
    r = res.results
    f32 = np.float32
    y_prompt = np.stack([r[b]["yp"] for b in range(4)]).astype(f32)
    y_sample = np.stack([r[c]["ys"] for c in range(8)]).astype(f32)

    def gather(name, cores, shape):
        a = np.stack([r[c][name] for c in cores], axis=1)
        return np.ascontiguousarray(a.reshape(shape)).astype(f32)
    P, Q = range(4), range(8)
    swa_k_p = gather("okp", P, (2, 4, 128, 2, 64))
    swa_v_p = gather("ovp", P, (2, 4, 128, 2, 64))
    hgrn_p = gather("ohp", P, (2, 4, 4, 128, 128))
    conv_p = np.stack([r[c]["ocp"].reshape(2, 2, D) for c in P], axis=1).astype(f32)
    swa_k_s = gather("oks", Q, (2, 8, 128, 2, 64))
    swa_v_s = gather("ovs", Q, (2, 8, 128, 2, 64))
    hgrn_s = gather("ohs", Q, (2, 8, 4, 128, 128))
    conv_s = np.stack([r[c]["ocs"].reshape(2, 2, D) for c in Q], axis=1).astype(f32)
    return (y_prompt, y_sample, swa_k_p, swa_v_p, hgrn_p, conv_p, swa_k_s, swa_v_s, hgrn_s, conv_s)
```

```python
import numpy as np
from contextlib import ExitStack
import concourse.bass as bass
import concourse.mybir as mybir
from concourse.bass_utils import run_bass_kernel_spmd

F32 = mybir.dt.float32
BF16 = mybir.dt.bfloat16
AF = mybir.ActivationFunctionType
ALU = mybir.AluOpType
AX = mybir.AxisListType


class Reg:
    __slots__ = ("name", "last_w", "readers", "multi", "writers", "excl")

    def __init__(self, name, multi=False, excl=False):
        self.name = name
        self.last_w = None
        self.readers = []
        self.multi = multi
        self.writers = []
        self.excl = excl


class _Op:
    __slots__ = ("eng", "fn", "deps", "is_dma", "needs_inc", "sem", "val", "lane_prev", "idx")


EPOCH = 30000
NLANES = 32


class Sched:
    def __init__(self, nc):
        self.nc = nc
        self.stack = ExitStack()
        self.ops = []
        self.nsem = 0

    def sb(self, name, shape, dtype):
        return self.stack.enter_context(self.nc.sbuf_tensor(name, shape, dtype))

    def ps(self, name, shape, dtype):
        return self.stack.enter_context(self.nc.psum_tensor(name, shape, dtype))

    def _sem(self, name):
        self.nsem += 1
        return self.stack.enter_context(self.nc.semaphore(name))

    @staticmethod
    def _flat(regs):
        out = []
        for r in regs:
            if isinstance(r, (list, tuple)):
                out.extend(Sched._flat(r))
            else:
                out.append(r)
        return out

    def _add(self, eng, fn, reads, writes, is_dma):
        reads = self._flat(reads)
        writes = self._flat(writes)
        op = _Op()
        op.eng = eng
        op.fn = fn
        op.is_dma = is_dma
        op.needs_inc = False
        op.sem = None
        op.val = 0
        op.lane_prev = None
        op.idx = len(self.ops)
        deps = {}
        for r in reads:
            if r.multi:
                for w in r.writers:
                    deps[w.idx] = w
            else:
                if r.last_w is not None:
                    deps[r.last_w.idx] = r.last_w
                if r.excl:
                    for rd in r.readers:
                        if rd.eng != eng:
                            deps[rd.idx] = rd
        for w in writes:
            if w.multi:
                continue
            if w.last_w is not None:
                deps[w.last_w.idx] = w.last_w
            for rd in w.readers:
                deps[rd.idx] = rd
        for r in reads:
            if not r.multi:
                r.readers.append(op)
        for w in writes:
            if w.multi:
                w.writers.append(op)
            else:
                w.last_w = op
                w.readers = []
        final = []
        for i in sorted(deps):
            d = deps[i]
            if d is op:
                continue
            if (not is_dma) and (not d.is_dma) and d.eng == "pe" and eng == "pe":
                continue
            d.needs_inc = True
            final.append(d)
        op.deps = final
        self.ops.append(op)
        return op

    def op(self, eng, fn, reads=(), writes=()):
        return self._add(eng, fn, reads, writes, False)

    def dma(self, eng, fn, reads=(), writes=()):
        return self._add(eng, fn, reads, writes, True)

    def finish(self, outputs):
        self._add("sp", None, outputs, (), False)
        cnt = {}
        engsems = {}
        lanes = {}
        lane_cnt = {}
        lane_last = {}
        ndma_e = {}
        ndma = 0
        for op in self.ops:
            if op.is_dma:
                nd = ndma_e.get(op.eng, 0)
                ndma_e[op.eng] = nd + 1
                ndma += 1
                ln = (op.eng, nd % NLANES)
                n = lane_cnt.get(ln, 0)
                ep = n // (EPOCH // 16)
                lst = lanes.setdefault(ln, [])
                while len(lst) <= ep:
                    lst.append(self._sem("ln%s%d_%d" % (op.eng, ln[1], len(lst))))
                op.sem = lst[ep]
                op.val = (n - ep * (EPOCH // 16) + 1) * 16
                op.lane_prev = lane_last.get(ln)
                lane_last[ln] = op
                lane_cnt[ln] = n + 1
            elif op.needs_inc:
                n = cnt.get(op.eng, 0)
                ep = n // EPOCH
                lst = engsems.setdefault(op.eng, [])
                while len(lst) <= ep:
                    lst.append(self._sem("%s_%d" % (op.eng, len(lst))))
                op.sem = lst[ep]
                op.val = n - ep * EPOCH + 1
                cnt[op.eng] = n + 1
        by_eng = {}
        for op in self.ops:
            by_eng.setdefault(op.eng, []).append(op)
        self.stats = {k: len(v) for k, v in by_eng.items()}
        self.stats["ndma"] = ndma
        self.stats["nsem"] = self.nsem

        def emit(name, e):
            known = {}
            for op in by_eng.get(name, []):
                ws = list(op.deps)
                if op.lane_prev is not None:
                    ws.append(op.lane_prev)
                for d in ws:
                    k = id(d.sem)
                    if known.get(k, 0) < d.val:
                        e.wait_ge(d.sem, d.val)
                        known[k] = d.val
                if op.fn is not None:
                    ins = op.fn(e)
                    if op.is_dma:
                        ins.then_inc(op.sem, 16)
                    elif op.needs_inc:
                        ins.then_inc(op.sem, 1)

        with self.nc.Block() as block:
            @block.tensor
            def _(e):
                emit("pe", e)

            @block.scalar
            def _(e):
                emit("act", e)

            @block.vector
            def _(e):
                emit("dve", e)

            @block.gpsimd
            def _(e):
                emit("pool", e)

            @block.sync
            def _(e):
                emit("sp", e)
        self.stack.close()


D = 1024
KC = 8
DFF = 2816
SEQ = 4096
NMETA = 16
NSAMP = 32
SBF = 1024
NSB = SEQ // SBF
NCM = SBF + NMETA + NSAMP
CB = 1080
META0 = SBF
SAMP0 = SBF + NMETA
EPS = 1e-6
NEG = -30000.0
NCONST = 128 + 256 + 256 + 64


def host_consts():
    c = np.zeros((128, NCONST), np.float32)
    c[:, 0:128] = np.eye(128, dtype=np.float32)
    q = np.arange(128)[:, None]
    k = np.arange(256)[None, :]
    gen = np.where(q < 64, k < 192, k >= 64)
    c[:, 128:384] = np.where(gen, 0.0, NEG)
    first = np.where(k < 16, True, np.where(k < 128, False, np.where(q < 64, k < 192, True)))
    c[:, 384:640] = np.where(first, 0.0, NEG)
    s = (np.arange(128) % 64)[:, None]
    t = np.arange(64)[None, :]
    c[:, 640:704] = (s <= t).astype(np.float32)
    return c


class Ring:
    def __init__(self, items):
        self.items = items
        self.i = 0

    def next(self):
        it = self.items[self.i % len(self.items)]
        self.i += 1
        return it


LA_D, LA_C = 3, 3


def build_program(nc, plan=None, n_sb=NSB, n_layers=4, dbg_at=None):
    S = Sched(nc)
    T = {}

    def din(name, shape):
        T[name] = nc.dram_tensor(name, list(shape), F32, kind="ExternalInput").ap()
        return T[name]

    def dout(name, shape):
        T[name] = nc.dram_tensor(name, list(shape), F32, kind="ExternalOutput").ap()
        return T[name]

    xp = din("xp", [SEQ, D]); xs = din("xs", [NSAMP, D])
    ck = din("ck", [2, 128, 128]); cv = din("cv", [2, 128, 128])
    sh = din("sh", [2, 4, 128, 128]); ccv = din("ccv", [4, D])
    meta = din("meta", [NMETA, D]); gains = din("gains", [24, D])
    din("wg", [8, D, DFF]); din("wu", [8, D, DFF]); din("wd", [8, DFF, D])
    din("wie", [2, D, DFF]); din("woe", [2, D, D])
    sinks = din("sinks", [1, 16]); lbl = din("lbl", [2, 512]); hng = din("hng", [2, 512])
    din("wio", [2, D, 3 * D]); cw = din("cw", [6, D]); din("woo", [2, D, D])
    consts = din("consts", [128, NCONST])
    yp = dout("yp", [SEQ, D]); ys = dout("ys", [NSAMP, D])
    okp = dout("okp", [2, 128, 128]); ovp = dout("ovp", [2, 128, 128])
    ohp = dout("ohp", [2, 4, 128, 128]); ocp = dout("ocp", [4, D])
    oks = dout("oks", [2, 128, 128]); ovs = dout("ovs", [2, 128, 128])
    ohs = dout("ohs", [2, 4, 128, 128]); ocs = dout("ocs", [4, D])
    OUT = Reg("outputs", multi=True)
    dbgx = dout("dbgx", [128, KC * NCM]) if dbg_at is not None else None

    def sbt(name, shape, dt=F32):
        return S.sb(name, shape, dt), Reg(name)

    NBLK = 5
    xT = S.sb("xT", [128, KC, NCM], F32)
    hT = S.sb("hT", [128, KC, NCM], BF16)
    R_x = [Reg("x%d" % b) for b in range(NBLK)]
    R_h = [Reg("h%d" % b) for b in range(NBLK)]
    zT = S.sb("zT", [128, KC, NCM], BF16)
    R_z = [Reg("z%d" % c) for c in range(KC)]
    NCB = KC + 3
    yacc = S.sb("yacc", [128, KC, CB], F32)
    cbs = [yacc[:, i, :] for i in range(KC)] + [S.sb("cb%d" % i, [128, CB], F32)[:] for i in range(KC, NCB)]
    sqb, R_sqb = sbt("sqb", [128, KC, 256], BF16)
    R_cb = [[Reg("cb%d_%d" % (i, b)) for b in range(NBLK)] for i in range(NCB)]
    NSLOT, NSTG = 6, 2
    wsl = [S.sb("wsl%d" % i, [128, 4096], BF16) for i in range(NSLOT)]
    R_wsl = [Reg("wsl%d" % i) for i in range(NSLOT)]
    wst = [S.sb("wst%d" % i, [128, 1024], F32) for i in range(NSTG)]
    R_wst = [Reg("wst%d" % i) for i in range(NSTG)]
    cst, R_cst = sbt("cst", [128, NCONST])
    ident_f = cst[:, 0:128]
    mask_gen = cst[:, 128:384]
    mask_first = cst[:, 384:640]
    mask_hg = cst[:, 640:704]
    ident_b, R_idb = sbt("ident_b", [128, 128], BF16)
    ones_b, R_onb = sbt("ones_b", [128, 128], BF16)
    ones_f, R_onf = sbt("ones_f", [128, 1], F32)
    gainT, R_gain = sbt("gainT", [128, KC, 24])
    gpost, R_gpost = sbt("gpost", [128, KC, 24])
    cwT, R_cw = sbt("cwT", [128, KC, 6])
    sinkc, R_sink = sbt("sinkc", [128, 16])
    negsink = S.sb("negsink", [128, 16], F32)
    mskb, R_mskb = sbt("mskb", [128, 512], BF16)
    lbT, R_lbT = sbt("lbT", [128, 4, 2])
    lbv, R_lbv = sbt("lbv", [128, 4, 2])
    omlb, R_omlb = sbt("omlb", [128, 4, 2])
    hngT, R_hng = sbt("hngT", [128, 4, 2])
    kprev = [sbt("kprev%d" % j, [128, 128], BF16) for j in range(2)]
    vprev = [sbt("vprev%d" % j, [128, 128], BF16) for j in range(2)]
    vtm, R_vtm = sbt("vtm", [128, 10, 128], BF16)
    vtmB, R_vtmB = sbt("vtmB", [128, 10, 128], BF16)
    kvf, R_kvf = sbt("kvf", [128, 2, 128], F32)
    ckT, R_ckT = sbt("ckT", [128, 128], BF16)
    cvb, R_cvb = sbt("cvb", [128, 128], BF16)
    col_r = Ring([sbt("col%d" % i, [128, 20]) for i in range(2)])
    Sst = [[sbt("S%d_%d" % (j, h), [128, 128]) for h in range(4)] for j in range(2)]
    Ssm = [sbt("Ss%d" % h, [128, 128]) for h in range(4)]
    Sb_r = Ring([sbt("Sb%d" % i, [128, 128], BF16) for i in range(6)])
    tmp_r = Ring([sbt("tmpS%d" % i, [128, 128]) for i in range(4)])
    ktm_r = Ring([sbt("ktm%d" % i, [128, 128], BF16) for i in range(4)])
    pth_r = Ring([sbt("pth%d" % i, [128, 64], BF16) for i in range(6)])
    ctabs = [sbt("ctab%d" % i, [128, 3, 20]) for i in range(2)]
    uhalo = [sbt("uhalo%d" % j, [128, KC, 2]) for j in range(2)]
    uSh, R_uSh = sbt("uSh", [128, KC, 2])
    ccT, R_ccT = sbt("ccT", [128, KC, 4])
    uaux2 = [sbt("uaux%d" % i, [128, 64]) for i in range(2)]
    sg_r = Ring([sbt("sg%d" % i, [128, 256]) for i in range(3)])
    a_r = Ring([sbt("a%d" % i, [128, 256], BF16) for i in range(8)])

    pY = S.ps("pY", [128, 2048], F32)
    pX = S.ps("pX", [128, 2048], F32)
    bank = [pY[:, i * 512:(i + 1) * 512] for i in range(4)] + [pX[:, i * 512:(i + 1) * 512] for i in range(4)]
    R_bank = [Reg("bank%d" % b, excl=True) for b in range(8)]
    gu_r = Ring([(bank[b], R_bank[b]) for b in range(4, 8)])
    misc_r = gu_r
    ya_r = Ring([(bank[b], R_bank[b]) for b in range(0, 2)])
    yb_r = Ring([(bank[b], R_bank[b]) for b in range(2, 4)])
    stg_r = Ring(list(zip(wst, R_wst)))
    slot_r = Ring(list(zip(wsl, R_wsl)))

    def pieces_of(spec):
        kind = spec[0]
        if kind == "km":
            _, tn, idx, cols = spec
            wm = T[tn][idx].rearrange("(k p) c -> p k c", p=128)
            out, off = [], 0
            for c0, w in cols:
                out.append((lambda st3, off=off, w=w: st3[:, :, off:off + w], wm[:, :, c0:c0 + w]))
                off += w
            return out, 8, off
        if kind == "rows":
            _, tn, idx, r0, ncc = spec
            src = T[tn][idx][r0:r0 + 128 * ncc, :].rearrange("(c p) m -> p c m", p=128)
            return [(lambda st3: st3[:, :, :], src)], ncc, 1024
        if kind == "woe":
            _, j, m0 = spec
            wm = T["woe"][j]
            out = []
            for q in range(4):
                out.append((lambda st3, q=q: st3[0:64, q, :], wm[64 * q:64 * q + 64, m0:m0 + 256]))
                out.append((lambda st3, q=q: st3[64:128, q, :], wm[256 + 64 * q:256 + 64 * q + 64, m0:m0 + 256]))
            out.append((lambda st3: st3[:, 4:8, :],
                        wm[512:1024, :].rearrange("(k p) c -> p k c", p=128)[:, :, m0:m0 + 256]))
            return out, 8, 256
        raise ValueError(kind)

    ws = {"rec": [], "dma": 0, "cast": 0, "h": [], "st": [], "pend": {}}

    def take_stage():
        idx = stg_r.i % NSTG
        stg, R_stg = stg_r.next()
        return idx, stg, R_stg

    ws["busy"] = [False] * NSLOT
    ws["nslot"] = 0

    def _w_dma(spec):
        si = ws["nslot"] % NSLOT
        if ws["busy"][si]:
            return False
        ws["nslot"] += 1
        ws["busy"][si] = True
        pieces, k, w = pieces_of(spec)
        slot, R_slot = wsl[si], R_wsl[si]
        sl3 = slot[:, 0:k * w].rearrange("p (k w) -> p k w", w=w)
        for dst_fn, src in pieces:
            S.dma("pool", lambda e, d=dst_fn(sl3), s=src: e.dma_start(out=d, in_=s), writes=[R_slot])
        ws["h"].append((sl3, R_slot))
        ws["dma"] += 1
        return True

    def relw(*regs):
        for r in regs:
            ws["busy"][R_wsl.index(r)] = False

    def getw(spec):
        i = len(ws["rec"])
        ws["rec"].append(spec)
        if plan is None:
            assert _w_dma(spec)
            return ws["h"][i]
        assert plan[i] == spec, (i, plan[i], spec)
        while ws["dma"] < min(len(plan), i + 1 + LA_D):
            if not _w_dma(plan[ws["dma"]]):
                break
        assert ws["dma"] > i, "weight slot ring exhausted (missing relw?)"
        return ws["h"][i]

    def rows_to_fm(src_rows, R, nch, dst_fn, wregs):
        _, stg, R_stg = take_stage()
        S.dma("pool", lambda e: e.dma_start(out=stg[0:R, 0:nch * 128], in_=src_rows), writes=[R_stg])
        g = max(1, min(256 // R, nch))
        for c0 in range(0, nch, g):
            n = min(g, nch - c0)
            pt, R_pt = misc_r.next()

            def f(e, c0=c0, n=n, pt=pt):
                ins = None
                for c in range(n):
                    ins = e.transpose(pt[:, c * R:(c + 1) * R],
                                      stg[0:R, (c0 + c) * 128:(c0 + c + 1) * 128], ident_f[0:R, 0:R])
                return ins
            S.op("pe", f, reads=[R_stg, R_cst], writes=[R_pt])
            S.op("dve", lambda e, c0=c0, n=n, pt=pt: e.tensor_copy(
                out=dst_fn(c0, n), in_=pt[:, 0:n * R].rearrange("p (c r) -> p c r", r=R)),
                reads=[R_pt], writes=wregs)

    def fm_to_rows(src_fn, R, nch, dst_rows, rregs):
        _, stg, R_stg = take_stage()
        for c0 in range(0, nch, 2):
            n = min(2, nch - c0)
            pt, R_pt = misc_r.next()

            def f(e, c0=c0, n=n, pt=pt):
                ins = None
                for c in range(n):
                    ins = e.transpose(pt[0:R, c * 128:(c + 1) * 128], src_fn(c0 + c), ident_f)
                return ins
            S.op("pe", f, reads=rregs + [R_cst], writes=[R_pt])
            S.op("dve", lambda e, c0=c0, n=n, pt=pt: e.tensor_copy(
                out=stg[0:R, c0 * 128:(c0 + n) * 128], in_=pt[0:R, 0:n * 128]),
                reads=[R_pt], writes=[R_stg])
        S.dma("pool", lambda e: e.dma_start(out=dst_rows, in_=stg[0:R, 0:nch * 128]), reads=[R_stg], writes=[OUT])

    def bfv(i):
        return cbs[i].bitcast(BF16)

    sm_r = Ring([(cbs[5][:, 0:1024], R_cb[5]), (cbs[6][:, 0:1024], R_cb[6])])
    pn_r = Ring([(bfv(7)[:, 0:1024], R_cb[7]), (bfv(8)[:, 0:1024], R_cb[8])])
    pT_r = Ring([(bfv(7)[:, 1024:2048].rearrange("p (g q) -> p g q", q=128), R_cb[7]),
                 (bfv(8)[:, 1024:2048].rearrange("p (g q) -> p g q", q=128), R_cb[8])])

    def blocks_of(ncols):
        bl = [(c, 256) for c in range(0, SBF, 256)]
        if ncols > SBF:
            bl.append((SBF, ncols - SBF))
        return bl

    def mm_group(out, lhs_fn, rhs_fn, nk, reads, writes):
        ops = [(lhs_fn(k), rhs_fn(k)) for k in range(nk)]

        def f(e):
            ins = None
            for k, (l, r) in enumerate(ops):
                ins = e.matmul(out, lhsT=l, rhs=r, start=(k == 0), stop=(k == nk - 1))
            return ins
        S.op("pe", f, reads=reads, writes=writes)

    def setup():
        S.dma("sp", lambda e: e.dma_start(out=cst[:], in_=consts[:, :]), writes=[R_cst])
        S.op("dve", lambda e: e.tensor_copy(out=ident_b[:], in_=ident_f), reads=[R_cst], writes=[R_idb])
        S.op("dve", lambda e: e.memset(ones_b[:], 1.0), writes=[R_onb])
        S.op("dve", lambda e: e.memset(ones_f[:], 1.0), writes=[R_onf])
        rows_to_fm(gains[:, :], 24, KC, lambda c0, n: gainT[:, c0:c0 + n, :], [R_gain])
        S.op("dve", lambda e: e.tensor_scalar(out=gpost[:], in0=gainT[:], scalar1=0.5, scalar2=None, op0=ALU.mult),
             reads=[R_gain], writes=[R_gpost])
        g4 = gainT[:].rearrange("p c (l i) -> p c l i", i=6)
        gp4 = gpost[:].rearrange("p c (l i) -> p c l i", i=6)
        S.op("dve", lambda e: e.tensor_copy(out=gp4[:, :, :, 3], in_=g4[:, :, :, 3]), reads=[R_gain], writes=[R_gpost])
        rows_to_fm(cw[:, :], 6, KC, lambda c0, n: cwT[:, c0:c0 + n, :], [R_cw])
        rows_to_fm(ccv[:, :], 4, KC, lambda c0, n: ccT[:, c0:c0 + n, :], [R_ccT])
        rows_to_fm(lbl[:, :], 2, 4, lambda c0, n: lbT[:, c0:c0 + n, :], [R_lbT])
        rows_to_fm(hng[:, :], 2, 4, lambda c0, n: hngT[:, c0:c0 + n, :], [R_hng])
        S.dma("pool", lambda e: e.dma_start(out=sinkc[:], in_=sinks[0, :].partition_broadcast(128)), writes=[R_sink])
        S.op("dve", lambda e: e.tensor_scalar(out=negsink[:], in0=sinkc[:], scalar1=-1.0, scalar2=None, op0=ALU.mult),
             reads=[R_sink], writes=[R_sink])
        S.op("dve", lambda e: e.tensor_scalar(out=mskb[:], in0=cst[:, 128:640], scalar1=1.0 / (64 ** -0.5), scalar2=None, op0=ALU.mult),
             reads=[R_cst], writes=[R_mskb])
        S.op("dve", lambda e: e.memset(lbv[:], 0.0), writes=[R_lbv])
        S.op("dve", lambda e: e.tensor_tensor(out=lbv[:, :, 1], in0=lbT[:, :, 1], in1=lbT[:, :, 0], op=ALU.subtract),
             reads=[R_lbT], writes=[R_lbv])
        S.op("act", lambda e: e.activation(out=lbv[:, :, 1], in_=lbv[:, :, 1], func=AF.Sigmoid), reads=[R_lbv], writes=[R_lbv])
        S.op("dve", lambda e: e.tensor_scalar(out=omlb[:], in0=lbv[:], scalar1=-1.0, scalar2=1.0, op0=ALU.mult, op1=ALU.add),
             reads=[R_lbv], writes=[R_omlb])
        for j in range(2):
            S.op("dve", lambda e, j=j: e.memset(kprev[j][0][:], 0.0), writes=[kprev[j][1]])
            S.op("dve", lambda e, j=j: e.memset(vprev[j][0][:], 0.0), writes=[vprev[j][1]])
            for h in range(4):
                S.op("dve", lambda e, j=j, h=h: e.memset(Sst[j][h][0][:], 0.0), writes=[Sst[j][h][1]])

    def load_x(sb):
        for t in range(SBF // 128):
            r0 = sb * SBF + t * 128
            rows_to_fm(xp[r0:r0 + 128, :], 128, KC,
                       lambda c0, n, t=t: xT[:, c0:c0 + n, t * 128:(t + 1) * 128], [R_x])
        if sb == 0:
            rows_to_fm(meta[:, :], NMETA, KC, lambda c0, n: xT[:, c0:c0 + n, META0:META0 + NMETA], [R_x])
            rows_to_fm(xs[:, :], NSAMP, KC, lambda c0, n: xT[:, c0:c0 + n, SAMP0:SAMP0 + NSAMP], [R_x])

    def store_y(sb):
        for t in range(SBF // 128):
            r0 = sb * SBF + t * 128
            fm_to_rows(lambda c, t=t: xT[:, c, t * 128:(t + 1) * 128], 128, KC, yp[r0:r0 + 128, :], [R_x])
        if sb == 0:
            fm_to_rows(lambda c: xT[:, c, SAMP0:SAMP0 + NSAMP], NSAMP, KC, ys[:, :], [R_x])

    def rstd_pieces(src3, src_regs, nchunks, c0, n, denom, out_cb):
        bi = c0 // 256
        rs = cbs[out_cb]
        st = {}

        def p1():
            S.op("act", lambda e: e.activation(out=sqb[:, 0:nchunks, 0:n], in_=src3, func=AF.Square), reads=src_regs, writes=[R_sqb])

        def p2():
            pt, R_pt = misc_r.next()
            mm_group(pt[:, 0:n], lambda k: ones_b[:], lambda k: sqb[:, k, 0:n], nchunks, [R_onb, R_sqb], [R_pt])
            S.op("act", lambda e: e.activation(out=rs[:, c0:c0 + n], in_=pt[:, 0:n], func=AF.Sqrt, bias=EPS, scale=1.0 / denom),
                 reads=[R_pt], writes=[R_cb[out_cb][bi]])
            S.op("dve", lambda e: e.reciprocal(out=rs[:, c0:c0 + n], in_=rs[:, c0:c0 + n]),
                 reads=[R_cb[out_cb][bi]], writes=[R_cb[out_cb][bi]])
        return [p1, p2]

    def prenorm_pieces(gi, c0, n):
        bi = c0 // 256
        ps = rstd_pieces(xT[:, :, c0:c0 + n], [R_x[bi]], KC, c0, n, float(D), 9)

        def hops(cs):
            for c in cs:
                S.op("dve", lambda e, c=c: e.scalar_tensor_tensor(
                    out=hT[:, c, c0:c0 + n], in0=xT[:, c, c0:c0 + n], scalar=gainT[:, c, gi:gi + 1], in1=cbs[9][:, c0:c0 + n],
                    op0=ALU.mult, op1=ALU.mult), reads=[R_x[bi], R_gain, R_cb[9][bi]], writes=[R_h[bi]])
        return ps + [lambda: hops(range(0, 4)), lambda: hops(range(4, 8))]

    def postnorm_pieces(gi, c0, n):
        bi = c0 // 256
        ps = rstd_pieces(yacc[:, 0:KC, c0:c0 + n], [R_cb[c][bi] for c in range(KC)], KC, c0, n, float(D), 9)

        def xops(cs):
            for c in cs:
                S.op("dve", lambda e, c=c: e.tensor_tensor(out=yacc[:, c, c0:c0 + n], in0=yacc[:, c, c0:c0 + n],
                                                           in1=cbs[9][:, c0:c0 + n], op=ALU.mult),
                     reads=[R_cb[c][bi], R_cb[9][bi]], writes=[R_cb[c][bi]])
                S.op("dve", lambda e, c=c: e.scalar_tensor_tensor(
                    out=xT[:, c, c0:c0 + n], in0=yacc[:, c, c0:c0 + n], scalar=gpost[:, c, gi:gi + 1], in1=xT[:, c, c0:c0 + n],
                    op0=ALU.mult, op1=ALU.add), reads=[R_cb[c][bi], R_gpost, R_x[bi]], writes=[R_x[bi]])
        return ps + [lambda: xops(range(0, 4)), lambda: xops(range(4, 8))]

    epi_q = []

    def pump(k=1):
        for _ in range(k):
            if not epi_q:
                return
            epi_q.pop(0)[1]()

    def need_blk(bi):
        while any(b_ == bi for (b_, _) in epi_q):
            epi_q.pop(0)[1]()

    def flush_epi():
        while epi_q:
            epi_q.pop(0)[1]()

    def prenorm_blk(gi, c0, n):
        for p in prenorm_pieces(gi, c0, n):
            p()

    def rstd_blk(src3, src_regs, nchunks, c0, n, denom, out_cb):
        for p in rstd_pieces(src3, src_regs, nchunks, c0, n, denom, out_cb):
            p()

    def prenorm(gi, ncols):
        for (c0, n) in blocks_of(ncols):
            prenorm_blk(gi, c0, n)

    def rstd_of(src_fn, src_regs, nchunks, ncols, denom, out_cb):
        for (c0, n) in blocks_of(ncols):
            rstd_blk(src_fn(c0, n), src_regs, nchunks, c0, n, denom, out_cb)

    Y3 = pY[:].rearrange("p (m n) -> p m n", n=256)
    R_Y = R_bank[0:4]

    def ffn(fi, ncols, on_block_final):
        blocks = blocks_of(ncols)
        r0 = 0
        while r0 < DFF:
            ncc = 2 if r0 == 0 else 4
            first = (r0 == 0)
            final = (r0 + 128 * ncc >= DFF)
            G, RG = getw(("km", "wg", fi, ((r0, 128 * ncc),)))
            U, RU = getw(("km", "wu", fi, ((r0, 128 * ncc),)))
            Dn, RD = getw(("rows", "wd", fi, r0, ncc))
            r0 += 128 * ncc
            steps = [(c0, n, cc) for (c0, n) in blocks for cc in range(ncc)]
            atiles = {}

            def down(hf, c0, n, ccs):
                ops = []
                for cc in ccs:
                    at, Ra = atiles[(c0, cc)]
                    for m in range(4 * hf, 4 * hf + 4):
                        ops.append((Y3[:, m, 0:n], Dn[:, cc, m * 128:(m + 1) * 128], at[:, 0:n],
                                    cc == 0 and m % 2 == 0, cc == ncc - 1))

                def f(e, ops=ops):
                    ins = None
                    for (o, l, r_, st, sp) in ops:
                        ins = e.matmul(o, lhsT=l, rhs=r_, start=st, stop=sp, skip_group_check=True)
                    return ins
                S.op("pe", f, reads=[RD] + [atiles[(c0, cc)][1] for cc in ccs], writes=R_Y[2 * hf:2 * hf + 2])
                if ccs[-1] == ncc - 1:
                    ysl = yacc[:, 4 * hf:4 * hf + 4, c0:c0 + n]
                    psl = Y3[:, 4 * hf:4 * hf + 4, 0:n]
                    yregs = [R_cb[c][c0 // 256] for c in range(4 * hf, 4 * hf + 4)]
                    if first:
                        S.op("dve", lambda e: e.tensor_copy(out=ysl, in_=psl), reads=R_Y[2 * hf:2 * hf + 2], writes=yregs)
                    else:
                        S.op("dve", lambda e: e.tensor_tensor(out=ysl, in0=psl, in1=ysl, op=ALU.add),
                             reads=R_Y[2 * hf:2 * hf + 2] + yregs, writes=yregs)

            def retire(step):
                c0, n, cc = step
                down(0, c0, n, [cc])
                if cc == ncc - 1:
                    down(1, c0, n, list(range(ncc)))

            prev = None
            fin_q = []
            for (c0, n, cc) in steps:
                need_blk(c0 // 256)
                pgu, Rpg = gu_r.next()
                pg, pu = pgu[:, 0:256], pgu[:, 256:512]
                mm_group(pg[:, 0:n], lambda k: G[:, k, cc * 128:(cc + 1) * 128],
                         lambda k: hT[:, k, c0:c0 + n], KC, [RG, R_h[c0 // 256]], [Rpg])
                mm_group(pu[:, 0:n], lambda k: U[:, k, cc * 128:(cc + 1) * 128],
                         lambda k: hT[:, k, c0:c0 + n], KC, [RU, R_h[c0 // 256]], [Rpg])
                if fin_q and cc == 1:
                    on_block_final(*fin_q.pop(0))
                pump(2)
                sgt, Rsg = sg_r.next()
                at, Ra = a_r.next()
                atiles[(c0, cc)] = (at, Ra)
                S.op("act", lambda e, sgt=sgt, pg=pg, n=n: e.activation(out=sgt[:, 0:n], in_=pg[:, 0:n], func=AF.Silu),
                     reads=[Rpg], writes=[Rsg])
                S.op("dve", lambda e, at=at, sgt=sgt, pu=pu, n=n: e.tensor_tensor(out=at[:, 0:n], in0=sgt[:, 0:n],
                                                                                   in1=pu[:, 0:n], op=ALU.mult),
                     reads=[Rsg, Rpg], writes=[Ra])
                if prev is not None:
                    retire(prev)
                    if final and prev[2] == ncc - 1:
                        fin_q.append((prev[0], prev[1]))
                prev = (c0, n, cc)
            retire(prev)
            relw(RG, RU, RD)
            if final:
                fin_q.append((prev[0], prev[1]))
                while fin_q:
                    on_block_final(*fin_q.pop(0))

    def proj_fm(Wt, RW, coff, blocks, evac):
        for (c0, n) in blocks:
            need_blk(c0 // 256)
            pt, R_pt = gu_r.next()
            mm_group(pt[:, 0:n], lambda k: Wt[:, k, coff:coff + 128], lambda k, c0=c0, n=n: hT[:, k, c0:c0 + n],
                     KC, [RW, R_h[c0 // 256]], [R_pt])
            evac(pt, R_pt, c0, n)
            pump(1)

    def proj_tm(Wt, RW, coff, c0, ntok):
        need_blk(c0 // 256)
        pt, R_pt = gu_r.next()
        mm_group(pt[0:ntok, 0:128], lambda k: hT[:, k, c0:c0 + ntok], lambda k: Wt[:, k, coff:coff + 128],
                 KC, [RW, R_h], [R_pt])
        return pt, R_pt

    def out_proj(specs, ncols, on_block_final):
        Ws = [getw(spec) for spec in specs]
        alt = 0
        pending = None
        for (c0, n) in blocks_of(ncols):
            for mp, (Wt, RW) in enumerate(Ws):
                for mm in range(2):
                    m = 2 * mp + mm
                    pt, R_pt = gu_r.next()
                    mm_group(pt[:, 0:n], lambda k: Wt[:, k, mm * 128:(mm + 1) * 128],
                             lambda k: zT[:, k, c0:c0 + n], KC, [RW] + R_z, [R_pt])
                    if alt % 2 == 0:
                        S.op("act", lambda e, pt=pt, m=m, c0=c0, n=n: e.activation(out=yacc[:, m, c0:c0 + n], in_=pt[:, 0:n],
                                                                                  func=AF.Copy),
                             reads=[R_pt], writes=[R_cb[m][c0 // 256]])
                    else:
                        S.op("dve", lambda e, pt=pt, m=m, c0=c0, n=n: e.tensor_copy(out=yacc[:, m, c0:c0 + n], in_=pt[:, 0:n]),
                             reads=[R_pt], writes=[R_cb[m][c0 // 256]])
                    alt += 1
                    pump(1)
                if mp == 0 and pending is not None:
                    on_block_final(*pending)
                    pending = None
            pending = (c0, n)
        relw(*[rw for (_, rw) in Ws])
        on_block_final(*pending)

    def tiles_of(sb):
        tl = [(t * 128, 128, t) for t in range(SBF // 128)]
        if sb == 0:
            tl += [(META0, NMETA, 8), (SAMP0, NSAMP, 9)]
        return tl

    SCALE = 64 ** -0.5

    S2_r = Ring([(pY[:, 0:1024], R_bank[0:2]), (pY[:, 1024:2048], R_bank[2:4])])

    def attn_parts(j, qa, R_qa, qc0, nq, key_tiles, key_regs, mask):
        nkt = sum(k[2] for k in key_tiles)
        st = {}

        def get_po():
            if "po" not in st:
                po, R_po = gu_r.next()
                st["po"] = (po.rearrange("p (z q) -> p z q", q=128), R_po)
            return st["po"]
        parts = []
        for half in range(2):
            parts.append(_attn_half(j, qa, R_qa, qc0, nq, key_tiles, key_regs, mask, nkt, half, get_po))

        def evac():
            po3, R_po = get_po()
            S.op("act", lambda e: e.activation(out=zT[:, 0:4, qc0:qc0 + nq], in_=po3[:, :, 0:nq], func=AF.Copy),
                 reads=[R_po], writes=R_z[0:4])
        return parts, evac

    def _attn_half(j, qa, R_qa, qc0, nq, key_tiles, key_regs, mask, nkt, half, get_po):
        hs = {}

        def partA():
            p0 = 64 * half
            s2, R_s2 = S2_r.next()
            s3 = s2.rearrange("p (z k) -> p z k", k=256)
            groups = []
            for zc in range(4):
                off = 0
                for ti, (kT, vt, nk) in enumerate(key_tiles):
                    groups.append((s3[0:nq, zc, off:off + nk], qa[zc][p0:p0 + 64, qc0:qc0 + nq], kT[p0:p0 + 64, 0:nk],
                                   zc % 2 == 0 and ti == 0, mask is None))
                    off += nk
                if mask is not None:
                    groups.append((s3[0:nq, zc, 0:nkt], ident_b[0:nq, 0:nq], mask[0:nq, 0:nkt], False, True))

            def fs(e, groups=groups):
                ins = None
                for (o, l, r_, st, sp) in groups:
                    ins = e.matmul(o, lhsT=l, rhs=r_, start=st, stop=sp, skip_group_check=True)
                return ins
            S.op("pe", fs, reads=R_qa + key_regs + [R_idb, R_mskb], writes=R_s2)
            sm, R_sm = sm_r.next()
            sm3 = sm.rearrange("p (z k) -> p z k", k=256)
            pn, R_pn = pn_r.next()
            pn3 = pn.rearrange("p (z k) -> p z k", k=256)
            pT, R_pT = pT_r.next()
            cl, R_cl = col_r.next()
            S.op("dve", lambda e, s3=s3, cl=cl: e.reduce_max(out=cl[0:nq, 0:4], in_=s3[0:nq, :, 0:nkt], axis=AX.X),
                 reads=R_s2, writes=[R_cl])
            nsk = negsink[0:nq, 8 * j + 4 * half:8 * j + 4 * half + 4]
            S.op("dve", lambda e, cl=cl, nsk=nsk: e.scalar_tensor_tensor(
                out=cl[0:nq, 4:8], in0=cl[0:nq, 0:4], scalar=-SCALE, in1=nsk, op0=ALU.mult, op1=ALU.min),
                reads=[R_cl, R_sink], writes=[R_cl])
            for zc in range(4):
                S.op("act", lambda e, s3=s3, sm3=sm3, cl=cl, zc=zc: e.activation(
                    out=sm3[0:nq, zc, 0:nkt], in_=s3[0:nq, zc, 0:nkt], func=AF.Exp, bias=cl[0:nq, 4 + zc:5 + zc], scale=SCALE),
                    reads=R_s2 + [R_cl], writes=[R_sm])
            S.op("dve", lambda e, cl=cl, nsk=nsk: e.tensor_tensor(out=cl[0:nq, 8:12], in0=cl[0:nq, 4:8], in1=nsk, op=ALU.subtract),
                 reads=[R_cl, R_sink], writes=[R_cl])
            S.op("act", lambda e, cl=cl: e.activation(out=cl[0:nq, 8:12], in_=cl[0:nq, 8:12], func=AF.Exp), reads=[R_cl], writes=[R_cl])
            S.op("dve", lambda e, sm3=sm3, cl=cl: e.reduce_sum(out=cl[0:nq, 12:16], in_=sm3[0:nq, :, 0:nkt], axis=AX.X),
                 reads=[R_sm], writes=[R_cl])
            S.op("dve", lambda e, cl=cl: e.tensor_tensor(out=cl[0:nq, 12:16], in0=cl[0:nq, 12:16], in1=cl[0:nq, 8:12], op=ALU.add),
                 reads=[R_cl], writes=[R_cl])
            S.op("dve", lambda e, cl=cl: e.reciprocal(out=cl[0:nq, 16:20], in_=cl[0:nq, 12:16]), reads=[R_cl], writes=[R_cl])
            S.op("dve", lambda e, sm3=sm3, pn3=pn3, cl=cl: e.tensor_tensor(
                out=pn3[0:nq, :, 0:nkt], in0=sm3[0:nq, :, 0:nkt], in1=cl[0:nq, 16:20].unsqueeze(2).to_broadcast([nq, 4, nkt]),
                op=ALU.mult), reads=[R_sm, R_cl], writes=[R_pn])
            hs["v"] = (pn3, R_pn, pT, R_pT)

        def partB():
            p0 = 64 * half
            pn3, R_pn, pT, R_pT = hs["v"]
            po3, R_po = get_po()
            pt, R_pt = gu_r.next()
            ptb = pt.bitcast(BF16).rearrange("p (g q) -> p g q", q=128)
            nkt_ = len(key_tiles)
            trs = []
            for zc in range(4):
                off = 0
                for ti, (kT, vt, nk) in enumerate(key_tiles):
                    trs.append((ptb[0:nk, zc * nkt_ + ti, 0:nq], pn3[0:nq, zc, off:off + nk]))
                    off += nk

            def ftr(e, trs=trs):
                ins = None
                for (o, i_) in trs:
                    ins = e.transpose(o, i_, ident_b[0:nq, 0:nq])
                return ins
            S.op("pe", ftr, reads=[R_pn, R_idb], writes=[R_pt])
            for ti, (kT, vt, nk) in enumerate(key_tiles):
                src_v = ptb[0:nk, ti:4 * nkt_:nkt_, 0:nq] if nkt_ > 1 else ptb[0:nk, 0:4, 0:nq]
                dst_v = pT[0:nk, ti:4 * nkt_:nkt_, 0:nq] if nkt_ > 1 else pT[0:nk, 0:4, 0:nq]
                S.op("act", lambda e, src_v=src_v, dst_v=dst_v: e.activation(out=dst_v, in_=src_v, func=AF.Copy),
                     reads=[R_pt], writes=[R_pT])
            pvs = []
            for zc in range(4):
                for ti, (kT, vt, nk) in enumerate(key_tiles):
                    pvs.append((po3[p0:p0 + 64, zc, 0:nq], vt[0:nk, p0:p0 + 64], pT[0:nk, zc * nkt_ + ti, 0:nq],
                                ti == 0, ti == nkt_ - 1))

            def fpv(e, pvs=pvs):
                ins = None
                for (o, l, r_, st, sp) in pvs:
                    ins = e.matmul(o, lhsT=l, rhs=r_, start=st, stop=sp, skip_group_check=True)
                return ins
            S.op("pe", fpv, reads=[R_pT] + key_regs, writes=[R_po])
            pump(1)
        return partA, partB

    def attn_pipeline(j, qa, R_qa, jobs):
        seq = []
        for job in jobs:
            parts, evac = attn_parts(j, qa, R_qa, *job)
            seq.append((parts[0], None))
            seq.append((parts[1], evac))
        prev = None
        for (pa, pb), evac in seq:
            pa()
            if prev is not None:
                prev[0]()
                if prev[1] is not None:
                    prev[1]()
            prev = (pb, evac)
        prev[0]()
        prev[1]()

    BOFF = {"fr": 0, "meta": SBF + 2, "samp": SBF + 2 + NMETA + 2}
    CTI = {"fr": 0, "meta": 16, "samp": 17}

    def even_mixer(j, sb, ncols, on_block_final):
        last = (sb == n_sb - 1)
        blocks = blocks_of(ncols)
        tiles = tiles_of(sb)
        wn = "wie"
        qa = [bfv(c) for c in range(4)]
        R_qa = [R_cb[c] for c in range(4)]
        kf, R_kf = bfv(4), R_cb[4]
        for half2 in range(2):
            z0 = 2 * half2
            W, RW = getw(("km", wn, j, ((64 * z0, 64), (256 + 64 * z0, 64), (64 * (z0 + 1), 64), (256 + 64 * (z0 + 1), 64))))
            for zi in range(2):
                zc = z0 + zi
                proj_fm(W, RW, 128 * zi, blocks, lambda pt, R_pt, c0, n, zc=zc: S.op(
                    "act", lambda e: e.activation(out=qa[zc][:, c0:c0 + n], in_=pt[:, 0:n], func=AF.Copy),
                    reads=[R_pt], writes=[R_qa[zc]]))
            relw(RW)
        W, RW = getw(("km", wn, j, ((512, 256),)))
        proj_fm(W, RW, 0, blocks, lambda pt, R_pt, c0, n: S.op(
            "act", lambda e: e.activation(out=kf[:, c0:c0 + n], in_=pt[:, 0:n], func=AF.Copy), reads=[R_pt], writes=[R_kf]))
        for (c0, ntok, ti) in tiles:
            pt, R_pt = proj_tm(W, RW, 128, c0, ntok)
            S.op("dve", lambda e, pt=pt, ntok=ntok, ti=ti: e.tensor_copy(out=vtm[0:ntok, ti, :], in_=pt[0:ntok, 0:128]),
                 reads=[R_pt], writes=[R_vtm])
            if last and ti == 7:
                S.op("act", lambda e, pt=pt: e.activation(out=kvf[:, 1, :], in_=pt[:, 0:128], func=AF.Copy),
                     reads=[R_pt], writes=[R_kvf])
                S.dma("pool", lambda e: e.dma_start(out=ovp[j], in_=kvf[:, 1, :]), reads=[R_kvf], writes=[OUT])
                pk, R_pk = proj_tm(W, RW, 0, c0, ntok)
                S.op("act", lambda e, pk=pk: e.activation(out=kvf[:, 0, :], in_=pk[:, 0:128], func=AF.Copy),
                     reads=[R_pk], writes=[R_kvf])
                S.dma("pool", lambda e: e.dma_start(out=okp[j], in_=kvf[:, 0, :]), reads=[R_kvf], writes=[OUT])
            if ti == 9:
                S.op("act", lambda e, pt=pt: e.activation(out=kvf[0:NSAMP, 1, :], in_=pt[0:NSAMP, 0:128], func=AF.Copy),
                     reads=[R_pt], writes=[R_kvf])
                S.dma("pool", lambda e: e.dma_start(out=ovs[j, 96:128, :], in_=kvf[0:NSAMP, 1, :]), reads=[R_kvf], writes=[OUT])
                pk, R_pk = proj_tm(W, RW, 0, c0, ntok)
                S.op("act", lambda e, pk=pk: e.activation(out=kvf[0:NSAMP, 0, :], in_=pk[0:NSAMP, 0:128], func=AF.Copy),
                     reads=[R_pk], writes=[R_kvf])
                S.dma("pool", lambda e: e.dma_start(out=oks[j, 96:128, :], in_=kvf[0:NSAMP, 0, :]), reads=[R_kvf], writes=[OUT])
                S.dma("pool", lambda e: e.dma_start(out=oks[j, 0:96, :], in_=ck[j, 32:128, :]), writes=[OUT])
                S.dma("pool", lambda e: e.dma_start(out=ovs[j, 0:96, :], in_=cv[j, 32:128, :]), writes=[OUT])
        relw(RW)
        kp, R_kp = kprev[j]
        vp, R_vp = vprev[j]
        jobs = []
        if sb == 0:
            jobs.append((META0, NMETA, [(kf[:, META0:META0 + NMETA], vtm[:, 8, :], NMETA)], [R_kf, R_vtm], None))
            S.op("dve", lambda e: e.tensor_copy(out=kp[:, 0:NMETA], in_=kf[:, META0:META0 + NMETA]), reads=[R_kf], writes=[R_kp])
            S.op("dve", lambda e: e.tensor_copy(out=vp[0:NMETA, :], in_=vtm[0:NMETA, 8, :]), reads=[R_vtm], writes=[R_vp])
        for t in range(SBF // 128):
            if t == 0:
                kts = [(kp[:], vp[:], 128), (kf[:, 0:128], vtm[:, 0, :], 128)]
                kregs = [R_kp, R_vp, R_kf, R_vtm]
            else:
                kts = [(kf[:, (t - 1) * 128:t * 128], vtm[:, t - 1, :], 128), (kf[:, t * 128:(t + 1) * 128], vtm[:, t, :], 128)]
                kregs = [R_kf, R_vtm]
            jobs.append((t * 128, 128, kts, kregs, mskb[:, 256:512] if (sb == 0 and t == 0) else mskb[:, 0:256]))
        if sb != 0:
            attn_pipeline(j, qa, R_qa, jobs)
        if not last and sb != 0:
            S.op("dve", lambda e: e.tensor_copy(out=kp[:], in_=kf[:, SBF - 128:SBF]), reads=[R_kf], writes=[R_kp])
            S.op("dve", lambda e: e.tensor_copy(out=vp[:], in_=vtm[:, 7, :]), reads=[R_vtm], writes=[R_vp])
        if sb == 0:
            _, stg, R_stg = take_stage()
            S.dma("pool", lambda e: e.dma_start(out=stg[:, 0:128], in_=ck[j]), writes=[R_stg])
            S.dma("pool", lambda e: e.dma_start(out=stg[:, 128:256], in_=cv[j]), writes=[R_stg])
            S.op("dve", lambda e: e.tensor_copy(out=cvb[:], in_=stg[:, 128:256]), reads=[R_stg], writes=[R_cvb])
            ckb, R_ckb = bfv(7)[:, 0:128], R_cb[7]
            S.op("dve", lambda e: e.tensor_copy(out=ckb, in_=stg[:, 0:128]), reads=[R_stg], writes=[R_ckb])
            pt, R_pt = gu_r.next()
            ptb = pt.bitcast(BF16)
            S.op("pe", lambda e: e.transpose(ptb[:, 0:128], ckb, ident_b[:]), reads=[R_ckb, R_idb], writes=[R_pt])
            S.op("act", lambda e: e.activation(out=ckT[:], in_=ptb[:, 0:128], func=AF.Copy), reads=[R_pt], writes=[R_ckT])
            jobs.append((SAMP0, NSAMP, [(ckT[:], cvb[:], 128), (kf[:, SAMP0:SAMP0 + NSAMP], vtm[:, 9, :], NSAMP)],
                         [R_ckT, R_cvb, R_kf, R_vtm], None))
            attn_pipeline(j, qa, R_qa, jobs)
            if not last:
                S.op("dve", lambda e: e.tensor_copy(out=kp[:], in_=kf[:, SBF - 128:SBF]), reads=[R_kf], writes=[R_kp])
                S.op("dve", lambda e: e.tensor_copy(out=vp[:], in_=vtm[:, 7, :]), reads=[R_vtm], writes=[R_vp])
        for h0 in (0, 2):
            pa = hgrn_head(j, h0, sb, ncols, blocks, tiles)
            pb = hgrn_head(j, h0 + 1, sb, ncols, blocks, tiles)
            prev = [None, None]
            for ci_ in range(len(pa[0])):
                for k_, p_ in enumerate((pa, pb)):
                    chunks, state_part, out_part, finish = p_
                    sbt_ = state_part(chunks[ci_])
                    if prev[k_] is not None:
                        out_part(*prev[k_])
                    prev[k_] = (chunks[ci_], sbt_)
                pump(1)
            for k_, p_ in enumerate((pa, pb)):
                p_[2](*prev[k_])
            pa[3]()
            pb[3]()
        out_proj([("woe", j, 256 * mp) for mp in range(4)], ncols, on_block_final)

    globals_vtm = (vtm, R_vtm)

    def hgrn_head(j, h, sb, ncols, blocks, tiles):
        last = (sb == n_sb - 1)
        o_ = 4 * (h % 2)
        qs, R_qs = bfv(o_), R_cb[o_]
        kin, R_kin = bfv(o_ + 1), R_cb[o_ + 1]
        gs, R_gs = bfv(o_ + 2), R_cb[o_ + 2]
        A, R_A = cbs[o_ + 3], R_cb[o_ + 3]
        Bx, R_B = cbs[8], R_cb[8]
        C, R_C = cbs[10], R_cb[10]
        oh, R_oh = A, R_A
        ctab, R_ctab = ctabs[h % 2]
        vtm, R_vtm = (globals_vtm[0], globals_vtm[1]) if h % 2 == 0 else (vtmB, R_vtmB)
        W1, RW1 = getw(("km", "wie", j, ((768 + 128 * h, 128), (1280 + 128 * h, 128))))
        W2, RW2 = getw(("km", "wie", j, ((1792 + 128 * h, 128), (2304 + 128 * h, 128))))
        proj_fm(W1, RW1, 0, blocks, lambda pt, R_pt, c0, n: S.op(
            "act", lambda e: e.activation(out=qs[:, c0:c0 + n], in_=pt[:, 0:n], func=AF.Silu), reads=[R_pt], writes=[R_qs]))
        proj_fm(W2, RW2, 128, blocks, lambda pt, R_pt, c0, n: S.op(
            "act", lambda e: e.activation(out=gs[:, c0:c0 + n], in_=pt[:, 0:n], func=AF.Silu), reads=[R_pt], writes=[R_gs]))
        proj_fm(W1, RW1, 128, blocks, lambda pt, R_pt, c0, n: S.op(
            "act", lambda e: e.activation(out=A[:, c0:c0 + n], in_=pt[:, 0:n], func=AF.Sigmoid), reads=[R_pt], writes=[R_A]))
        for (c0, ntok, ti) in tiles:
            pt, R_pt = proj_tm(W2, RW2, 0, c0, ntok)
            S.op("dve", lambda e, pt=pt, ntok=ntok, ti=ti: e.tensor_copy(out=vtm[0:ntok, ti, :], in_=pt[0:ntok, 0:128]),
                 reads=[R_pt], writes=[R_vtm])
        relw(RW1, RW2)
        S.op("dve", lambda e: e.tensor_scalar(out=A[:, 0:ncols], in0=A[:, 0:ncols], scalar1=omlb[:, h, j:j + 1],
                                              scalar2=lbv[:, h, j:j + 1], op0=ALU.mult, op1=ALU.add),
             reads=[R_A, R_omlb, R_lbv], writes=[R_A])
        S.op("dve", lambda e: e.tensor_scalar(out=kin[:, 0:ncols], in0=A[:, 0:ncols], scalar1=-1.0, scalar2=1.0,
                                              op0=ALU.mult, op1=ALU.add), reads=[R_A], writes=[R_kin])
        S.op("act", lambda e: e.activation(out=A[:, 0:ncols], in_=A[:, 0:ncols], func=AF.Ln), reads=[R_A], writes=[R_A])
        segs = [("fr", 0, SBF, 64)]
        if sb == 0:
            segs = [("meta", META0, NMETA, NMETA), ("fr", 0, SBF, 64), ("samp", SAMP0, NSAMP, NSAMP)]
        for (sn, col0, Ltot, Lc) in segs:
            o = BOFF[sn]
            nch = Ltot // Lc
            ci = CTI[sn]
            S.op("dve", lambda e, o=o: e.memset(Bx[:, o:o + 1], 0.0), writes=[R_B])
            S.op("dve", lambda e, o=o, col0=col0, Ltot=Ltot: e.tensor_tensor_scan(
                out=Bx[:, o + 1:o + 1 + Ltot], data0=ones_f[:, 0:1].to_broadcast([128, Ltot]), data1=A[:, col0:col0 + Ltot], initial=0.0,
                op0=ALU.mult, op1=ALU.add), reads=[R_onf, R_A, R_B], writes=[R_B])
            Bst = Bx[:, o:o + Ltot].rearrange("p (c l) -> p c l", l=Lc)[:, :, 0]
            Bin = Bx[:, o + 1:o + 1 + Ltot].rearrange("p (c l) -> p c l", l=Lc)
            Bmd = Bin[:, :, Lc // 2 - 1]
            Bls = Bin[:, :, Lc - 1]
            S.op("dve", lambda e, Bin=Bin, col0=col0, Ltot=Ltot, Lc=Lc, nch=nch: e.tensor_tensor(
                out=A[:, col0:col0 + Ltot].rearrange("p (c l) -> p c l", l=Lc), in0=Bin,
                in1=Bin[:, :, Lc // 2 - 1:Lc // 2].to_broadcast([128, nch, Lc]), op=ALU.subtract),
                reads=[R_B, R_A], writes=[R_A])
            S.op("dve", lambda e, Bmd=Bmd, Bst=Bst, ci=ci, nch=nch: e.tensor_tensor(
                out=ctab[:, 0, ci:ci + nch], in0=Bmd, in1=Bst, op=ALU.subtract), reads=[R_B], writes=[R_ctab])
            S.op("dve", lambda e, Bmd=Bmd, Bls=Bls, ci=ci, nch=nch: e.tensor_tensor(
                out=ctab[:, 1, ci:ci + nch], in0=Bls, in1=Bmd, op=ALU.subtract), reads=[R_B], writes=[R_ctab])
            S.op("dve", lambda e, Bst=Bst, Bls=Bls, ci=ci, nch=nch: e.tensor_tensor(
                out=ctab[:, 2, ci:ci + nch], in0=Bls, in1=Bst, op=ALU.subtract), reads=[R_B], writes=[R_ctab])
        nct = 18 if sb == 0 else 16
        S.op("act", lambda e: e.activation(out=ctab[:, :, 0:nct], in_=ctab[:, :, 0:nct], func=AF.Exp), reads=[R_ctab], writes=[R_ctab])
        S.op("act", lambda e: e.activation(out=C[:, 0:ncols], in_=A[:, 0:ncols], func=AF.Exp), reads=[R_A], writes=[R_C])
        S.op("dve", lambda e: e.tensor_tensor(out=qs[:, 0:ncols], in0=qs[:, 0:ncols], in1=C[:, 0:ncols], op=ALU.mult),
             reads=[R_qs, R_C], writes=[R_qs])
        S.op("act", lambda e: e.activation(out=C[:, 0:ncols], in_=A[:, 0:ncols], func=AF.Exp, scale=-1.0), reads=[R_A], writes=[R_C])
        S.op("dve", lambda e: e.tensor_tensor(out=kin[:, 0:ncols], in0=kin[:, 0:ncols], in1=C[:, 0:ncols], op=ALU.mult),
             reads=[R_kin, R_C], writes=[R_kin])
        if sb == 0:
            St, R_St = Ssm[h]
            S.dma("pool", lambda e: e.dma_start(out=St[:], in_=sh[j, h]), writes=[R_St])
        chunks = []
        for (sn, col0, Ltot, Lc) in segs:
            for c in range(Ltot // Lc):
                if sn == "fr":
                    chunks.append((sn, col0 + c * Lc, Lc, CTI[sn] + c, c // 2, 64 * (c % 2), (c // 2) * 128, 128, c == Ltot // Lc - 1))
                else:
                    chunks.append((sn, col0, Lc, CTI[sn], 8 if sn == "meta" else 9, 0, col0, Ltot, True))
        ktm_cur = {}

        def state_part(ch):
            sn, cs, Lc, ci, ti, p0, tcol0, ntile, seg_last = ch
            St, R_St = Ssm[h] if sn == "samp" else Sst[j][h]
            if p0 == 0:
                pt, R_pt = gu_r.next()
                ptb = pt.bitcast(BF16)
                ktm, R_ktm = ktm_r.next()
                ktm_cur["k"] = (ktm, R_ktm)
                S.op("pe", lambda e: e.transpose(ptb[0:ntile, 0:128], kin[:, tcol0:tcol0 + ntile], ident_b[:]),
                     reads=[R_kin, R_idb], writes=[R_pt])
                S.op("act", lambda e: e.activation(out=ktm[0:ntile, :], in_=ptb[0:ntile, 0:128], func=AF.Copy),
                     reads=[R_pt], writes=[R_ktm])
            ktm, R_ktm = ktm_cur["k"]
            Sb, R_Sb = Sb_r.next()
            S.op("dve", lambda e: e.tensor_scalar(out=Sb[:], in0=St[:], scalar1=ctab[:, 0, ci:ci + 1], scalar2=None, op0=ALU.mult),
                 reads=[R_St, R_ctab], writes=[R_Sb])
            pn_, R_pn_ = ya_r.next()
            mm_group(pn_[:, 0:128], lambda k: ktm[p0:p0 + Lc, :], lambda k: vtm[p0:p0 + Lc, ti, :], 1, [R_ktm, R_vtm], [R_pn_])
            tmp, R_tmp = tmp_r.next()
            S.op("act", lambda e: e.activation(out=tmp[:], in_=pn_[:, 0:128], func=AF.Copy, scale=ctab[:, 1, ci:ci + 1]),
                 reads=[R_pn_, R_ctab], writes=[R_tmp])
            S.op("dve", lambda e: e.scalar_tensor_tensor(out=St[:], in0=St[:], scalar=ctab[:, 2, ci:ci + 1], in1=tmp[:],
                                                         op0=ALU.mult, op1=ALU.add), reads=[R_St, R_ctab, R_tmp], writes=[R_St])
            if seg_last and sn == "samp":
                S.dma("pool", lambda e: e.dma_start(out=ohs[j, h], in_=St[:]), reads=[R_St], writes=[OUT])
            if seg_last and sn == "fr" and last:
                S.dma("pool", lambda e: e.dma_start(out=ohp[j, h], in_=St[:]), reads=[R_St], writes=[OUT])
            return (Sb, R_Sb)

        def out_part(ch, Sbt):
            sn, cs, Lc, ci, ti, p0, tcol0, ntile, seg_last = ch
            Sb, R_Sb = Sbt
            ps, R_s = yb_r.next()
            mm_group(ps[p0:p0 + Lc, 0:Lc], lambda k: kin[:, cs:cs + Lc], lambda k: qs[:, cs:cs + Lc], 1, [R_kin, R_qs], [R_s])
            pth, R_pth = pth_r.next()
            S.op("dve", lambda e: e.tensor_tensor(out=pth[p0:p0 + Lc, 0:Lc], in0=ps[p0:p0 + Lc, 0:Lc],
                                                  in1=mask_hg[p0:p0 + Lc, 0:Lc], op=ALU.mult), reads=[R_s, R_cst], writes=[R_pth])
            po, R_po = gu_r.next()

            def fo(e):
                e.matmul(po[:, 0:Lc], lhsT=Sb[:], rhs=qs[:, cs:cs + Lc], start=True, stop=False)
                return e.matmul(po[:, 0:Lc], lhsT=vtm[p0:p0 + Lc, ti, :], rhs=pth[p0:p0 + Lc, 0:Lc], start=False, stop=True)
            S.op("pe", fo, reads=[R_Sb, R_qs, R_vtm, R_pth], writes=[R_po])
            S.op("act", lambda e: e.activation(out=oh[:, cs:cs + Lc], in_=po[:, 0:Lc], func=AF.Copy), reads=[R_po], writes=[R_oh])

        def finish():
            rstd_of(lambda c0, n: oh[:, c0:c0 + n].unsqueeze(1), [R_oh], 1, ncols, 128.0, 9)
            S.op("dve", lambda e: e.tensor_tensor(out=oh[:, 0:ncols], in0=oh[:, 0:ncols], in1=cbs[9][:, 0:ncols], op=ALU.mult),
                 reads=[R_oh, R_cb[9]], writes=[R_oh])
            S.op("dve", lambda e: e.scalar_tensor_tensor(
                out=zT[:, 4 + h, 0:ncols], in0=oh[:, 0:ncols], scalar=hngT[:, h, j:j + 1], in1=gs[:, 0:ncols],
                op0=ALU.mult, op1=ALU.mult), reads=[R_oh, R_hng, R_gs], writes=[R_z[4 + h]])
        return chunks, state_part, out_part, finish

    def odd_mixer(j, sb, ncols, on_block_final):
        last = (sb == n_sb - 1)
        blocks = blocks_of(ncols)
        uh, R_uh = uhalo[j]
        Wst = {}

        def bufs(jc):
            o = 4 * (jc % 2)
            return (bfv(o), R_cb[o]), (cbs[o + 1], R_cb[o + 1]), (cbs[o + 2], R_cb[o + 2]), (cbs[o + 3], R_cb[o + 3])

        def proj(jc):
            (bgb, R_bg), (cgf, R_cg), (uF, R_uF), (yv, R_yv) = bufs(jc)
            uaux, R_uaux = uaux2[jc % 2]
            W1, RW1 = getw(("km", "wio", j, ((128 * jc, 128), (D + 128 * jc, 128))))
            if jc % 2 == 0:
                Wst["w2"] = getw(("km", "wio", j, ((2 * D + 128 * jc, 256),)))
            W2, RW2 = Wst["w2"]
            proj_fm(W1, RW1, 0, blocks, lambda pt, R_pt, c0, n: S.op(
                "act", lambda e: e.activation(out=bgb[:, c0:c0 + n], in_=pt[:, 0:n], func=AF.Copy), reads=[R_pt], writes=[R_bg]))
            proj_fm(W1, RW1, 128, blocks, lambda pt, R_pt, c0, n: S.op(
                "act", lambda e: e.activation(out=cgf[:, c0:c0 + n], in_=pt[:, 0:n], func=AF.Copy), reads=[R_pt], writes=[R_cg]))

            def evac_u(pt, R_pt, c0, n):
                if c0 < SBF:
                    S.op("dve", lambda e: e.tensor_tensor(out=uF[:, 2 + c0:2 + c0 + n], in0=cgf[:, c0:c0 + n], in1=pt[:, 0:n],
                                                          op=ALU.mult), reads=[R_cg, R_pt], writes=[R_uF])
                else:
                    S.op("dve", lambda e: e.tensor_tensor(out=uaux[:, 2:2 + NMETA], in0=cgf[:, META0:META0 + NMETA],
                                                          in1=pt[:, 0:NMETA], op=ALU.mult), reads=[R_cg, R_pt], writes=[R_uaux])
                    S.op("dve", lambda e: e.tensor_tensor(out=uaux[:, 20:20 + NSAMP], in0=cgf[:, SAMP0:SAMP0 + NSAMP],
                                                          in1=pt[:, NMETA:NMETA + NSAMP], op=ALU.mult),
                         reads=[R_cg, R_pt], writes=[R_uaux])
            proj_fm(W2, RW2, 128 * (jc % 2), blocks, evac_u)
            relw(RW1)
            if jc % 2 == 1:
                relw(RW2)

        def conv(jc):
            (bgb, R_bg), (cgf, R_cg), (uF, R_uF), (yv, R_yv) = bufs(jc)
            uaux, R_uaux = uaux2[jc % 2]
            if sb == 0:
                S.op("dve", lambda e: e.memset(uaux[:, 0:2], 0.0), writes=[R_uaux])
                S.op("dve", lambda e: e.tensor_copy(out=uaux[:, 18:20], in_=ccT[:, jc, 2 * j:2 * j + 2]),
                     reads=[R_ccT], writes=[R_uaux])
                S.op("dve", lambda e: e.tensor_copy(out=uF[:, 0:2], in_=uaux[:, NMETA:NMETA + 2]), reads=[R_uaux], writes=[R_uF])
            else:
                S.op("dve", lambda e: e.tensor_copy(out=uF[:, 0:2], in_=uh[:, jc, :]), reads=[R_uh], writes=[R_uF])
            segs = [(uF, R_uF, 0, 0, SBF)]
            if sb == 0:
                segs += [(uaux, R_uaux, 0, META0, NMETA), (uaux, R_uaux, 18, SAMP0, NSAMP)]
            for (ub, R_ub, uo, yc0, N) in segs:
                w0 = cwT[:, jc, 3 * j + 0:3 * j + 1]
                w1 = cwT[:, jc, 3 * j + 1:3 * j + 2]
                w2 = cwT[:, jc, 3 * j + 2:3 * j + 3]
                S.op("dve", lambda e, ub=ub, uo=uo, yc0=yc0, N=N, w0=w0: e.tensor_scalar(
                    out=yv[:, yc0:yc0 + N], in0=ub[:, uo:uo + N], scalar1=w0, scalar2=None, op0=ALU.mult),
                    reads=[R_ub, R_cw], writes=[R_yv])
                S.op("dve", lambda e, ub=ub, uo=uo, yc0=yc0, N=N, w1=w1: e.scalar_tensor_tensor(
                    out=yv[:, yc0:yc0 + N], in0=ub[:, uo + 1:uo + 1 + N], scalar=w1, in1=yv[:, yc0:yc0 + N],
                    op0=ALU.mult, op1=ALU.add), reads=[R_ub, R_cw, R_yv], writes=[R_yv])
                S.op("dve", lambda e, ub=ub, uo=uo, yc0=yc0, N=N, w2=w2: e.scalar_tensor_tensor(
                    out=yv[:, yc0:yc0 + N], in0=ub[:, uo + 2:uo + 2 + N], scalar=w2, in1=yv[:, yc0:yc0 + N],
                    op0=ALU.mult, op1=ALU.add), reads=[R_ub, R_cw, R_yv], writes=[R_yv])
            S.op("dve", lambda e: e.tensor_tensor(out=zT[:, jc, 0:ncols], in0=bgb[:, 0:ncols], in1=yv[:, 0:ncols], op=ALU.mult),
                 reads=[R_bg, R_yv], writes=[R_z[jc]])
            S.op("dve", lambda e: e.tensor_copy(out=uh[:, jc, :], in_=uF[:, SBF:SBF + 2]), reads=[R_uF], writes=[R_uh])
            if sb == 0:
                S.op("dve", lambda e: e.tensor_copy(out=uSh[:, jc, :], in_=uaux[:, 18 + NSAMP:20 + NSAMP]),
                     reads=[R_uaux], writes=[R_uSh])

        proj(0)
        for jc in range(KC):
            if jc + 1 < KC:
                proj(jc + 1)
            conv(jc)
        if last:
            fm_to_rows(lambda c: uh[:, c, :], 2, KC, ocp[2 * j:2 * j + 2, :], [R_uh])
        if sb == 0:
            fm_to_rows(lambda c: uSh[:, c, :], 2, KC, ocs[2 * j:2 * j + 2, :], [R_uSh])
        out_proj([("km", "woo", j, ((256 * mp, 256),)) for mp in range(4)], ncols, on_block_final)

    MIXERS = {"even": even_mixer, "odd": odd_mixer}

    def main(max_stage=None):
        import os
        skip = os.environ.get("KSKIP", "")
        if "setup" in skip:
            S.dma("sp", lambda e: e.dma_start(out=cst[:], in_=consts[:, :]), writes=[R_cst])
        else:
            setup()
        if "io" in skip:
            return
        for sb in range(n_sb):
            ncols = NCM if sb == 0 else SBF
            load_x(sb)
            stages = [(l, s) for l in range(n_layers) for s in range(3)]
            if max_stage is not None:
                stages = stages[:max_stage]
            gidx = [l * 6 + 2 * s for (l, s) in stages]
            if stages:
                prenorm(gidx[0], ncols)
            for si, (l, s) in enumerate(stages):
                nxt = gidx[si + 1] if si + 1 < len(stages) else None

                import os
                late = ("late%d" % s) in os.environ.get("KSKIP", "")

                def epilogue(c0, n, g=gidx[si], nxt=nxt, late=late):
                    for p in postnorm_pieces(g + 1, c0, n):
                        epi_q.append((c0 // 256, p))
                    if nxt is not None and not late:
                        for p in prenorm_pieces(nxt, c0, n):
                            epi_q.append((c0 // 256, p))
                if s == 0:
                    ffn(2 * l, ncols, epilogue)
                elif s == 1:
                    MIXERS["even" if l % 2 == 0 else "odd"](l // 2, sb, ncols, epilogue)
                else:
                    ffn(2 * l + 1, ncols, epilogue)
                if late and nxt is not None:
                    flush_epi()
                    prenorm(nxt, ncols)
            flush_epi()
            store_y(sb)

    main(dbg_at)
    if plan is None:
        S.stack.close()
        return None, ws["rec"]
    S.finish([OUT])
    return S, ws["rec"]


IN_NAMES = ["xp", "xs", "ck", "cv", "sh", "ccv", "meta", "gains", "wg", "wu", "wd", "wie", "woe", "sinks", "lbl", "hng",
            "wio", "cw", "woo", "consts"]


def make_nc(**kw):
    nc0 = bass.Bass("TRN2", target_bir_lowering=False)
    _, plan = build_program(nc0, plan=None, **kw)
    nc = bass.Bass("TRN2", target_bir_lowering=False)
    S, rec = build_program(nc, plan=plan, **kw)
    assert rec == plan
    return nc, S


def core_inputs(inp, b, sbi, shared):
    m = dict(shared)
    m["xp"] = np.ascontiguousarray(inp["x_prompt"][b])
    m["xs"] = np.ascontiguousarray(inp["x_sample"][sbi])
    m["ck"] = np.ascontiguousarray(inp["cache_swa_k"][:, sbi]).reshape(2, 128, 128)
    m["cv"] = np.ascontiguousarray(inp["cache_swa_v"][:, sbi]).reshape(2, 128, 128)
    m["sh"] = np.ascontiguousarray(inp["state_hgrn"][:, sbi])
    m["ccv"] = np.ascontiguousarray(inp["cache_conv"][:, sbi]).reshape(4, D)
    return m


def shared_inputs(inp):
    f = lambda a: np.ascontiguousarray(np.asarray(a, dtype=np.float32))
    return {
        "meta": f(inp["meta_tokens"]), "gains": f(inp["norm_gains"]).reshape(24, D),
        "wg": f(inp["w_ffn_gate"]).reshape(8, D, DFF), "wu": f(inp["w_ffn_up"]).reshape(8, D, DFF),
        "wd": f(inp["w_ffn_down"]).reshape(8, DFF, D), "wie": f(inp["w_in_even"]), "woe": f(inp["w_out_even"]),
        "sinks": f(inp["attn_sinks"]).reshape(1, 16), "lbl": f(inp["hgrn_lb_logits"]),
        "hng": f(inp["hgrn_norm_gain"]).reshape(2, 512), "wio": f(inp["w_in_odd"]),
        "cw": f(inp["conv_w"]).reshape(6, D), "woo": f(inp["w_out_odd"]), "consts": host_consts(),
    }


_CACHE = {}


def kernel(**inputs):
    if "nc" not in _CACHE:
        _CACHE["nc"] = make_nc()[0]
    nc = _CACHE["nc"]
    inp = {k: np.asarray(v) for k, v in inputs.items()}
    sh = shared_inputs(inp)
    in_maps = [core_inputs(inp, c % 4, c, sh) for c in range(8)]
    res = run_bass_kernel_spmd(nc, in_maps, core_ids=list(range(8)))
    r = res.results
    f32 = np.float32
    y_prompt = np.stack([r[b]["yp"] for b in range(4)]).astype(f32)
    y_sample = np.stack([r[c]["ys"] for c in range(8)]).astype(f32)

    def gather(name, cores, shape):
        a = np.stack([r[c][name] for c in cores], axis=1)
        return np.ascontiguousarray(a.reshape(shape)).astype(f32)
    P, Q = range(4), range(8)
    swa_k_p = gather("okp", P, (2, 4, 128, 2, 64))
    swa_v_p = gather("ovp", P, (2, 4, 128, 2, 64))
    hgrn_p = gather("ohp", P, (2, 4, 4, 128, 128))
    conv_p = np.stack([r[c]["ocp"].reshape(2, 2, D) for c in P], axis=1).astype(f32)
    swa_k_s = gather("oks", Q, (2, 8, 128, 2, 64))
    swa_v_s = gather("ovs", Q, (2, 8, 128, 2, 64))
    hgrn_s = gather("ohs", Q, (2, 8, 4, 128, 128))
    conv_s = np.stack([r[c]["ocs"].reshape(2, 2, D) for c in Q], axis=1).astype(f32)
    return (y_prompt, y_sample, swa_k_p, swa_v_p, hgrn_p, conv_p, swa_k_s, swa_v_s, hgrn_s, conv_s)
```

```python
import numpy as np
from contextlib import ExitStack
import concourse.bass as bass
import concourse.mybir as mybir
from concourse.bass_utils import run_bass_kernel_spmd

F32 = mybir.dt.float32
BF16 = mybir.dt.bfloat16
AF = mybir.ActivationFunctionType
ALU = mybir.AluOpType
AX = mybir.AxisListType


class Reg:
    __slots__ = ("name", "last_w", "readers", "multi", "writers", "excl")

    def __init__(self, name, multi=False, excl=False):
        self.name = name
        self.last_w = None
        self.readers = []
        self.multi = multi
        self.writers = []
        self.excl = excl


class _Op:
    __slots__ = ("eng", "fn", "deps", "is_dma", "needs_inc", "sem", "val", "lane_prev", "idx")


EPOCH = 30000
NLANES = 32


class Sched:
    def __init__(self, nc):
        self.nc = nc
        self.stack = ExitStack()
        self.ops = []
        self.nsem = 0

    def sb(self, name, shape, dtype):
        return self.stack.enter_context(self.nc.sbuf_tensor(name, shape, dtype))

    def ps(self, name, shape, dtype):
        return self.stack.enter_context(self.nc.psum_tensor(name, shape, dtype))

    def _sem(self, name):
        self.nsem += 1
        return self.stack.enter_context(self.nc.semaphore(name))

    @staticmethod
    def _flat(regs):
        out = []
        for r in regs:
            if isinstance(r, (list, tuple)):
                out.extend(Sched._flat(r))
            else:
                out.append(r)
        return out

    def _add(self, eng, fn, reads, writes, is_dma):
        reads = self._flat(reads)
        writes = self._flat(writes)
        op = _Op()
        op.eng = eng
        op.fn = fn
        op.is_dma = is_dma
        op.needs_inc = False
        op.sem = None
        op.val = 0
        op.lane_prev = None
        op.idx = len(self.ops)
        deps = {}
        for r in reads:
            if r.multi:
                for w in r.writers:
                    deps[w.idx] = w
            else:
                if r.last_w is not None:
                    deps[r.last_w.idx] = r.last_w
                if r.excl:
                    for rd in r.readers:
                        if rd.eng != eng:
                            deps[rd.idx] = rd
        for w in writes:
            if w.multi:
                continue
            if w.last_w is not None:
                deps[w.last_w.idx] = w.last_w
            for rd in w.readers:
                deps[rd.idx] = rd
        for r in reads:
            if not r.multi:
                r.readers.append(op)
        for w in writes:
            if w.multi:
                w.writers.append(op)
            else:
                w.last_w = op
                w.readers = []
        final = []
        for i in sorted(deps):
            d = deps[i]
            if d is op:
                continue
            if (not is_dma) and (not d.is_dma) and d.eng == "pe" and eng == "pe":
                continue
            d.needs_inc = True
            final.append(d)
        op.deps = final
        self.ops.append(op)
        return op

    def op(self, eng, fn, reads=(), writes=()):
        return self._add(eng, fn, reads, writes, False)

    def dma(self, eng, fn, reads=(), writes=()):
        return self._add(eng, fn, reads, writes, True)

    def finish(self, outputs):
        self._add("sp", None, outputs, (), False)
        cnt = {}
        engsems = {}
        lanes = {}
        lane_cnt = {}
        lane_last = {}
        ndma_e = {}
        ndma = 0
        for op in self.ops:
            if op.is_dma:
                nd = ndma_e.get(op.eng, 0)
                ndma_e[op.eng] = nd + 1
                ndma += 1
                ln = (op.eng, nd % NLANES)
                n = lane_cnt.get(ln, 0)
                ep = n // (EPOCH // 16)
                lst = lanes.setdefault(ln, [])
                while len(lst) <= ep:
                    lst.append(self._sem("ln%s%d_%d" % (op.eng, ln[1], len(lst))))
                op.sem = lst[ep]
                op.val = (n - ep * (EPOCH // 16) + 1) * 16
                op.lane_prev = lane_last.get(ln)
                lane_last[ln] = op
                lane_cnt[ln] = n + 1
            elif op.needs_inc:
                n = cnt.get(op.eng, 0)
                ep = n // EPOCH
                lst = engsems.setdefault(op.eng, [])
                while len(lst) <= ep:
                    lst.append(self._sem("%s_%d" % (op.eng, len(lst))))
                op.sem = lst[ep]
                op.val = n - ep * EPOCH + 1
                cnt[op.eng] = n + 1
        by_eng = {}
        for op in self.ops:
            by_eng.setdefault(op.eng, []).append(op)
        self.stats = {k: len(v) for k, v in by_eng.items()}
        self.stats["ndma"] = ndma
        self.stats["nsem"] = self.nsem

        def emit(name, e):
            known = {}
            for op in by_eng.get(name, []):
                ws = list(op.deps)
                if op.lane_prev is not None:
                    ws.append(op.lane_prev)
                for d in ws:
                    k = id(d.sem)
                    if known.get(k, 0) < d.val:
                        e.wait_ge(d.sem, d.val)
                        known[k] = d.val
                if op.fn is not None:
                    ins = op.fn(e)
                    if op.is_dma:
                        ins.then_inc(op.sem, 16)
                    elif op.needs_inc:
                        ins.then_inc(op.sem, 1)

        with self.nc.Block() as block:
            @block.tensor
            def _(e):
                emit("pe", e)

            @block.scalar
            def _(e):
                emit("act", e)

            @block.vector
            def _(e):
                emit("dve", e)

            @block.gpsimd
            def _(e):
                emit("pool", e)

            @block.sync
            def _(e):
                emit("sp", e)
        self.stack.close()


D = 1024
KC = 8
DFF = 2816
SEQ = 4096
NMETA = 16
NSAMP = 32
SBF = 1024
NSB = SEQ // SBF
NCM = SBF + NMETA + NSAMP
CB = 1080
META0 = SBF
SAMP0 = SBF + NMETA
EPS = 1e-6
NEG = -30000.0
NCONST = 128 + 256 + 256 + 64


def host_consts():
    c = np.zeros((128, NCONST), np.float32)
    c[:, 0:128] = np.eye(128, dtype=np.float32)
    q = np.arange(128)[:, None]
    k = np.arange(256)[None, :]
    gen = np.where(q < 64, k < 192, k >= 64)
    c[:, 128:384] = np.where(gen, 0.0, NEG)
    first = np.where(k < 16, True, np.where(k < 128, False, np.where(q < 64, k < 192, True)))
    c[:, 384:640] = np.where(first, 0.0, NEG)
    s = (np.arange(128) % 64)[:, None]
    t = np.arange(64)[None, :]
    c[:, 640:704] = (s <= t).astype(np.float32)
    return c


class Ring:
    def __init__(self, items):
        self.items = items
        self.i = 0

    def next(self):
        it = self.items[self.i % len(self.items)]
        self.i += 1
        return it


LA_D, LA_C = 3, 3


def build_program(nc, plan=None, n_sb=NSB, n_layers=4, dbg_at=None):
    S = Sched(nc)
    T = {}

    def din(name, shape):
        T[name] = nc.dram_tensor(name, list(shape), F32, kind="ExternalInput").ap()
        return T[name]

    def dout(name, shape):
        T[name] = nc.dram_tensor(name, list(shape), F32, kind="ExternalOutput").ap()
        return T[name]

    xp = din("xp", [SEQ, D]); xs = din("xs", [NSAMP, D])
    ck = din("ck", [2, 128, 128]); cv = din("cv", [2, 128, 128])
    sh = din("sh", [2, 4, 128, 128]); ccv = din("ccv", [4, D])
    meta = din("meta", [NMETA, D]); gains = din("gains", [24, D])
    din("wg", [8, D, DFF]); din("wu", [8, D, DFF]); din("wd", [8, DFF, D])
    din("wie", [2, D, DFF]); din("woe", [2, D, D])
    sinks = din("sinks", [1, 16]); lbl = din("lbl", [2, 512]); hng = din("hng", [2, 512])
    din("wio", [2, D, 3 * D]); cw = din("cw", [6, D]); din("woo", [2, D, D])
    consts = din("consts", [128, NCONST])
    yp = dout("yp", [SEQ, D]); ys = dout("ys", [NSAMP, D])
    okp = dout("okp", [2, 128, 128]); ovp = dout("ovp", [2, 128, 128])
    ohp = dout("ohp", [2, 4, 128, 128]); ocp = dout("ocp", [4, D])
    oks = dout("oks", [2, 128, 128]); ovs = dout("ovs", [2, 128, 128])
    ohs = dout("ohs", [2, 4, 128, 128]); ocs = dout("ocs", [4, D])
    OUT = Reg("outputs", multi=True)
    dbgx = dout("dbgx", [128, KC * NCM]) if dbg_at is not None else None

    def sbt(name, shape, dt=F32):
        return S.sb(name, shape, dt), Reg(name)

    NBLK = 5
    xT = S.sb("xT", [128, KC, NCM], F32)
    hT = S.sb("hT", [128, KC, NCM], BF16)
    R_x = [Reg("x%d" % b) for b in range(NBLK)]
    R_h = [Reg("h%d" % b) for b in range(NBLK)]
    zT = S.sb("zT", [128, KC, NCM], BF16)
    R_z = [Reg("z%d" % c) for c in range(KC)]
    NCB = KC + 3
    yacc = S.sb("yacc", [128, KC, CB], F32)
    cbs = [yacc[:, i, :] for i in range(KC)] + [S.sb("cb%d" % i, [128, CB], F32)[:] for i in range(KC, NCB)]
    sqb, R_sqb = sbt("sqb", [128, KC, 256], BF16)
    R_cb = [[Reg("cb%d_%d" % (i, b)) for b in range(NBLK)] for i in range(NCB)]
    NSLOT, NSTG = 6, 2
    wsl = [S.sb("wsl%d" % i, [128, 4096], BF16) for i in range(NSLOT)]
    R_wsl = [Reg("wsl%d" % i) for i in range(NSLOT)]
    wst = [S.sb("wst%d" % i, [128, 1024], F32) for i in range(NSTG)]
    R_wst = [Reg("wst%d" % i) for i in range(NSTG)]
    cst, R_cst = sbt("cst", [128, NCONST])
    ident_f = cst[:, 0:128]
    mask_gen = cst[:, 128:384]
    mask_first = cst[:, 384:640]
    mask_hg = cst[:, 640:704]
    ident_b, R_idb = sbt("ident_b", [128, 128], BF16)
    ones_b, R_onb = sbt("ones_b", [128, 128], BF16)
    ones_f, R_onf = sbt("ones_f", [128, 1], F32)
    gainT, R_gain = sbt("gainT", [128, KC, 24])
    gpost, R_gpost = sbt("gpost", [128, KC, 24])
    cwT, R_cw = sbt("cwT", [128, KC, 6])
    sinkc, R_sink = sbt("sinkc", [128, 16])
    negsink = S.sb("negsink", [128, 16], F32)
    mskb, R_mskb = sbt("mskb", [128, 512], BF16)
    lbT, R_lbT = sbt("lbT", [128, 4, 2])
    lbv, R_lbv = sbt("lbv", [128, 4, 2])
    omlb, R_omlb = sbt("omlb", [128, 4, 2])
    hngT, R_hng = sbt("hngT", [128, 4, 2])
    kprev = [sbt("kprev%d" % j, [128, 128], BF16) for j in range(2)]
    vprev = [sbt("vprev%d" % j, [128, 128], BF16) for j in range(2)]
    vtm, R_vtm = sbt("vtm", [128, 10, 128], BF16)
    vtmB, R_vtmB = sbt("vtmB", [128, 10, 128], BF16)
    kvf, R_kvf = sbt("kvf", [128, 2, 128], F32)
    ckT, R_ckT = sbt("ckT", [128, 128], BF16)
    cvb, R_cvb = sbt("cvb", [128, 128], BF16)
    col_r = Ring([sbt("col%d" % i, [128, 20]) for i in range(2)])
    Sst = [[sbt("S%d_%d" % (j, h), [128, 128]) for h in range(4)] for j in range(2)]
    Ssm = [sbt("Ss%d" % h, [128, 128]) for h in range(4)]
    Sb_r = Ring([sbt("Sb%d" % i, [128, 128], BF16) for i in range(6)])
    tmp_r = Ring([sbt("tmpS%d" % i, [128, 128]) for i in range(4)])
    ktm_r = Ring([sbt("ktm%d" % i, [128, 128], BF16) for i in range(4)])
    pth_r = Ring([sbt("pth%d" % i, [128, 64], BF16) for i in range(6)])
    ctabs = [sbt("ctab%d" % i, [128, 3, 20]) for i in range(2)]
    uhalo = [sbt("uhalo%d" % j, [128, KC, 2]) for j in range(2)]
    uSh, R_uSh = sbt("uSh", [128, KC, 2])
    ccT, R_ccT = sbt("ccT", [128, KC, 4])
    uaux2 = [sbt("uaux%d" % i, [128, 64]) for i in range(2)]
    sg_r = Ring([sbt("sg%d" % i, [128, 256]) for i in range(3)])
    a_r = Ring([sbt("a%d" % i, [128, 256], BF16) for i in range(8)])

    pY = S.ps("pY", [128, 2048], F32)
    pX = S.ps("pX", [128, 2048], F32)
    bank = [pY[:, i * 512:(i + 1) * 512] for i in range(4)] + [pX[:, i * 512:(i + 1) * 512] for i in range(4)]
    R_bank = [Reg("bank%d" % b, excl=True) for b in range(8)]
    gu_r = Ring([(bank[b], R_bank[b]) for b in range(4, 8)])
    misc_r = gu_r
    ya_r = Ring([(bank[b], R_bank[b]) for b in range(0, 2)])
    yb_r = Ring([(bank[b], R_bank[b]) for b in range(2, 4)])
    stg_r = Ring(list(zip(wst, R_wst)))
    slot_r = Ring(list(zip(wsl, R_wsl)))

    def pieces_of(spec):
        kind = spec[0]
        if kind == "km":
            _, tn, idx, cols = spec
            wm = T[tn][idx].rearrange("(k p) c -> p k c", p=128)
            out, off = [], 0
            for c0, w in cols:
                out.append((lambda st3, off=off, w=w: st3[:, :, off:off + w], wm[:, :, c0:c0 + w]))
                off += w
            return out, 8, off
        if kind == "rows":
            _, tn, idx, r0, ncc = spec
            src = T[tn][idx][r0:r0 + 128 * ncc, :].rearrange("(c p) m -> p c m", p=128)
            return [(lambda st3: st3[:, :, :], src)], ncc, 1024
        if kind == "woe":
            _, j, m0 = spec
            wm = T["woe"][j]
            out = []
            for q in range(4):
                out.append((lambda st3, q=q: st3[0:64, q, :], wm[64 * q:64 * q + 64, m0:m0 + 256]))
                out.append((lambda st3, q=q: st3[64:128, q, :], wm[256 + 64 * q:256 + 64 * q + 64, m0:m0 + 256]))
            out.append((lambda st3: st3[:, 4:8, :],
                        wm[512:1024, :].rearrange("(k p) c -> p k c", p=128)[:, :, m0:m0 + 256]))
            return out, 8, 256
        raise ValueError(kind)

    ws = {"rec": [], "dma": 0, "cast": 0, "h": [], "st": [], "pend": {}}

    def take_stage():
        idx = stg_r.i % NSTG
        stg, R_stg = stg_r.next()
        return idx, stg, R_stg

    ws["busy"] = [False] * NSLOT
    ws["nslot"] = 0

    def _w_dma(spec):
        si = ws["nslot"] % NSLOT
        if ws["busy"][si]:
            return False
        ws["nslot"] += 1
        ws["busy"][si] = True
        pieces, k, w = pieces_of(spec)
        slot, R_slot = wsl[si], R_wsl[si]
        sl3 = slot[:, 0:k * w].rearrange("p (k w) -> p k w", w=w)
        for dst_fn, src in pieces:
            S.dma("pool", lambda e, d=dst_fn(sl3), s=src: e.dma_start(out=d, in_=s), writes=[R_slot])
        ws["h"].append((sl3, R_slot))
        ws["dma"] += 1
        return True

    def relw(*regs):
        for r in regs:
            ws["busy"][R_wsl.index(r)] = False

    def getw(spec):
        i = len(ws["rec"])
        ws["rec"].append(spec)
        if plan is None:
            assert _w_dma(spec)
            return ws["h"][i]
        assert plan[i] == spec, (i, plan[i], spec)
        while ws["dma"] < min(len(plan), i + 1 + LA_D):
            if not _w_dma(plan[ws["dma"]]):
                break
        assert ws["dma"] > i, "weight slot ring exhausted (missing relw?)"
        return ws["h"][i]

    def rows_to_fm(src_rows, R, nch, dst_fn, wregs):
        _, stg, R_stg = take_stage()
        S.dma("pool", lambda e: e.dma_start(out=stg[0:R, 0:nch * 128], in_=src_rows), writes=[R_stg])
        g = max(1, min(256 // R, nch))
        for c0 in range(0, nch, g):
            n = min(g, nch - c0)
            pt, R_pt = misc_r.next()

            def f(e, c0=c0, n=n, pt=pt):
                ins = None
                for c in range(n):
                    ins = e.transpose(pt[:, c * R:(c + 1) * R],
                                      stg[0:R, (c0 + c) * 128:(c0 + c + 1) * 128], ident_f[0:R, 0:R])
                return ins
            S.op("pe", f, reads=[R_stg, R_cst], writes=[R_pt])
            S.op("dve", lambda e, c0=c0, n=n, pt=pt: e.tensor_copy(
                out=dst_fn(c0, n), in_=pt[:, 0:n * R].rearrange("p (c r) -> p c r", r=R)),
                reads=[R_pt], writes=wregs)

    def fm_to_rows(src_fn, R, nch, dst_rows, rregs):
        _, stg, R_stg = take_stage()
        for c0 in range(0, nch, 2):
            n = min(2, nch - c0)
            pt, R_pt = misc_r.next()

            def f(e, c0=c0, n=n, pt=pt):
                ins = None
                for c in range(n):
                    ins = e.transpose(pt[0:R, c * 128:(c + 1) * 128], src_fn(c0 + c), ident_f)
                return ins
            S.op("pe", f, reads=rregs + [R_cst], writes=[R_pt])
            S.op("dve", lambda e, c0=c0, n=n, pt=pt: e.tensor_copy(
                out=stg[0:R, c0 * 128:(c0 + n) * 128], in_=pt[0:R, 0:n * 128]),
                reads=[R_pt], writes=[R_stg])
        S.dma("pool", lambda e: e.dma_start(out=dst_rows, in_=stg[0:R, 0:nch * 128]), reads=[R_stg], writes=[OUT])

    def bfv(i):
        return cbs[i].bitcast(BF16)

    sm_r = Ring([(cbs[5][:, 0:1024], R_cb[5]), (cbs[6][:, 0:1024], R_cb[6])])
    pn_r = Ring([(bfv(7)[:, 0:1024], R_cb[7]), (bfv(8)[:, 0:1024], R_cb[8])])
    pT_r = Ring([(bfv(7)[:, 1024:2048].rearrange("p (g q) -> p g q", q=128), R_cb[7]),
                 (bfv(8)[:, 1024:2048].rearrange("p (g q) -> p g q", q=128), R_cb[8])])

    def blocks_of(ncols):
        bl = [(c, 256) for c in range(0, SBF, 256)]
        if ncols > SBF:
            bl.append((SBF, ncols - SBF))
        return bl

    def mm_group(out, lhs_fn, rhs_fn, nk, reads, writes):
        ops = [(lhs_fn(k), rhs_fn(k)) for k in range(nk)]

        def f(e):
            ins = None
            for k, (l, r) in enumerate(ops):
                ins = e.matmul(out, lhsT=l, rhs=r, start=(k == 0), stop=(k == nk - 1))
            return ins
        S.op("pe", f, reads=reads, writes=writes)

    def setup():
        S.dma("sp", lambda e: e.dma_start(out=cst[:], in_=consts[:, :]), writes=[R_cst])
        S.op("dve", lambda e: e.tensor_copy(out=ident_b[:], in_=ident_f), reads=[R_cst], writes=[R_idb])
        S.op("dve", lambda e: e.memset(ones_b[:], 1.0), writes=[R_onb])
        S.op("dve", lambda e: e.memset(ones_f[:], 1.0), writes=[R_onf])
        rows_to_fm(gains[:, :], 24, KC, lambda c0, n: gainT[:, c0:c0 + n, :], [R_gain])
        S.op("dve", lambda e: e.tensor_scalar(out=gpost[:], in0=gainT[:], scalar1=0.5, scalar2=None, op0=ALU.mult),
             reads=[R_gain], writes=[R_gpost])
        g4 = gainT[:].rearrange("p c (l i) -> p c l i", i=6)
        gp4 = gpost[:].rearrange("p c (l i) -> p c l i", i=6)
        S.op("dve", lambda e: e.tensor_copy(out=gp4[:, :, :, 3], in_=g4[:, :, :, 3]), reads=[R_gain], writes=[R_gpost])
        rows_to_fm(cw[:, :], 6, KC, lambda c0, n: cwT[:, c0:c0 + n, :], [R_cw])
        rows_to_fm(ccv[:, :], 4, KC, lambda c0, n: ccT[:, c0:c0 + n, :], [R_ccT])
        rows_to_fm(lbl[:, :], 2, 4, lambda c0, n: lbT[:, c0:c0 + n, :], [R_lbT])
        rows_to_fm(hng[:, :], 2, 4, lambda c0, n: hngT[:, c0:c0 + n, :], [R_hng])
        S.dma("pool", lambda e: e.dma_start(out=sinkc[:], in_=sinks[0, :].partition_broadcast(128)), writes=[R_sink])
        S.op("dve", lambda e: e.tensor_scalar(out=negsink[:], in0=sinkc[:], scalar1=-1.0, scalar2=None, op0=ALU.mult),
             reads=[R_sink], writes=[R_sink])
        S.op("dve", lambda e: e.tensor_scalar(out=mskb[:], in0=cst[:, 128:640], scalar1=1.0 / (64 ** -0.5), scalar2=None, op0=ALU.mult),
             reads=[R_cst], writes=[R_mskb])
        S.op("dve", lambda e: e.memset(lbv[:], 0.0), writes=[R_lbv])
        S.op("dve", lambda e: e.tensor_tensor(out=lbv[:, :, 1], in0=lbT[:, :, 1], in1=lbT[:, :, 0], op=ALU.subtract),
             reads=[R_lbT], writes=[R_lbv])
        S.op("act", lambda e: e.activation(out=lbv[:, :, 1], in_=lbv[:, :, 1], func=AF.Sigmoid), reads=[R_lbv], writes=[R_lbv])
        S.op("dve", lambda e: e.tensor_scalar(out=omlb[:], in0=lbv[:], scalar1=-1.0, scalar2=1.0, op0=ALU.mult, op1=ALU.add),
             reads=[R_lbv], writes=[R_omlb])
        for j in range(2):
            S.op("dve", lambda e, j=j: e.memset(kprev[j][0][:], 0.0), writes=[kprev[j][1]])
            S.op("dve", lambda e, j=j: e.memset(vprev[j][0][:], 0.0), writes=[vprev[j][1]])
            for h in range(4):
                S.op("dve", lambda e, j=j, h=h: e.memset(Sst[j][h][0][:], 0.0), writes=[Sst[j][h][1]])

    def load_x(sb):
        for t in range(SBF // 128):
            r0 = sb * SBF + t * 128
            rows_to_fm(xp[r0:r0 + 128, :], 128, KC,
                       lambda c0, n, t=t: xT[:, c0:c0 + n, t * 128:(t + 1) * 128], [R_x])
        if sb == 0:
            rows_to_fm(meta[:, :], NMETA, KC, lambda c0, n: xT[:, c0:c0 + n, META0:META0 + NMETA], [R_x])
            rows_to_fm(xs[:, :], NSAMP, KC, lambda c0, n: xT[:, c0:c0 + n, SAMP0:SAMP0 + NSAMP], [R_x])

    def store_y(sb):
        for t in range(SBF // 128):
            r0 = sb * SBF + t * 128
            fm_to_rows(lambda c, t=t: xT[:, c, t * 128:(t + 1) * 128], 128, KC, yp[r0:r0 + 128, :], [R_x])
        if sb == 0:
            fm_to_rows(lambda c: xT[:, c, SAMP0:SAMP0 + NSAMP], NSAMP, KC, ys[:, :], [R_x])

    def rstd_pieces(src3, src_regs, nchunks, c0, n, denom, out_cb):
        bi = c0 // 256
        rs = cbs[out_cb]
        st = {}

        def p1():
            S.op("act", lambda e: e.activation(out=sqb[:, 0:nchunks, 0:n], in_=src3, func=AF.Square), reads=src_regs, writes=[R_sqb])

        def p2():
            pt, R_pt = misc_r.next()
            mm_group(pt[:, 0:n], lambda k: ones_b[:], lambda k: sqb[:, k, 0:n], nchunks, [R_onb, R_sqb], [R_pt])
            S.op("act", lambda e: e.activation(out=rs[:, c0:c0 + n], in_=pt[:, 0:n], func=AF.Sqrt, bias=EPS, scale=1.0 / denom),
                 reads=[R_pt], writes=[R_cb[out_cb][bi]])
            S.op("dve", lambda e: e.reciprocal(out=rs[:, c0:c0 + n], in_=rs[:, c0:c0 + n]),
                 reads=[R_cb[out_cb][bi]], writes=[R_cb[out_cb][bi]])
        return [p1, p2]

    def prenorm_pieces(gi, c0, n):
        bi = c0 // 256
        ps = rstd_pieces(xT[:, :, c0:c0 + n], [R_x[bi]], KC, c0, n, float(D), 9)

        def hops(cs):
            for c in cs:
                S.op("dve", lambda e, c=c: e.scalar_tensor_tensor(
                    out=hT[:, c, c0:c0 + n], in0=xT[:, c, c0:c0 + n], scalar=gainT[:, c, gi:gi + 1], in1=cbs[9][:, c0:c0 + n],
                    op0=ALU.mult, op1=ALU.mult), reads=[R_x[bi], R_gain, R_cb[9][bi]], writes=[R_h[bi]])
        return ps + [lambda: hops(range(0, 4)), lambda: hops(range(4, 8))]

    def postnorm_pieces(gi, c0, n):
        bi = c0 // 256
        ps = rstd_pieces(yacc[:, 0:KC, c0:c0 + n], [R_cb[c][bi] for c in range(KC)], KC, c0, n, float(D), 9)

        def yscale():
            S.op("dve", lambda e: e.tensor_tensor(
                out=yacc[:, 0:KC, c0:c0 + n], in0=yacc[:, 0:KC, c0:c0 + n],
                in1=cbs[9][:, c0:c0 + n].unsqueeze(1).to_broadcast([128, KC, n]), op=ALU.mult),
                reads=[R_cb[c][bi] for c in range(KC)] + [R_cb[9][bi]], writes=[R_cb[c][bi] for c in range(KC)])

        def xops(cs):
            for c in cs:
                S.op("dve", lambda e, c=c: e.scalar_tensor_tensor(
                    out=xT[:, c, c0:c0 + n], in0=yacc[:, c, c0:c0 + n], scalar=gpost[:, c, gi:gi + 1], in1=xT[:, c, c0:c0 + n],
                    op0=ALU.mult, op1=ALU.add), reads=[R_cb[c][bi], R_gpost, R_x[bi]], writes=[R_x[bi]])
        return ps + [yscale, lambda: xops(range(0, 8))]

    epi_q = []

    def pump(k=1):
        for _ in range(k):
            if not epi_q:
                return
            epi_q.pop(0)[1]()

    def need_blk(bi):
        while any(b_ == bi for (b_, _) in epi_q):
            epi_q.pop(0)[1]()

    def flush_epi():
        while epi_q:
            epi_q.pop(0)[1]()

    def prenorm_blk(gi, c0, n):
        for p in prenorm_pieces(gi, c0, n):
            p()

    def rstd_blk(src3, src_regs, nchunks, c0, n, denom, out_cb):
        for p in rstd_pieces(src3, src_regs, nchunks, c0, n, denom, out_cb):
            p()

    def prenorm(gi, ncols):
        for (c0, n) in blocks_of(ncols):
            prenorm_blk(gi, c0, n)

    def rstd_of(src_fn, src_regs, nchunks, ncols, denom, out_cb):
        for (c0, n) in blocks_of(ncols):
            rstd_blk(src_fn(c0, n), src_regs, nchunks, c0, n, denom, out_cb)

    Y3 = pY[:].rearrange("p (m n) -> p m n", n=256)
    R_Y = R_bank[0:4]

    def ffn(fi, ncols, on_block_final):
        blocks = blocks_of(ncols)
        r0 = 0
        while r0 < DFF:
            ncc = 2 if r0 == 0 else 4
            first = (r0 == 0)
            final = (r0 + 128 * ncc >= DFF)
            G, RG = getw(("km", "wg", fi, ((r0, 128 * ncc),)))
            U, RU = getw(("km", "wu", fi, ((r0, 128 * ncc),)))
            Dn, RD = getw(("rows", "wd", fi, r0, ncc))
            r0 += 128 * ncc
            steps = [(c0, n, cc) for (c0, n) in blocks for cc in range(ncc)]
            atiles = {}

            def down(hf, c0, n, ccs):
                ops = []
                for cc in ccs:
                    at, Ra = atiles[(c0, cc)]
                    for m in range(4 * hf, 4 * hf + 4):
                        ops.append((Y3[:, m, 0:n], Dn[:, cc, m * 128:(m + 1) * 128], at[:, 0:n],
                                    cc == 0 and m % 2 == 0, cc == ncc - 1))

                def f(e, ops=ops):
                    ins = None
                    for (o, l, r_, st, sp) in ops:
                        ins = e.matmul(o, lhsT=l, rhs=r_, start=st, stop=sp, skip_group_check=True)
                    return ins
                S.op("pe", f, reads=[RD] + [atiles[(c0, cc)][1] for cc in ccs], writes=R_Y[2 * hf:2 * hf + 2])
                if ccs[-1] == ncc - 1:
                    ysl = yacc[:, 4 * hf:4 * hf + 4, c0:c0 + n]
                    psl = Y3[:, 4 * hf:4 * hf + 4, 0:n]
                    yregs = [R_cb[c][c0 // 256] for c in range(4 * hf, 4 * hf + 4)]
                    if first:
                        S.op("dve", lambda e: e.tensor_copy(out=ysl, in_=psl), reads=R_Y[2 * hf:2 * hf + 2], writes=yregs)
                    else:
                        S.op("dve", lambda e: e.tensor_tensor(out=ysl, in0=psl, in1=ysl, op=ALU.add),
                             reads=R_Y[2 * hf:2 * hf + 2] + yregs, writes=yregs)

            def retire(step):
                c0, n, cc = step
                down(0, c0, n, [cc])
                if cc == ncc - 1:
                    down(1, c0, n, list(range(ncc)))

            prev = None
            fin_q = []
            for (c0, n, cc) in steps:
                need_blk(c0 // 256)
                pgu, Rpg = gu_r.next()
                pg, pu = pgu[:, 0:256], pgu[:, 256:512]
                mm_group(pg[:, 0:n], lambda k: G[:, k, cc * 128:(cc + 1) * 128],
                         lambda k: hT[:, k, c0:c0 + n], KC, [RG, R_h[c0 // 256]], [Rpg])
                mm_group(pu[:, 0:n], lambda k: U[:, k, cc * 128:(cc + 1) * 128],
                         lambda k: hT[:, k, c0:c0 + n], KC, [RU, R_h[c0 // 256]], [Rpg])
                if fin_q and cc == 1:
                    on_block_final(*fin_q.pop(0))
                pump(2)
                sgt, Rsg = sg_r.next()
                at, Ra = a_r.next()
                atiles[(c0, cc)] = (at, Ra)
                S.op("act", lambda e, sgt=sgt, pg=pg, n=n: e.activation(out=sgt[:, 0:n], in_=pg[:, 0:n], func=AF.Silu),
                     reads=[Rpg], writes=[Rsg])
                S.op("dve", lambda e, at=at, sgt=sgt, pu=pu, n=n: e.tensor_tensor(out=at[:, 0:n], in0=sgt[:, 0:n],
                                                                                   in1=pu[:, 0:n], op=ALU.mult),
                     reads=[Rsg, Rpg], writes=[Ra])
                if prev is not None:
                    retire(prev)
                    if final and prev[2] == ncc - 1:
                        fin_q.append((prev[0], prev[1]))
                prev = (c0, n, cc)
            retire(prev)
            relw(RG, RU, RD)
            if final:
                fin_q.append((prev[0], prev[1]))
                while fin_q:
                    on_block_final(*fin_q.pop(0))

    def proj_fm(Wt, RW, coff, blocks, evac):
        for (c0, n) in blocks:
            need_blk(c0 // 256)
            pt, R_pt = gu_r.next()
            mm_group(pt[:, 0:n], lambda k: Wt[:, k, coff:coff + 128], lambda k, c0=c0, n=n: hT[:, k, c0:c0 + n],
                     KC, [RW, R_h[c0 // 256]], [R_pt])
            evac(pt, R_pt, c0, n)
            pump(1)

    def proj_tm(Wt, RW, coff, c0, ntok):
        need_blk(c0 // 256)
        pt, R_pt = gu_r.next()
        mm_group(pt[0:ntok, 0:128], lambda k: hT[:, k, c0:c0 + ntok], lambda k: Wt[:, k, coff:coff + 128],
                 KC, [RW, R_h], [R_pt])
        return pt, R_pt

    def out_proj(specs, ncols, on_block_final):
        Ws = [getw(spec) for spec in specs]
        alt = 0
        pending = None
        for (c0, n) in blocks_of(ncols):
            for mp, (Wt, RW) in enumerate(Ws):
                for mm in range(2):
                    m = 2 * mp + mm
                    pt, R_pt = gu_r.next()
                    mm_group(pt[:, 0:n], lambda k: Wt[:, k, mm * 128:(mm + 1) * 128],
                             lambda k: zT[:, k, c0:c0 + n], KC, [RW] + R_z, [R_pt])
                    if alt % 2 == 0:
                        S.op("act", lambda e, pt=pt, m=m, c0=c0, n=n: e.activation(out=yacc[:, m, c0:c0 + n], in_=pt[:, 0:n],
                                                                                  func=AF.Copy),
                             reads=[R_pt], writes=[R_cb[m][c0 // 256]])
                    else:
                        S.op("dve", lambda e, pt=pt, m=m, c0=c0, n=n: e.tensor_copy(out=yacc[:, m, c0:c0 + n], in_=pt[:, 0:n]),
                             reads=[R_pt], writes=[R_cb[m][c0 // 256]])
                    alt += 1
                    pump(1)
                if mp == 0 and pending is not None:
                    on_block_final(*pending)
                    pending = None
            pending = (c0, n)
        relw(*[rw for (_, rw) in Ws])
        on_block_final(*pending)

    def tiles_of(sb):
        tl = [(t * 128, 128, t) for t in range(SBF // 128)]
        if sb == 0:
            tl += [(META0, NMETA, 8), (SAMP0, NSAMP, 9)]
        return tl

    SCALE = 64 ** -0.5

    S2_r = Ring([(pY[:, 0:1024], R_bank[0:2]), (pY[:, 1024:2048], R_bank[2:4])])

    def attn_parts(j, qa, R_qa, qc0, nq, key_tiles, key_regs, mask):
        nkt = sum(k[2] for k in key_tiles)
        st = {}

        def get_po():
            if "po" not in st:
                po, R_po = gu_r.next()
                st["po"] = (po.rearrange("p (z q) -> p z q", q=128), R_po)
            return st["po"]
        parts = []
        for half in range(2):
            parts.append(_attn_half(j, qa, R_qa, qc0, nq, key_tiles, key_regs, mask, nkt, half, get_po))

        def evac():
            po3, R_po = get_po()
            S.op("act", lambda e: e.activation(out=zT[:, 0:4, qc0:qc0 + nq], in_=po3[:, :, 0:nq], func=AF.Copy),
                 reads=[R_po], writes=R_z[0:4])
        return parts, evac

    def _attn_half(j, qa, R_qa, qc0, nq, key_tiles, key_regs, mask, nkt, half, get_po):
        hs = {}

        def partA():
            p0 = 64 * half
            s2, R_s2 = S2_r.next()
            s3 = s2.rearrange("p (z k) -> p z k", k=256)
            groups = []
            for zc in range(4):
                off = 0
                for ti, (kT, vt, nk) in enumerate(key_tiles):
                    groups.append((s3[0:nq, zc, off:off + nk], qa[zc][p0:p0 + 64, qc0:qc0 + nq], kT[p0:p0 + 64, 0:nk],
                                   zc % 2 == 0 and ti == 0, mask is None))
                    off += nk
                if mask is not None:
                    groups.append((s3[0:nq, zc, 0:nkt], ident_b[0:nq, 0:nq], mask[0:nq, 0:nkt], False, True))

            def fs(e, groups=groups):
                ins = None
                for (o, l, r_, st, sp) in groups:
                    ins = e.matmul(o, lhsT=l, rhs=r_, start=st, stop=sp, skip_group_check=True)
                return ins
            S.op("pe", fs, reads=R_qa + key_regs + [R_idb, R_mskb], writes=R_s2)
            sm, R_sm = sm_r.next()
            sm3 = sm.rearrange("p (z k) -> p z k", k=256)
            pn, R_pn = pn_r.next()
            pn3 = pn.rearrange("p (z k) -> p z k", k=256)
            pT, R_pT = pT_r.next()
            cl, R_cl = col_r.next()
            S.op("dve", lambda e, s3=s3, cl=cl: e.reduce_max(out=cl[0:nq, 0:4], in_=s3[0:nq, :, 0:nkt], axis=AX.X),
                 reads=R_s2, writes=[R_cl])
            nsk = negsink[0:nq, 8 * j + 4 * half:8 * j + 4 * half + 4]
            S.op("dve", lambda e, cl=cl, nsk=nsk: e.scalar_tensor_tensor(
                out=cl[0:nq, 4:8], in0=cl[0:nq, 0:4], scalar=-SCALE, in1=nsk, op0=ALU.mult, op1=ALU.min),
                reads=[R_cl, R_sink], writes=[R_cl])
            for zc in range(4):
                S.op("act", lambda e, s3=s3, sm3=sm3, cl=cl, zc=zc: e.activation(
                    out=sm3[0:nq, zc, 0:nkt], in_=s3[0:nq, zc, 0:nkt], func=AF.Exp, bias=cl[0:nq, 4 + zc:5 + zc], scale=SCALE),
                    reads=R_s2 + [R_cl], writes=[R_sm])
            S.op("dve", lambda e, cl=cl, nsk=nsk: e.tensor_tensor(out=cl[0:nq, 8:12], in0=cl[0:nq, 4:8], in1=nsk, op=ALU.subtract),
                 reads=[R_cl, R_sink], writes=[R_cl])
            S.op("act", lambda e, cl=cl: e.activation(out=cl[0:nq, 8:12], in_=cl[0:nq, 8:12], func=AF.Exp), reads=[R_cl], writes=[R_cl])
            S.op("dve", lambda e, sm3=sm3, cl=cl: e.reduce_sum(out=cl[0:nq, 12:16], in_=sm3[0:nq, :, 0:nkt], axis=AX.X),
                 reads=[R_sm], writes=[R_cl])
            S.op("dve", lambda e, cl=cl: e.tensor_tensor(out=cl[0:nq, 12:16], in0=cl[0:nq, 12:16], in1=cl[0:nq, 8:12], op=ALU.add),
                 reads=[R_cl], writes=[R_cl])
            S.op("dve", lambda e, cl=cl: e.reciprocal(out=cl[0:nq, 16:20], in_=cl[0:nq, 12:16]), reads=[R_cl], writes=[R_cl])
            S.op("dve", lambda e, sm3=sm3, pn3=pn3, cl=cl: e.tensor_tensor(
                out=pn3[0:nq, :, 0:nkt], in0=sm3[0:nq, :, 0:nkt], in1=cl[0:nq, 16:20].unsqueeze(2).to_broadcast([nq, 4, nkt]),
                op=ALU.mult), reads=[R_sm, R_cl], writes=[R_pn])
            hs["v"] = (pn3, R_pn, pT, R_pT)

        def partB():
            p0 = 64 * half
            pn3, R_pn, pT, R_pT = hs["v"]
            po3, R_po = get_po()
            pt, R_pt = gu_r.next()
            ptb = pt.bitcast(BF16).rearrange("p (g q) -> p g q", q=128)
            nkt_ = len(key_tiles)
            trs = []
            for zc in range(4):
                off = 0
                for ti, (kT, vt, nk) in enumerate(key_tiles):
                    trs.append((ptb[0:nk, zc * nkt_ + ti, 0:nq], pn3[0:nq, zc, off:off + nk]))
                    off += nk

            def ftr(e, trs=trs):
                ins = None
                for (o, i_) in trs:
                    ins = e.transpose(o, i_, ident_b[0:nq, 0:nq])
                return ins
            S.op("pe", ftr, reads=[R_pn, R_idb], writes=[R_pt])
            for ti, (kT, vt, nk) in enumerate(key_tiles):
                src_v = ptb[0:nk, ti:4 * nkt_:nkt_, 0:nq] if nkt_ > 1 else ptb[0:nk, 0:4, 0:nq]
                dst_v = pT[0:nk, ti:4 * nkt_:nkt_, 0:nq] if nkt_ > 1 else pT[0:nk, 0:4, 0:nq]
                S.op("act", lambda e, src_v=src_v, dst_v=dst_v: e.activation(out=dst_v, in_=src_v, func=AF.Copy),
                     reads=[R_pt], writes=[R_pT])
            pvs = []
            for zc in range(4):
                for ti, (kT, vt, nk) in enumerate(key_tiles):
                    pvs.append((po3[p0:p0 + 64, zc, 0:nq], vt[0:nk, p0:p0 + 64], pT[0:nk, zc * nkt_ + ti, 0:nq],
                                ti == 0, ti == nkt_ - 1))

            def fpv(e, pvs=pvs):
                ins = None
                for (o, l, r_, st, sp) in pvs:
                    ins = e.matmul(o, lhsT=l, rhs=r_, start=st, stop=sp, skip_group_check=True)
                return ins
            S.op("pe", fpv, reads=[R_pT] + key_regs, writes=[R_po])
            pump(1)
        return partA, partB

    def attn_pipeline(j, qa, R_qa, jobs):
        seq = []
        for job in jobs:
            parts, evac = attn_parts(j, qa, R_qa, *job)
            seq.append((parts[0], None))
            seq.append((parts[1], evac))
        prev = None
        for (pa, pb), evac in seq:
            pa()
            if prev is not None:
                prev[0]()
                if prev[1] is not None:
                    prev[1]()
            prev = (pb, evac)
        prev[0]()
        prev[1]()

    BOFF = {"fr": 0, "meta": SBF + 2, "samp": SBF + 2 + NMETA + 2}
    CTI = {"fr": 0, "meta": 16, "samp": 17}

    def even_mixer(j, sb, ncols, on_block_final):
        last = (sb == n_sb - 1)
        blocks = blocks_of(ncols)
        tiles = tiles_of(sb)
        wn = "wie"
        qa = [bfv(c) for c in range(4)]
        R_qa = [R_cb[c] for c in range(4)]
        kf, R_kf = bfv(4), R_cb[4]
        for half2 in range(2):
            z0 = 2 * half2
            W, RW = getw(("km", wn, j, ((64 * z0, 64), (256 + 64 * z0, 64), (64 * (z0 + 1), 64), (256 + 64 * (z0 + 1), 64))))
            for zi in range(2):
                zc = z0 + zi
                proj_fm(W, RW, 128 * zi, blocks, lambda pt, R_pt, c0, n, zc=zc: S.op(
                    "act", lambda e: e.activation(out=qa[zc][:, c0:c0 + n], in_=pt[:, 0:n], func=AF.Copy),
                    reads=[R_pt], writes=[R_qa[zc]]))
            relw(RW)
        W, RW = getw(("km", wn, j, ((512, 256),)))
        proj_fm(W, RW, 0, blocks, lambda pt, R_pt, c0, n: S.op(
            "act", lambda e: e.activation(out=kf[:, c0:c0 + n], in_=pt[:, 0:n], func=AF.Copy), reads=[R_pt], writes=[R_kf]))
        for (c0, ntok, ti) in tiles:
            pt, R_pt = proj_tm(W, RW, 128, c0, ntok)
            S.op("dve", lambda e, pt=pt, ntok=ntok, ti=ti: e.tensor_copy(out=vtm[0:ntok, ti, :], in_=pt[0:ntok, 0:128]),
                 reads=[R_pt], writes=[R_vtm])
            if last and ti == 7:
                S.op("act", lambda e, pt=pt: e.activation(out=kvf[:, 1, :], in_=pt[:, 0:128], func=AF.Copy),
                     reads=[R_pt], writes=[R_kvf])
                S.dma("pool", lambda e: e.dma_start(out=ovp[j], in_=kvf[:, 1, :]), reads=[R_kvf], writes=[OUT])
                pk, R_pk = proj_tm(W, RW, 0, c0, ntok)
                S.op("act", lambda e, pk=pk: e.activation(out=kvf[:, 0, :], in_=pk[:, 0:128], func=AF.Copy),
                     reads=[R_pk], writes=[R_kvf])
                S.dma("pool", lambda e: e.dma_start(out=okp[j], in_=kvf[:, 0, :]), reads=[R_kvf], writes=[OUT])
            if ti == 9:
                S.op("act", lambda e, pt=pt: e.activation(out=kvf[0:NSAMP, 1, :], in_=pt[0:NSAMP, 0:128], func=AF.Copy),
                     reads=[R_pt], writes=[R_kvf])
                S.dma("pool", lambda e: e.dma_start(out=ovs[j, 96:128, :], in_=kvf[0:NSAMP, 1, :]), reads=[R_kvf], writes=[OUT])
                pk, R_pk = proj_tm(W, RW, 0, c0, ntok)
                S.op("act", lambda e, pk=pk: e.activation(out=kvf[0:NSAMP, 0, :], in_=pk[0:NSAMP, 0:128], func=AF.Copy),
                     reads=[R_pk], writes=[R_kvf])
                S.dma("pool", lambda e: e.dma_start(out=oks[j, 96:128, :], in_=kvf[0:NSAMP, 0, :]), reads=[R_kvf], writes=[OUT])
                S.dma("pool", lambda e: e.dma_start(out=oks[j, 0:96, :], in_=ck[j, 32:128, :]), writes=[OUT])
                S.dma("pool", lambda e: e.dma_start(out=ovs[j, 0:96, :], in_=cv[j, 32:128, :]), writes=[OUT])
        relw(RW)
        kp, R_kp = kprev[j]
        vp, R_vp = vprev[j]
        jobs = []
        if sb == 0:
            jobs.append((META0, NMETA, [(kf[:, META0:META0 + NMETA], vtm[:, 8, :], NMETA)], [R_kf, R_vtm], None))
            S.op("dve", lambda e: e.tensor_copy(out=kp[:, 0:NMETA], in_=kf[:, META0:META0 + NMETA]), reads=[R_kf], writes=[R_kp])
            S.op("dve", lambda e: e.tensor_copy(out=vp[0:NMETA, :], in_=vtm[0:NMETA, 8, :]), reads=[R_vtm], writes=[R_vp])
        for t in range(SBF // 128):
            if t == 0:
                kts = [(kp[:], vp[:], 128), (kf[:, 0:128], vtm[:, 0, :], 128)]
                kregs = [R_kp, R_vp, R_kf, R_vtm]
            else:
                kts = [(kf[:, (t - 1) * 128:t * 128], vtm[:, t - 1, :], 128), (kf[:, t * 128:(t + 1) * 128], vtm[:, t, :], 128)]
                kregs = [R_kf, R_vtm]
            jobs.append((t * 128, 128, kts, kregs, mskb[:, 256:512] if (sb == 0 and t == 0) else mskb[:, 0:256]))
        if sb != 0:
            attn_pipeline(j, qa, R_qa, jobs)
        if not last and sb != 0:
            S.op("dve", lambda e: e.tensor_copy(out=kp[:], in_=kf[:, SBF - 128:SBF]), reads=[R_kf], writes=[R_kp])
            S.op("dve", lambda e: e.tensor_copy(out=vp[:], in_=vtm[:, 7, :]), reads=[R_vtm], writes=[R_vp])
        if sb == 0:
            _, stg, R_stg = take_stage()
            S.dma("pool", lambda e: e.dma_start(out=stg[:, 0:128], in_=ck[j]), writes=[R_stg])
            S.dma("pool", lambda e: e.dma_start(out=stg[:, 128:256], in_=cv[j]), writes=[R_stg])
            S.op("dve", lambda e: e.tensor_copy(out=cvb[:], in_=stg[:, 128:256]), reads=[R_stg], writes=[R_cvb])
            ckb, R_ckb = bfv(7)[:, 0:128], R_cb[7]
            S.op("dve", lambda e: e.tensor_copy(out=ckb, in_=stg[:, 0:128]), reads=[R_stg], writes=[R_ckb])
            pt, R_pt = gu_r.next()
            ptb = pt.bitcast(BF16)
            S.op("pe", lambda e: e.transpose(ptb[:, 0:128], ckb, ident_b[:]), reads=[R_ckb, R_idb], writes=[R_pt])
            S.op("act", lambda e: e.activation(out=ckT[:], in_=ptb[:, 0:128], func=AF.Copy), reads=[R_pt], writes=[R_ckT])
            jobs.append((SAMP0, NSAMP, [(ckT[:], cvb[:], 128), (kf[:, SAMP0:SAMP0 + NSAMP], vtm[:, 9, :], NSAMP)],
                         [R_ckT, R_cvb, R_kf, R_vtm], None))
            attn_pipeline(j, qa, R_qa, jobs)
            if not last:
                S.op("dve", lambda e: e.tensor_copy(out=kp[:], in_=kf[:, SBF - 128:SBF]), reads=[R_kf], writes=[R_kp])
                S.op("dve", lambda e: e.tensor_copy(out=vp[:], in_=vtm[:, 7, :]), reads=[R_vtm], writes=[R_vp])
        for h0 in (0, 2):
            pa = hgrn_head(j, h0, sb, ncols, blocks, tiles)
            pb = hgrn_head(j, h0 + 1, sb, ncols, blocks, tiles)
            prev = [None, None]
            for ci_ in range(len(pa[0])):
                for k_, p_ in enumerate((pa, pb)):
                    chunks, state_part, out_part, finish = p_
                    sbt_ = state_part(chunks[ci_])
                    if prev[k_] is not None:
                        out_part(*prev[k_])
                    prev[k_] = (chunks[ci_], sbt_)
                pump(1)
            for k_, p_ in enumerate((pa, pb)):
                p_[2](*prev[k_])
            pa[3]()
            pb[3]()
        out_proj([("woe", j, 256 * mp) for mp in range(4)], ncols, on_block_final)

    globals_vtm = (vtm, R_vtm)

    def hgrn_head(j, h, sb, ncols, blocks, tiles):
        last = (sb == n_sb - 1)
        o_ = 4 * (h % 2)
        qs, R_qs = bfv(o_), R_cb[o_]
        kin, R_kin = bfv(o_ + 1), R_cb[o_ + 1]
        gs, R_gs = bfv(o_ + 2), R_cb[o_ + 2]
        A, R_A = cbs[o_ + 3], R_cb[o_ + 3]
        Bx, R_B = cbs[8], R_cb[8]
        C, R_C = cbs[10], R_cb[10]
        oh, R_oh = A, R_A
        ctab, R_ctab = ctabs[h % 2]
        vtm, R_vtm = (globals_vtm[0], globals_vtm[1]) if h % 2 == 0 else (vtmB, R_vtmB)
        W1, RW1 = getw(("km", "wie", j, ((768 + 128 * h, 128), (1280 + 128 * h, 128))))
        W2, RW2 = getw(("km", "wie", j, ((1792 + 128 * h, 128), (2304 + 128 * h, 128))))
        proj_fm(W1, RW1, 0, blocks, lambda pt, R_pt, c0, n: S.op(
            "act", lambda e: e.activation(out=qs[:, c0:c0 + n], in_=pt[:, 0:n], func=AF.Silu), reads=[R_pt], writes=[R_qs]))
        proj_fm(W2, RW2, 128, blocks, lambda pt, R_pt, c0, n: S.op(
            "act", lambda e: e.activation(out=gs[:, c0:c0 + n], in_=pt[:, 0:n], func=AF.Silu), reads=[R_pt], writes=[R_gs]))
        proj_fm(W1, RW1, 128, blocks, lambda pt, R_pt, c0, n: S.op(
            "act", lambda e: e.activation(out=A[:, c0:c0 + n], in_=pt[:, 0:n], func=AF.Sigmoid), reads=[R_pt], writes=[R_A]))
        for (c0, ntok, ti) in tiles:
            pt, R_pt = proj_tm(W2, RW2, 0, c0, ntok)
            S.op("dve", lambda e, pt=pt, ntok=ntok, ti=ti: e.tensor_copy(out=vtm[0:ntok, ti, :], in_=pt[0:ntok, 0:128]),
                 reads=[R_pt], writes=[R_vtm])
        relw(RW1, RW2)
        S.op("dve", lambda e: e.tensor_scalar(out=A[:, 0:ncols], in0=A[:, 0:ncols], scalar1=omlb[:, h, j:j + 1],
                                              scalar2=lbv[:, h, j:j + 1], op0=ALU.mult, op1=ALU.add),
             reads=[R_A, R_omlb, R_lbv], writes=[R_A])
        S.op("dve", lambda e: e.tensor_scalar(out=kin[:, 0:ncols], in0=A[:, 0:ncols], scalar1=-1.0, scalar2=1.0,
                                              op0=ALU.mult, op1=ALU.add), reads=[R_A], writes=[R_kin])
        S.op("act", lambda e: e.activation(out=A[:, 0:ncols], in_=A[:, 0:ncols], func=AF.Ln), reads=[R_A], writes=[R_A])
        segs = [("fr", 0, SBF, 64)]
        if sb == 0:
            segs = [("meta", META0, NMETA, NMETA), ("fr", 0, SBF, 64), ("samp", SAMP0, NSAMP, NSAMP)]
        for (sn, col0, Ltot, Lc) in segs:
            o = BOFF[sn]
            nch = Ltot // Lc
            ci = CTI[sn]
            S.op("dve", lambda e, o=o: e.memset(Bx[:, o:o + 1], 0.0), writes=[R_B])
            S.op("dve", lambda e, o=o, col0=col0, Ltot=Ltot: e.tensor_tensor_scan(
                out=Bx[:, o + 1:o + 1 + Ltot], data0=ones_f[:, 0:1].to_broadcast([128, Ltot]), data1=A[:, col0:col0 + Ltot], initial=0.0,
                op0=ALU.mult, op1=ALU.add), reads=[R_onf, R_A, R_B], writes=[R_B])
            Bst = Bx[:, o:o + Ltot].rearrange("p (c l) -> p c l", l=Lc)[:, :, 0]
            Bin = Bx[:, o + 1:o + 1 + Ltot].rearrange("p (c l) -> p c l", l=Lc)
            Bmd = Bin[:, :, Lc // 2 - 1]
            Bls = Bin[:, :, Lc - 1]
            S.op("dve", lambda e, Bin=Bin, col0=col0, Ltot=Ltot, Lc=Lc, nch=nch: e.tensor_tensor(
                out=A[:, col0:col0 + Ltot].rearrange("p (c l) -> p c l", l=Lc), in0=Bin,
                in1=Bin[:, :, Lc // 2 - 1:Lc // 2].to_broadcast([128, nch, Lc]), op=ALU.subtract),
                reads=[R_B, R_A], writes=[R_A])
            S.op("dve", lambda e, Bmd=Bmd, Bst=Bst, ci=ci, nch=nch: e.tensor_tensor(
                out=ctab[:, 0, ci:ci + nch], in0=Bmd, in1=Bst, op=ALU.subtract), reads=[R_B], writes=[R_ctab])
            S.op("dve", lambda e, Bmd=Bmd, Bls=Bls, ci=ci, nch=nch: e.tensor_tensor(
                out=ctab[:, 1, ci:ci + nch], in0=Bls, in1=Bmd, op=ALU.subtract), reads=[R_B], writes=[R_ctab])
            S.op("dve", lambda e, Bst=Bst, Bls=Bls, ci=ci, nch=nch: e.tensor_tensor(
                out=ctab[:, 2, ci:ci + nch], in0=Bls, in1=Bst, op=ALU.subtract), reads=[R_B], writes=[R_ctab])
        nct = 18 if sb == 0 else 16
        S.op("act", lambda e: e.activation(out=ctab[:, :, 0:nct], in_=ctab[:, :, 0:nct], func=AF.Exp), reads=[R_ctab], writes=[R_ctab])
        S.op("act", lambda e: e.activation(out=C[:, 0:ncols], in_=A[:, 0:ncols], func=AF.Exp), reads=[R_A], writes=[R_C])
        S.op("dve", lambda e: e.tensor_tensor(out=qs[:, 0:ncols], in0=qs[:, 0:ncols], in1=C[:, 0:ncols], op=ALU.mult),
             reads=[R_qs, R_C], writes=[R_qs])
        S.op("act", lambda e: e.activation(out=C[:, 0:ncols], in_=A[:, 0:ncols], func=AF.Exp, scale=-1.0), reads=[R_A], writes=[R_C])
        S.op("dve", lambda e: e.tensor_tensor(out=kin[:, 0:ncols], in0=kin[:, 0:ncols], in1=C[:, 0:ncols], op=ALU.mult),
             reads=[R_kin, R_C], writes=[R_kin])
        if sb == 0:
            St, R_St = Ssm[h]
            S.dma("pool", lambda e: e.dma_start(out=St[:], in_=sh[j, h]), writes=[R_St])
        chunks = []
        for (sn, col0, Ltot, Lc) in segs:
            for c in range(Ltot // Lc):
                if sn == "fr":
                    chunks.append((sn, col0 + c * Lc, Lc, CTI[sn] + c, c // 2, 64 * (c % 2), (c // 2) * 128, 128, c == Ltot // Lc - 1))
                else:
                    chunks.append((sn, col0, Lc, CTI[sn], 8 if sn == "meta" else 9, 0, col0, Ltot, True))
        ktm_cur = {}

        def state_part(ch):
            sn, cs, Lc, ci, ti, p0, tcol0, ntile, seg_last = ch
            St, R_St = Ssm[h] if sn == "samp" else Sst[j][h]
            if p0 == 0:
                pt, R_pt = gu_r.next()
                ptb = pt.bitcast(BF16)
                ktm, R_ktm = ktm_r.next()
                ktm_cur["k"] = (ktm, R_ktm)
                S.op("pe", lambda e: e.transpose(ptb[0:ntile, 0:128], kin[:, tcol0:tcol0 + ntile], ident_b[:]),
                     reads=[R_kin, R_idb], writes=[R_pt])
                S.op("act", lambda e: e.activation(out=ktm[0:ntile, :], in_=ptb[0:ntile, 0:128], func=AF.Copy),
                     reads=[R_pt], writes=[R_ktm])
            ktm, R_ktm = ktm_cur["k"]
            Sb, R_Sb = Sb_r.next()
            S.op("dve", lambda e: e.tensor_scalar(out=Sb[:], in0=St[:], scalar1=ctab[:, 0, ci:ci + 1], scalar2=None, op0=ALU.mult),
                 reads=[R_St, R_ctab], writes=[R_Sb])
            pn_, R_pn_ = ya_r.next()
            mm_group(pn_[:, 0:128], lambda k: ktm[p0:p0 + Lc, :], lambda k: vtm[p0:p0 + Lc, ti, :], 1, [R_ktm, R_vtm], [R_pn_])
            tmp, R_tmp = tmp_r.next()
            S.op("act", lambda e: e.activation(out=tmp[:], in_=pn_[:, 0:128], func=AF.Copy, scale=ctab[:, 1, ci:ci + 1]),
                 reads=[R_pn_, R_ctab], writes=[R_tmp])
            S.op("dve", lambda e: e.scalar_tensor_tensor(out=St[:], in0=St[:], scalar=ctab[:, 2, ci:ci + 1], in1=tmp[:],
                                                         op0=ALU.mult, op1=ALU.add), reads=[R_St, R_ctab, R_tmp], writes=[R_St])
            if seg_last and sn == "samp":
                S.dma("pool", lambda e: e.dma_start(out=ohs[j, h], in_=St[:]), reads=[R_St], writes=[OUT])
            if seg_last and sn == "fr" and last:
                S.dma("pool", lambda e: e.dma_start(out=ohp[j, h], in_=St[:]), reads=[R_St], writes=[OUT])
            return (Sb, R_Sb)

        def out_part(ch, Sbt):
            sn, cs, Lc, ci, ti, p0, tcol0, ntile, seg_last = ch
            Sb, R_Sb = Sbt
            ps, R_s = yb_r.next()
            mm_group(ps[p0:p0 + Lc, 0:Lc], lambda k: kin[:, cs:cs + Lc], lambda k: qs[:, cs:cs + Lc], 1, [R_kin, R_qs], [R_s])
            pth, R_pth = pth_r.next()
            S.op("dve", lambda e: e.tensor_tensor(out=pth[p0:p0 + Lc, 0:Lc], in0=ps[p0:p0 + Lc, 0:Lc],
                                                  in1=mask_hg[p0:p0 + Lc, 0:Lc], op=ALU.mult), reads=[R_s, R_cst], writes=[R_pth])
            po, R_po = gu_r.next()

            def fo(e):
                e.matmul(po[:, 0:Lc], lhsT=Sb[:], rhs=qs[:, cs:cs + Lc], start=True, stop=False)
                return e.matmul(po[:, 0:Lc], lhsT=vtm[p0:p0 + Lc, ti, :], rhs=pth[p0:p0 + Lc, 0:Lc], start=False, stop=True)
            S.op("pe", fo, reads=[R_Sb, R_qs, R_vtm, R_pth], writes=[R_po])
            S.op("act", lambda e: e.activation(out=oh[:, cs:cs + Lc], in_=po[:, 0:Lc], func=AF.Copy), reads=[R_po], writes=[R_oh])

        def finish():
            rstd_of(lambda c0, n: oh[:, c0:c0 + n].unsqueeze(1), [R_oh], 1, ncols, 128.0, 9)
            S.op("dve", lambda e: e.tensor_tensor(out=oh[:, 0:ncols], in0=oh[:, 0:ncols], in1=cbs[9][:, 0:ncols], op=ALU.mult),
                 reads=[R_oh, R_cb[9]], writes=[R_oh])
            S.op("dve", lambda e: e.scalar_tensor_tensor(
                out=zT[:, 4 + h, 0:ncols], in0=oh[:, 0:ncols], scalar=hngT[:, h, j:j + 1], in1=gs[:, 0:ncols],
                op0=ALU.mult, op1=ALU.mult), reads=[R_oh, R_hng, R_gs], writes=[R_z[4 + h]])
        return chunks, state_part, out_part, finish

    def odd_mixer(j, sb, ncols, on_block_final):
        last = (sb == n_sb - 1)
        blocks = blocks_of(ncols)
        uh, R_uh = uhalo[j]
        Wst = {}

        def bufs(jc):
            o = 4 * (jc % 2)
            return (bfv(o), R_cb[o]), (cbs[o + 1], R_cb[o + 1]), (cbs[o + 2], R_cb[o + 2]), (cbs[o + 3], R_cb[o + 3])

        def proj(jc):
            (bgb, R_bg), (cgf, R_cg), (uF, R_uF), (yv, R_yv) = bufs(jc)
            uaux, R_uaux = uaux2[jc % 2]
            W1, RW1 = getw(("km", "wio", j, ((128 * jc, 128), (D + 128 * jc, 128))))
            if jc % 2 == 0:
                Wst["w2"] = getw(("km", "wio", j, ((2 * D + 128 * jc, 256),)))
            W2, RW2 = Wst["w2"]
            proj_fm(W1, RW1, 0, blocks, lambda pt, R_pt, c0, n: S.op(
                "act", lambda e: e.activation(out=bgb[:, c0:c0 + n], in_=pt[:, 0:n], func=AF.Copy), reads=[R_pt], writes=[R_bg]))
            proj_fm(W1, RW1, 128, blocks, lambda pt, R_pt, c0, n: S.op(
                "act", lambda e: e.activation(out=cgf[:, c0:c0 + n], in_=pt[:, 0:n], func=AF.Copy), reads=[R_pt], writes=[R_cg]))

            def evac_u(pt, R_pt, c0, n):
                if c0 < SBF:
                    S.op("dve", lambda e: e.tensor_tensor(out=uF[:, 2 + c0:2 + c0 + n], in0=cgf[:, c0:c0 + n], in1=pt[:, 0:n],
                                                          op=ALU.mult), reads=[R_cg, R_pt], writes=[R_uF])
                else:
                    S.op("dve", lambda e: e.tensor_tensor(out=uaux[:, 2:2 + NMETA], in0=cgf[:, META0:META0 + NMETA],
                                                          in1=pt[:, 0:NMETA], op=ALU.mult), reads=[R_cg, R_pt], writes=[R_uaux])
                    S.op("dve", lambda e: e.tensor_tensor(out=uaux[:, 20:20 + NSAMP], in0=cgf[:, SAMP0:SAMP0 + NSAMP],
                                                          in1=pt[:, NMETA:NMETA + NSAMP], op=ALU.mult),
                         reads=[R_cg, R_pt], writes=[R_uaux])
            proj_fm(W2, RW2, 128 * (jc % 2), blocks, evac_u)
            relw(RW1)
            if jc % 2 == 1:
                relw(RW2)

        def conv(jc):
            (bgb, R_bg), (cgf, R_cg), (uF, R_uF), (yv, R_yv) = bufs(jc)
            uaux, R_uaux = uaux2[jc % 2]
            if sb == 0:
                S.op("dve", lambda e: e.memset(uaux[:, 0:2], 0.0), writes=[R_uaux])
                S.op("dve", lambda e: e.tensor_copy(out=uaux[:, 18:20], in_=ccT[:, jc, 2 * j:2 * j + 2]),
                     reads=[R_ccT], writes=[R_uaux])
                S.op("dve", lambda e: e.tensor_copy(out=uF[:, 0:2], in_=uaux[:, NMETA:NMETA + 2]), reads=[R_uaux], writes=[R_uF])
            else:
                S.op("dve", lambda e: e.tensor_copy(out=uF[:, 0:2], in_=uh[:, jc, :]), reads=[R_uh], writes=[R_uF])
            segs = [(uF, R_uF, 0, 0, SBF)]
            if sb == 0:
                segs += [(uaux, R_uaux, 0, META0, NMETA), (uaux, R_uaux, 18, SAMP0, NSAMP)]
            for (ub, R_ub, uo, yc0, N) in segs:
                w0 = cwT[:, jc, 3 * j + 0:3 * j + 1]
                w1 = cwT[:, jc, 3 * j + 1:3 * j + 2]
                w2 = cwT[:, jc, 3 * j + 2:3 * j + 3]
                S.op("dve", lambda e, ub=ub, uo=uo, yc0=yc0, N=N, w0=w0: e.tensor_scalar(
                    out=yv[:, yc0:yc0 + N], in0=ub[:, uo:uo + N], scalar1=w0, scalar2=None, op0=ALU.mult),
                    reads=[R_ub, R_cw], writes=[R_yv])
                S.op("dve", lambda e, ub=ub, uo=uo, yc0=yc0, N=N, w1=w1: e.scalar_tensor_tensor(
                    out=yv[:, yc0:yc0 + N], in0=ub[:, uo + 1:uo + 1 + N], scalar=w1, in1=yv[:, yc0:yc0 + N],
                    op0=ALU.mult, op1=ALU.add), reads=[R_ub, R_cw, R_yv], writes=[R_yv])
                S.op("dve", lambda e, ub=ub, uo=uo, yc0=yc0, N=N, w2=w2: e.scalar_tensor_tensor(
                    out=yv[:, yc0:yc0 + N], in0=ub[:, uo + 2:uo + 2 + N], scalar=w2, in1=yv[:, yc0:yc0 + N],
                    op0=ALU.mult, op1=ALU.add), reads=[R_ub, R_cw, R_yv], writes=[R_yv])
            S.op("dve", lambda e: e.tensor_tensor(out=zT[:, jc, 0:ncols], in0=bgb[:, 0:ncols], in1=yv[:, 0:ncols], op=ALU.mult),
                 reads=[R_bg, R_yv], writes=[R_z[jc]])
            S.op("dve", lambda e: e.tensor_copy(out=uh[:, jc, :], in_=uF[:, SBF:SBF + 2]), reads=[R_uF], writes=[R_uh])
            if sb == 0:
                S.op("dve", lambda e: e.tensor_copy(out=uSh[:, jc, :], in_=uaux[:, 18 + NSAMP:20 + NSAMP]),
                     reads=[R_uaux], writes=[R_uSh])

        proj(0)
        for jc in range(KC):
            if jc + 1 < KC:
                proj(jc + 1)
            conv(jc)
        if last:
            fm_to_rows(lambda c: uh[:, c, :], 2, KC, ocp[2 * j:2 * j + 2, :], [R_uh])
        if sb == 0:
            fm_to_rows(lambda c: uSh[:, c, :], 2, KC, ocs[2 * j:2 * j + 2, :], [R_uSh])
        out_proj([("km", "woo", j, ((256 * mp, 256),)) for mp in range(4)], ncols, on_block_final)

    MIXERS = {"even": even_mixer, "odd": odd_mixer}

    def main(max_stage=None):
        import os
        skip = os.environ.get("KSKIP", "")
        if "setup" in skip:
            S.dma("sp", lambda e: e.dma_start(out=cst[:], in_=consts[:, :]), writes=[R_cst])
        else:
            setup()
        if "io" in skip:
            return
        for sb in range(n_sb):
            ncols = NCM if sb == 0 else SBF
            load_x(sb)
            stages = [(l, s) for l in range(n_layers) for s in range(3)]
            if max_stage is not None:
                stages = stages[:max_stage]
            gidx = [l * 6 + 2 * s for (l, s) in stages]
            if stages:
                prenorm(gidx[0], ncols)
            for si, (l, s) in enumerate(stages):
                nxt = gidx[si + 1] if si + 1 < len(stages) else None

                import os
                late = ("late%d" % s) in os.environ.get("KSKIP", "")

                def epilogue(c0, n, g=gidx[si], nxt=nxt, late=late):
                    for p in postnorm_pieces(g + 1, c0, n):
                        epi_q.append((c0 // 256, p))
                    if nxt is not None and not late:
                        for p in prenorm_pieces(nxt, c0, n):
                            epi_q.append((c0 // 256, p))
                if s == 0:
                    ffn(2 * l, ncols, epilogue)
                elif s == 1:
                    MIXERS["even" if l % 2 == 0 else "odd"](l // 2, sb, ncols, epilogue)
                else:
                    ffn(2 * l + 1, ncols, epilogue)
                if late and nxt is not None:
                    flush_epi()
                    prenorm(nxt, ncols)
            flush_epi()
            store_y(sb)

    main(dbg_at)
    if plan is None:
        S.stack.close()
        return None, ws["rec"]
    S.finish([OUT])
    return S, ws["rec"]


IN_NAMES = ["xp", "xs", "ck", "cv", "sh", "ccv", "meta", "gains", "wg", "wu", "wd", "wie", "woe", "sinks", "lbl", "hng",
            "wio", "cw", "woo", "consts"]


def make_nc(**kw):
    nc0 = bass.Bass("TRN2", target_bir_lowering=False)
    _, plan = build_program(nc0, plan=None, **kw)
    nc = bass.Bass("TRN2", target_bir_lowering=False)
    S, rec = build_program(nc, plan=plan, **kw)
    assert rec == plan
    return nc, S


def core_inputs(inp, b, sbi, shared):
    m = dict(shared)
    m["xp"] = np.ascontiguousarray(inp["x_prompt"][b])
    m["xs"] = np.ascontiguousarray(inp["x_sample"][sbi])
    m["ck"] = np.ascontiguousarray(inp["cache_swa_k"][:, sbi]).reshape(2, 128, 128)
    m["cv"] = np.ascontiguousarray(inp["cache_swa_v"][:, sbi]).reshape(2, 128, 128)
    m["sh"] = np.ascontiguousarray(inp["state_hgrn"][:, sbi])
    m["ccv"] = np.ascontiguousarray(inp["cache_conv"][:, sbi]).reshape(4, D)
    return m


def shared_inputs(inp):
    f = lambda a: np.ascontiguousarray(np.asarray(a, dtype=np.float32))
    return {
        "meta": f(inp["meta_tokens"]), "gains": f(inp["norm_gains"]).reshape(24, D),
        "wg": f(inp["w_ffn_gate"]).reshape(8, D, DFF), "wu": f(inp["w_ffn_up"]).reshape(8, D, DFF),
        "wd": f(inp["w_ffn_down"]).reshape(8, DFF, D), "wie": f(inp["w_in_even"]), "woe": f(inp["w_out_even"]),
        "sinks": f(inp["attn_sinks"]).reshape(1, 16), "lbl": f(inp["hgrn_lb_logits"]),
        "hng": f(inp["hgrn_norm_gain"]).reshape(2, 512), "wio": f(inp["w_in_odd"]),
        "cw": f(inp["conv_w"]).reshape(6, D), "woo": f(inp["w_out_odd"]), "consts": host_consts(),
    }


_CACHE = {}


def kernel(**inputs):
    if "nc" not in _CACHE:
        _CACHE["nc"] = make_nc()[0]
    nc = _CACHE["nc"]
    inp = {k: np.asarray(v) for k, v in inputs.items()}
    sh = shared_inputs(inp)
    in_maps = [core_inputs(inp, c % 4, c, sh) for c in range(8)]
    res = run_bass_kernel_spmd(nc, in_maps, core_ids=list(range(8)))
    r = res.results
    f32 = np.float32
    y_prompt = np.stack([r[b]["yp"] for b in range(4)]).astype(f32)
    y_sample = np.stack([r[c]["ys"] for c in range(8)]).astype(f32)

    def gather(name, cores, shape):
        a = np.stack([r[c][name] for c in cores], axis=1)
        return np.ascontiguousarray(a.reshape(shape)).astype(f32)
    P, Q = range(4), range(8)
    swa_k_p = gather("okp", P, (2, 4, 128, 2, 64))
    swa_v_p = gather("ovp", P, (2, 4, 128, 2, 64))
    hgrn_p = gather("ohp", P, (2, 4, 4, 128, 128))
    conv_p = np.stack([r[c]["ocp"].reshape(2, 2, D) for c in P], axis=1).astype(f32)
    swa_k_s = gather("oks", Q, (2, 8, 128, 2, 64))
    swa_v_s = gather("ovs", Q, (2, 8, 128, 2, 64))
    hgrn_s = gather("ohs", Q, (2, 8, 4, 128, 128))
    conv_s = np.stack([r[c]["ocs"].reshape(2, 2, D) for c in Q], axis=1).astype(f32)
    return (y_prompt, y_sample, swa_k_p, swa_v_p, hgrn_p, conv_p, swa_k_s, swa_v_s, hgrn_s, conv_s)
```
